# Optimizing a Trainium2 kernel written in Bass

```python
import jax, jax.numpy as jnp
from jax import lax
import numpy as np

D_MODEL = 2048
BATCH = 2
SEQ = 8192
DEPTH = 4

N_EVEN = (DEPTH + 1) // 2
N_ODD = DEPTH // 2

A_HEAD_DIM = 64
A_HEADS = D_MODEL // 128
A_KV_HEADS = A_HEADS // 8
A_WIDTH = A_HEADS * A_HEAD_DIM
A_KV_WIDTH = A_KV_HEADS * A_HEAD_DIM
WINDOW = 128
A_BLOCK = 128
ROPE_THETA = 10000.0

B_HEAD_DIM = 128
B_HEADS = (D_MODEL // 2) // B_HEAD_DIM
B_WIDTH = B_HEADS * B_HEAD_DIM
B_CHUNK = 16
LB_FLOOR = 1e-30

C_HEAD_DIM = 128
C_HEADS = D_MODEL // C_HEAD_DIM
C_WIDTH = C_HEADS * C_HEAD_DIM
C_BLOCK = 128

EVEN_SPLITS = [A_WIDTH, A_KV_WIDTH, A_KV_WIDTH, A_WIDTH, B_WIDTH, B_WIDTH, B_WIDTH, B_WIDTH]
EVEN_IN = sum(EVEN_SPLITS)
EVEN_MIX = A_WIDTH + B_WIDTH
ODD_IN = 4 * C_WIDTH
EPS = 1e-6

kernel_name = "hybrid_swa_hgrn2_stickbreaking"


def rmsnorm(x, w):
    xf = x.astype(jnp.float32)
    return xf * lax.rsqrt(jnp.mean(xf * xf, axis=-1, keepdims=True) + EPS) * w.astype(jnp.float32)


def rope(x, pos):
    half = x.shape[-1] // 2
    inv_freq = 1.0 / (ROPE_THETA ** (jnp.arange(half, dtype=jnp.float32) / half))
    ang = pos[:, None] * inv_freq[None, :]
    cos = jnp.cos(ang)[None, :, None, :]
    sin = jnp.sin(ang)[None, :, None, :]
    x1, x2 = x[..., :half], x[..., half:]
    return jnp.concatenate([x1 * cos - x2 * sin, x2 * cos + x1 * sin], axis=-1)


def sliding_window_attention(q, k, v, sinks):
    b_, s_ = q.shape[:2]
    nb = s_ // A_BLOCK
    g = A_HEADS // A_KV_HEADS
    qb = q.reshape(b_, nb, A_BLOCK, A_KV_HEADS, g, A_HEAD_DIM)

    def band(t):
        t = t.reshape(b_, nb, A_BLOCK, A_KV_HEADS, A_HEAD_DIM)
        prev = jnp.concatenate([jnp.zeros_like(t[:, :1]), t[:, :-1]], axis=1)
        return jnp.concatenate([prev, t], axis=2)

    kb, vb = band(k), band(v)
    scores = jnp.einsum('bnqhgd,bnkhd->bnhgqk', qb, kb) * (A_HEAD_DIM ** -0.5)
    qpos = jnp.arange(nb)[:, None] * A_BLOCK + jnp.arange(A_BLOCK)[None, :]
    kpos = jnp.arange(nb)[:, None] * A_BLOCK - A_BLOCK + jnp.arange(2 * A_BLOCK)[None, :]
    diff = qpos[:, :, None] - kpos[:, None, :]
    valid = (diff >= 0) & (diff < WINDOW) & (kpos[:, None, :] >= 0)
    scores = jnp.where(valid[None, :, None, None], scores, -1e30)
    sink = jnp.broadcast_to(sinks.astype(jnp.float32).reshape(1, 1, A_KV_HEADS, g, 1, 1),
                            scores.shape[:-1] + (1,))
    probs = jax.nn.softmax(jnp.concatenate([scores, sink], axis=-1), axis=-1)[..., :-1]
    out = jnp.einsum('bnhgqk,bnkhd->bnqhgd', probs, vb)
    return out.reshape(b_, s_, A_WIDTH)


def hgrn2_chunkwise(q, logf, k, v):
    b_, s_, h_, kd = q.shape
    vd = v.shape[-1]
    nc = s_ // B_CHUNK

    def chunks(t):
        return t.reshape(b_, nc, B_CHUNK, h_, t.shape[-1]).transpose(0, 3, 1, 2, 4)

    qc, lc, kc, vc = chunks(q), chunks(logf), chunks(k), chunks(v)
    bcum = jnp.cumsum(lc, axis=3)
    causal = jnp.tril(jnp.ones((B_CHUNK, B_CHUNK), dtype=bool))
    pair = jnp.exp(jnp.minimum(bcum[:, :, :, :, None, :] - bcum[:, :, :, None, :, :], 0.0))
    att = jnp.einsum('bhnik,bhnjk,bhnijk->bhnij', qc, kc, pair)
    att = jnp.where(causal, att, 0.0)
    o_intra = jnp.einsum('bhnij,bhnjv->bhniv', att, vc)

    blast = bcum[:, :, :, -1:]
    u = jnp.einsum('bhnjk,bhnjv->bhnkv', kc * jnp.exp(blast - bcum), vc)
    decay = jnp.exp(blast[:, :, :, 0])

    def step(state, inp):
        d, inc = inp
        return d[..., None] * state + inc, state

    s0 = jnp.zeros((b_, h_, kd, vd), jnp.float32)
    _, s_prev = lax.scan(step, s0, (decay.transpose(2, 0, 1, 3), u.transpose(2, 0, 1, 3, 4)))
    s_prev = s_prev.transpose(1, 2, 0, 3, 4)
    o_inter = jnp.einsum('bhnik,bhnkv->bhniv', qc * jnp.exp(bcum), s_prev)
    o = o_intra + o_inter
    return o.transpose(0, 2, 3, 1, 4).reshape(b_, s_, h_, vd)


def stick_breaking_attention(q, k, v):
    b_, h_, s_, d_ = q.shape
    nb = s_ // C_BLOCK
    qb = q.reshape(b_, h_, nb, C_BLOCK, d_).transpose(2, 0, 1, 3, 4)
    kpos = jnp.arange(s_)
    scale = d_ ** -0.5

    def block(args):
        c, qblk = args
        qpos = c * C_BLOCK + jnp.arange(C_BLOCK)
        z = jnp.einsum('bhqd,bhkd->bhqk', qblk, k) * scale
        before = kpos[None, :] < qpos[:, None]
        log1m = jnp.where(before, jax.nn.log_sigmoid(-z), 0.0)
        rc = lax.cumsum(log1m, axis=3, reverse=True)
        rc_excl = jnp.concatenate([rc[..., 1:], jnp.zeros_like(rc[..., :1])], axis=-1)
        w = jnp.where(before, jnp.exp(jax.nn.log_sigmoid(z) + rc_excl), 0.0)
        return jnp.einsum('bhqk,bhkd->bhqd', w, v)

    o = lax.map(block, (jnp.arange(nb), qb))
    return o.transpose(1, 0, 3, 2, 4).reshape(b_, s_, h_ * d_)


def even_layer(x, pos, ln, w_in, qn, kn, sinks, lb, gn, w_out):
    b_, s_, _ = x.shape
    h = rmsnorm(x, ln).astype(x.dtype)
    proj = h @ w_in
    idx = list(np.cumsum(EVEN_SPLITS)[:-1])
    aq, ak, av, ag, bq, bf, bi, bg = jnp.split(proj, idx, axis=-1)

    aq = rope(rmsnorm(aq.reshape(b_, s_, A_HEADS, A_HEAD_DIM), qn), pos)
    ak = rope(rmsnorm(ak.reshape(b_, s_, A_KV_HEADS, A_HEAD_DIM), kn), pos)
    av = av.reshape(b_, s_, A_KV_HEADS, A_HEAD_DIM).astype(jnp.float32)
    ya = sliding_window_attention(aq, ak, av, sinks) * jax.nn.silu(ag.astype(jnp.float32))

    lbf = lb.astype(jnp.float32)
    logf = jnp.logaddexp(jnp.log(jnp.maximum(lbf, LB_FLOOR)),
                         jnp.log1p(-lbf) + jax.nn.log_sigmoid(bf.astype(jnp.float32)))
    kin = -jnp.expm1(logf)
    shp = (b_, s_, B_HEADS, B_HEAD_DIM)
    ob = hgrn2_chunkwise(jax.nn.silu(bq.astype(jnp.float32)).reshape(shp), logf.reshape(shp),
                         kin.reshape(shp), bi.astype(jnp.float32).reshape(shp))
    yb = rmsnorm(ob, gn) * jax.nn.silu(bg.astype(jnp.float32)).reshape(shp)
    yb = yb.reshape(b_, s_, B_WIDTH)

    y = jnp.concatenate([ya, yb], axis=-1).astype(x.dtype) @ w_out
    return x + y


def odd_layer(x, ln, w_in, w_out):
    b_, s_, _ = x.shape
    h = rmsnorm(x, ln).astype(x.dtype)
    cq, ck, cv, cg = jnp.split(h @ w_in, 4, axis=-1)

    def heads(t):
        return t.reshape(b_, s_, C_HEADS, C_HEAD_DIM).transpose(0, 2, 1, 3).astype(jnp.float32)

    oc = stick_breaking_attention(heads(cq), heads(ck), heads(cv))
    y = (oc * jax.nn.silu(cg.astype(jnp.float32))).astype(x.dtype) @ w_out
    return x + y


def setup_inputs(seed: int = 0) -> dict:
    key = jax.random.key(seed)
    ks = jax.random.split(key, 12)
    f32 = jnp.float32
    return {
        "x": jax.random.normal(ks[0], (BATCH, SEQ, D_MODEL), f32),
        "ln_even": 1.0 + 0.02 * jax.random.normal(ks[1], (N_EVEN, D_MODEL), f32),
        "w_in_even": jax.random.normal(ks[2], (N_EVEN, D_MODEL, EVEN_IN), f32) * D_MODEL ** -0.5,
        "q_norm_a": 1.0 + 0.02 * jax.random.normal(ks[3], (N_EVEN, A_HEAD_DIM), f32),
        "k_norm_a": 1.0 + 0.02 * jax.random.normal(ks[4], (N_EVEN, A_HEAD_DIM), f32),
        "sinks_a": jax.random.normal(ks[5], (N_EVEN, A_HEADS), f32),
        "lower_bounds": 0.5 * jax.random.normal(ks[6], (N_EVEN, B_WIDTH), f32),
        "g_norm_b": 1.0 + 0.02 * jax.random.normal(ks[7], (N_EVEN, B_HEAD_DIM), f32),
        "w_out_even": jax.random.normal(ks[8], (N_EVEN, EVEN_MIX, D_MODEL), f32) * EVEN_MIX ** -0.5,
        "ln_odd": 1.0 + 0.02 * jax.random.normal(ks[9], (N_ODD, D_MODEL), f32),
        "w_in_odd": jax.random.normal(ks[10], (N_ODD, D_MODEL, ODD_IN), f32) * D_MODEL ** -0.5,
        "w_out_odd": jax.random.normal(ks[11], (N_ODD, C_WIDTH, D_MODEL), f32) * C_WIDTH ** -0.5,
    }


def reference(x, ln_even, w_in_even, q_norm_a, k_norm_a, sinks_a, lower_bounds, g_norm_b,
              w_out_even, ln_odd, w_in_odd, w_out_odd):
    pos = jnp.arange(x.shape[1], dtype=jnp.float32)
    lbs = jax.nn.softmax(lower_bounds.astype(jnp.float32), axis=0)
    lbs = jnp.cumsum(lbs, axis=0) - lbs[0]
    for layer in range(DEPTH):
        if layer % 2 == 0:
            e = layer // 2
            x = even_layer(x, pos, ln_even[e], w_in_even[e], q_norm_a[e], k_norm_a[e], sinks_a[e],
                           lbs[e], g_norm_b[e], w_out_even[e])
        else:
            o = layer // 2
            x = odd_layer(x, ln_odd[o], w_in_odd[o], w_out_odd[o])
    return x
```

```python
import numpy as np
import ml_dtypes
import concourse.bass as bass
import concourse.mybir as mybir
from concourse.bass_utils import run_bass_kernel_spmd

F32 = mybir.dt.float32
BF16 = mybir.dt.bfloat16
AF = mybir.ActivationFunctionType
ALU = mybir.AluOpType
AX = mybir.AxisListType

D_MODEL = 2048
SEQ = 8192
EPS = 1e-6
NEG = -30000.0
TP = 4
GROUPS = [[0, 1, 2, 3], [4, 5, 6, 7]]
KC = D_MODEL // 128


class Rec:
    COMPUTE = ("pe", "act", "dve", "pool")
    ND = 40
    ROT = 20000

    def __init__(self):
        self.ops = []
        self.lastw = {}
        self.readers = {}
        self.last_on = {}

    def add(self, eng, fn, r=(), w=(), kind="c", extra=()):
        idx = len(self.ops)
        raw = set()
        oth = set()
        for k in r:
            if k in self.lastw:
                raw.add(self.lastw[k])
        for k in w:
            if k in self.lastw:
                oth.add(self.lastw[k])
            for j in self.readers.get(k, ()):
                oth.add(j)
        for j in extra:
            raw.add(j)
        self.ops.append(dict(eng=eng, fn=fn, raw=raw, oth=oth - raw, kind=kind))
        for k in r:
            self.readers.setdefault(k, []).append(idx)
        for k in w:
            self.lastw[k] = idx
            self.readers[k] = []
        self.last_on[eng] = idx
        return idx

    def barrier(self):
        pend = [i for i, o in enumerate(self.ops) if o["kind"] in ("d", "cc") and not o.get("barred")]
        lasts = [v for v in self.last_on.values()]
        for i in pend:
            self.ops[i]["barred"] = True
        deps = set(pend) | set(lasts)
        for eng in ("pe", "act", "dve", "pool", "sp"):
            self.add(eng, None, kind="n", extra=deps)
        self.lastw = {}
        self.readers = {}

    def emit(self, nc):
        ops = self.ops
        n = len(ops)
        for i, o in enumerate(ops):
            keep = set()
            for d in o["raw"] | o["oth"]:
                od = ops[d]
                if od["kind"] == "n":
                    pass
                same = od["eng"] == o["eng"] and od["kind"] in ("c", "n") and o["kind"] in ("c", "n")
                if same:
                    if o["eng"] == "pe":
                        continue
                keep.add(d)
            o["deps"] = keep
        needed = [False] * n
        for o in ops:
            for d in o["deps"]:
                needed[d] = True
        sig = {}
        cnt = {e: 0 for e in ("pe", "act", "dve", "pool", "sp")}
        ndma = 0
        ncc = 0
        dma_prev = {}
        for i, o in enumerate(ops):
            if o["kind"] == "d":
                slot = ndma % self.ND
                val = 16 * (ndma // self.ND + 1)
                sig[i] = ("dma", slot, val)
                if ndma >= self.ND:
                    o["deps"].add(dma_prev[slot])
                    needed[dma_prev[slot]] = True
                dma_prev[slot] = i
                ndma += 1
            elif o["kind"] == "cc":
                ncc += 1
                sig[i] = ("cc", 0, ncc)
            elif needed[i]:
                e = o["eng"]
                cnt[e] += 1
                sig[i] = (e, (cnt[e] - 1) // self.ROT, (cnt[e] - 1) % self.ROT + 1)
        nrot = {e: max(1, (cnt[e] + self.ROT - 1) // self.ROT) for e in cnt}
        from contextlib import ExitStack
        with ExitStack() as st:
            sems = {}
            for e in cnt:
                for k in range(nrot[e]):
                    sems[(e, k)] = st.enter_context(nc.semaphore(f"s_{e}{k}"))
            for k in range(min(self.ND, max(ndma, 1))):
                sems[("dma", k)] = st.enter_context(nc.semaphore(f"s_d{k}"))
            sems[("cc", 0)] = st.enter_context(nc.semaphore("s_cc"))
            block = st.enter_context(nc.Block())
            engs = {"pe": "tensor", "act": "scalar", "dve": "vector", "pool": "gpsimd", "sp": "sync"}
            per = {e: [i for i, o in enumerate(ops) if o["eng"] == e] for e in engs}

            def run(ename, eobj):
                seen = {}
                seen_async = set()
                for i in per[ename]:
                    o = ops[i]
                    for d in sorted(o["deps"]):
                        od = ops[d]
                        s = sig[d]
                        if od["kind"] in ("d", "cc"):
                            if d in seen_async:
                                continue
                            seen_async.add(d)
                        else:
                            if seen.get(od["eng"], -1) >= d:
                                continue
                            seen[od["eng"]] = d
                        eobj.wait_ge(sems[(s[0], s[1])], s[2])
                    if o["fn"] is None:
                        if i in sig:
                            s = sig[i]
                            eobj.nop().then_inc(sems[(s[0], s[1])], 1)
                        continue
                    ins = o["fn"](eobj)
                    if i in sig:
                        s = sig[i]
                        if o["kind"] == "d":
                            ins.then_inc(sems[(s[0], s[1])], 16)
                        else:
                            ins.then_inc(sems[(s[0], s[1])], 1)

            for ename, attr in engs.items():
                if not per[ename]:
                    continue
                deco = getattr(block, attr)

                def body(eobj, _en=ename):
                    run(_en, eobj)
                deco(body)


class Prog:
    def __init__(self, S, layers):
        self.S = S
        self.layers = layers
        self.NT = S // 128
        self.NG = S // 512
        self.nc = bass.Bass("TRN2", target_bir_lowering=False)
        self.rec = Rec()
        self.sb_off = 0
        self.arena = None
        self.ARENA = 206 * 1024
        self.ps_banks = []
        self.inputs = {}
        self._uid = 0

    def dram_in(self, name, shape, dt=F32):
        t = self.nc.dram_tensor(name, list(shape), dt, kind="ExternalInput")
        self.inputs[name] = t
        return t.ap()

    def dram(self, name, shape, dt):
        return self.nc.dram_tensor(name, list(shape), dt).ap()

    def sb(self, name, shape, dt):
        if self.arena is None:
            self.arena = self.nc.alloc_sbuf_tensor("arena", [128, self.ARENA], mybir.dt.uint8).ap()
        esz = 2 if dt == BF16 else 4
        nbytes = int(np.prod(shape[1:])) * esz
        v = self.arena[:, self.sb_off:self.sb_off + nbytes].bitcast(dt)
        if len(shape) == 3:
            v = v.rearrange("p (a b) -> p a b", b=shape[2])
        elif len(shape) == 4:
            v = v.rearrange("p (a b c) -> p a b c", b=shape[2], c=shape[3])
        self.sb_off += (nbytes + 31) // 32 * 32
        assert self.sb_off <= self.ARENA, (name, self.sb_off)
        return v

    def op(self, eng, fn, r=(), w=(), kind="c"):
        return self.rec.add(eng, fn, r, w, kind)

    def dma(self, q, out, in_, r=(), w=()):
        return self.rec.add(q, lambda e: e.dma_start(out=out, in_=in_), r, w, kind="d")


def build(S, layers, final_full=True):
    P = Prog(S, layers)
    nc = P.nc
    NT, NG = P.NT, P.NG
    n_even = sum(1 for l in layers if l == "e")
    n_odd = sum(1 for l in layers if l == "o")

    x_in = P.dram_in("x", [S, D_MODEL])
    y_out = nc.dram_tensor("y", [S, D_MODEL], F32, kind="ExternalOutput").ap()
    cmask = P.dram_in("cmask", [4, 128, 512], BF16)
    cmat = P.dram_in("cmat", [4, 128, 128], BF16)
    wio = [P.dram_in(f"wio{i}", [D_MODEL, 2048]) for i in range(n_odd)]
    lno = [P.dram_in(f"lno{i}", [128, KC]) for i in range(n_odd)]
    woo = [P.dram_in(f"woo{i}", [512, D_MODEL]) for i in range(n_odd)]
    wie = [P.dram_in(f"wie{i}", [D_MODEL, 1664]) for i in range(n_even)]
    lne = [P.dram_in(f"lne{i}", [128, KC]) for i in range(n_even)]
    woe = [P.dram_in(f"woe{i}", [512, D_MODEL]) for i in range(n_even)]
    if n_even:
        qkn = [P.dram_in(f"qkn{i}", [1, 320]) for i in range(n_even)]
        snk = [P.dram_in(f"snk{i}", [1, 4]) for i in range(n_even)]
        lbt = P.dram_in("lbt", [128, 4])
        gnt = [P.dram_in(f"gnt{i}", [128, 1]) for i in range(n_even)]
        rope = P.dram_in("rope", [S, 64])
        emask = P.dram_in("emask", [3, 128, 512], BF16)
        bmask = P.dram_in("bmask", [128, 8], BF16)

    xres = P.dram("xres", [S, D_MODEL], F32)
    ypart = P.dram("ypart", [S, D_MODEL], F32)
    qT_d = P.dram("qT_d", [4, 128, S], BF16)
    kT_d = P.dram("kT_d", [4, 128, S], BF16)
    sgT_d = P.dram("sgT_d", [4, 128, S], BF16)
    v_d = P.dram("v_d", [S, 512], BF16)
    ogT_d = P.dram("ogT_d", [4, 128, S], BF16)

    ident = P.sb("ident", [128, 128], BF16)
    trineg = P.sb("trineg", [128, 128], BF16)
    onesneg = P.sb("onesneg", [128, 128], BF16)
    ones = P.sb("ones", [128, 128], BF16)
    negmask = P.sb("negmask", [128, 4, 512], BF16)
    PERSIST = P.sb_off
    for i, t in enumerate((ident, trineg, onesneg, ones)):
        P.dma("sp", t, cmat[i], w=[("const", i)])
    P.dma("sp", negmask, cmask.rearrange("j p t -> p j t"), w=[("const", "negmask")])
    CONST_KEYS = [("const", i) for i in range(4)] + [("const", "negmask")]

    psum = nc.alloc_psum_tensor("psum", [128, 4096], F32).ap()
    banks = [psum[:, i * 512:(i + 1) * 512] for i in range(8)]
    tpb = [psum[:, i * 1024:(i + 1) * 1024].bitcast(BF16).rearrange("p (k t) -> p k t", t=128) for i in range(2)]
    tph = [psum[:, i * 512:(i + 1) * 512].bitcast(BF16).rearrange("p (k t) -> p k t", t=128) for i in range(2)]

    def pview(bank, off, n, dt=F32, inner=None):
        v = psum[:, bank * 512 + off: bank * 512 + off + n]
        if dt == BF16:
            v = v.bitcast(BF16)
        if inner is not None:
            v = v.rearrange("p (a b) -> p a b", b=inner)
        return v

    def sb_reset():
        P.sb_off = PERSIST

    def load_weights(w_dram, ncols, ln_dram, wbf, wst, lnT):
        P.dma("sp", lnT, ln_dram, w=["lnT"])
        for kc in range(KC):
            s = kc % 2
            P.dma("sp", wst[s][:, :ncols], w_dram[kc * 128:(kc + 1) * 128, :], w=[("wst", s)])
            eng = "pool" if kc % 2 else "dve"
            P.op(eng, lambda e, kc=kc, s=s: e.tensor_scalar(
                wbf[:, kc, :], wst[s][:, :ncols], lnT[:, kc:kc + 1], None, ALU.mult),
                r=[("wst", s), "lnT"], w=[("wbf", kc)])

    def load_wout(w_dram, wob, wst):
        for c in range(4):
            s = c % 2
            P.dma("sp", wst[s][:, :D_MODEL], w_dram[c * 128:(c + 1) * 128, :], w=[("wst", s)])
            eng = "pool" if c % 2 else "dve"
            P.op(eng, lambda e, c=c, s=s: e.tensor_copy(wob[:, c, :], wst[s][:, :D_MODEL]),
                 r=[("wst", s)], w=[("wob", c)])

    def norm_transpose_group(G, x_src, xin, junk, ss, rstd, xn, hT, hs, tpb, even=False):
        gp = G % 2
        nx = len(xin)
        for t in range(4):
            tile_i = G * 4 + t
            xs_ = t % nx
            P.dma("sp", xin[xs_], x_src[tile_i * 128:(tile_i + 1) * 128, :], r=[("xres", G)], w=[("xin", xs_)])
            P.op("act", lambda e, t=t, gp=gp, xs_=xs_: e.activation(junk, xin[xs_], AF.Square, accum_out=ss[gp][:, t:t + 1]),
                 r=[("xin", xs_)], w=["junk", ("ss", gp)])
        if even:
            P.op("act", lambda e, gp=gp: e.activation(rstd[gp], ss[gp], AF.Ln, bias=EPS, scale=1.0 / D_MODEL),
                 r=[("ss", gp)], w=[("rstd", gp)])
            P.op("act", lambda e, gp=gp: e.activation(rstd[gp], rstd[gp], AF.Exp, scale=-0.5),
                 r=[("rstd", gp)], w=[("rstd", gp)])
        else:
            P.op("act", lambda e, gp=gp: e.activation(rstd[gp], ss[gp], AF.Sqrt, bias=EPS, scale=1.0 / D_MODEL),
                 r=[("ss", gp)], w=[("rstd", gp)])
            P.op("dve", lambda e, gp=gp: e.reciprocal(rstd[gp], rstd[gp]), r=[("rstd", gp)], w=[("rstd", gp)])
        nh = 2 if even else 1
        kh = KC // nh

        def tpk(sl):
            return ("bank", sl) if even else ("tp", sl)
        for t in range(4):
            s2 = t % 2
            xs_ = t % nx
            if nx < 4:
                tile_i = G * 4 + t
                P.dma("sp", xin[xs_], x_src[tile_i * 128:(tile_i + 1) * 128, :], r=[("xres", G)], w=[("xin", xs_)])
            P.op("dve", lambda e, s2=s2, t=t, gp=gp, xs_=xs_: e.tensor_scalar(xn[s2], xin[xs_], rstd[gp][:, t:t + 1], None, ALU.mult),
                 r=[("xin", xs_), ("rstd", gp)], w=[("xn", s2)])
            for hf in range(nh):
                sl = (t * nh + hf) % 2
                tp = tpb[sl]
                for k in range(kh):
                    kc = hf * kh + k
                    P.op("pe", lambda e, kc=kc, k=k, s2=s2, tp=tp: e.transpose(tp[:, k, :], xn[s2][:, kc * 128:(kc + 1) * 128], ident),
                         r=[("xn", s2), ("const", 0)], w=[tpk(sl)])
                dst = hT[hs][:, hf * kh:(hf + 1) * kh, t * 128:(t + 1) * 128]
                if (t * nh + hf) % 2 == 0:
                    P.op("act", lambda e, dst=dst, tp=tp: e.activation(dst, tp, AF.Copy),
                         r=[tpk(sl)], w=[("hT", hs)])
                else:
                    P.op("dve", lambda e, dst=dst, tp=tp: e.tensor_copy(dst, tp),
                         r=[tpk(sl)], w=[("hT", hs)])

    def out_phase(x_src, wob, last):
        og = [P.sb(f"og{i}", [128, 4, 512], BF16) for i in range(2)]
        xt = [P.sb(f"xt{i}", [128, D_MODEL], F32) for i in range(2)]
        ys = [P.sb(f"ys{i}", [128, D_MODEL], F32) for i in range(2)]
        for G in range(NG):
            gs = G % 2
            P.dma("sp", og[gs], ogT_d.rearrange("c p t -> p c t")[:, :, G * 512:(G + 1) * 512],
                  r=[("ogT_d", G)], w=[("og", gs)])
            for t in range(4):
                ti = G * 4 + t
                s2 = ti % 2
                P.dma("sp", xt[s2], x_src[ti * 128:(ti + 1) * 128, :], r=[("xres", G)], w=[("xt", s2)])
                for jg in range(4):
                    b = (ti * 4 + jg) % 4
                    for c in range(4):
                        P.op("pe", lambda e, b=b, c=c, jg=jg, t=t, gs=gs: e.matmul(
                            banks[b], og[gs][:, c, t * 128:(t + 1) * 128], wob[:, c, jg * 512:(jg + 1) * 512],
                            start=(c == 0), stop=(c == 3)),
                            r=[("og", gs), ("wob", c)], w=[("bank", b)])
                    P.op("dve", lambda e, b=b, jg=jg, s2=s2: e.scalar_tensor_tensor(
                        ys[s2][:, jg * 512:(jg + 1) * 512], xt[s2][:, jg * 512:(jg + 1) * 512], 0.25, banks[b],
                        ALU.mult, ALU.add),
                        r=[("bank", b), ("xt", s2)], w=[("ys", s2)])
                P.dma("sp", ypart[ti * 128:(ti + 1) * 128, :], ys[s2], r=[("ys", s2)], w=[("ypart", G)])
            rows = slice(G * 512, (G + 1) * 512)
            P.rec.add("pool", lambda e, rows=rows: e.collective_compute(
                "AllReduce", ALU.add, replica_groups=GROUPS, ins=[ypart[rows, :]], outs=[xres[rows, :]]),
                r=[("ypart", G)], w=[("xres", G)], kind="cc")
            if last:
                P.dma("sp", y_out[rows, :], xres[rows, :], r=[("xres", G)], w=[("yout", G)])

    def odd_layer(li, oi, x_src, last):
        rec = P.rec
        sb_reset()
        wbf = P.sb("wbf", [128, KC, 2048], BF16)
        wst = [P.sb(f"wst{i}", [128, 2048], F32) for i in range(2)]
        lnT = P.sb("lnT", [128, KC], F32)
        xin = [P.sb(f"xin{i}", [128, D_MODEL], F32) for i in range(4)]
        junk = P.sb("junk", [128, D_MODEL], BF16)
        ss = [P.sb(f"ss{i}", [128, 4], F32) for i in range(2)]
        rstd = [P.sb(f"rstd{i}", [128, 4], F32) for i in range(2)]
        xn = [P.sb(f"xn{i}", [128, D_MODEL], BF16) for i in range(2)]
        hT = [P.sb(f"hT{i}", [128, KC, 512], BF16) for i in range(2)]
        stq = [P.sb(f"stq{i}", [128, 4, 512], BF16) for i in range(2)]
        stk = [P.sb(f"stk{i}", [128, 4, 512], BF16) for i in range(2)]
        stg = [P.sb(f"stg{i}", [128, 4, 512], BF16) for i in range(2)]
        stv = [P.sb(f"stv{i}", [128, 4, 512], BF16) for i in range(2)]
        load_weights(wio[oi], 2048, lno[oi], wbf, wst, lnT)
        scale = 128 ** -0.5
        norm_transpose_group(0, x_src, xin, junk, ss, rstd, xn, hT, 0, tpb)
        for G in range(NG):
            hs = G % 2
            if G + 1 < NG:
                norm_transpose_group(G + 1, x_src, xin, junk, ss, rstd, xn, hT, (G + 1) % 2, tpb)
            n_acc = 0
            for typ in range(4):
                if typ == 2:
                    for t in range(4):
                        b = 4 + (n_acc % 4)
                        n_acc += 1
                        for kc in range(KC):
                            P.op("pe", lambda e, b=b, kc=kc, t=t, hs=hs: e.matmul(
                                banks[b], hT[hs][:, kc, t * 128:(t + 1) * 128], wbf[:, kc, 1024:1536],
                                start=(kc == 0), stop=(kc == KC - 1)),
                                r=[("hT", hs), ("wbf", kc)], w=[("bank", b)])
                        P.op("dve", lambda e, b=b, t=t, hs=hs: e.tensor_copy(stv[hs][:, t, :], banks[b]),
                             r=[("bank", b)], w=[("stv", hs)])
                    continue
                for h in range(4):
                    b = 4 + (n_acc % 4)
                    n_acc += 1
                    col = typ * 512 + h * 128
                    for kc in range(KC):
                        P.op("pe", lambda e, b=b, kc=kc, col=col, hs=hs: e.matmul(
                            banks[b], wbf[:, kc, col:col + 128], hT[hs][:, kc, :],
                            start=(kc == 0), stop=(kc == KC - 1)),
                            r=[("hT", hs), ("wbf", kc)], w=[("bank", b)])
                    if typ == 0:
                        P.op("act", lambda e, b=b, h=h, hs=hs: e.activation(stq[hs][:, h, :], banks[b], AF.Copy, scale=scale),
                             r=[("bank", b)], w=[("stq", hs)])
                    elif typ == 1:
                        P.op("dve", lambda e, b=b, h=h, hs=hs: e.tensor_copy(stk[hs][:, h, :], banks[b]),
                             r=[("bank", b)], w=[("stk", hs)])
                    else:
                        P.op("act", lambda e, b=b, h=h, hs=hs: e.activation(stg[hs][:, h, :], banks[b], AF.Silu),
                             r=[("bank", b)], w=[("stg", hs)])
            cols = slice(G * 512, (G + 1) * 512)
            P.dma("sp", qT_d.rearrange("h p t -> p h t")[:, :, cols], stq[hs], r=[("stq", hs)], w=[("qT_d", G)])
            P.dma("sp", kT_d.rearrange("h p t -> p h t")[:, :, cols], stk[hs], r=[("stk", hs)], w=[("kT_d", G)])
            P.dma("sp", sgT_d.rearrange("h p t -> p h t")[:, :, cols], stg[hs], r=[("stg", hs)], w=[("sgT_d", G)])
            P.dma("sp", v_d[G * 512:(G + 1) * 512, :].rearrange("(t p) c -> p t c", p=128), stv[hs],
                  r=[("stv", hs)], w=[("v_d", G)])
        rec.barrier()
        sb_reset()
        wst2 = [P.sb(f"wsto{i}", [128, 2048], F32) for i in range(2)]
        wob = P.sb("wob", [128, 4, D_MODEL], BF16)
        load_wout(woo[oi], wob, wst2)
        kT = [P.sb(f"kT{i}", [128, S], BF16) for i in range(2)]
        vv = [P.sb(f"vv{i}", [128, NT, 128], BF16) for i in range(2)]
        NQ = 3
        qg = [P.sb(f"qg{i}", [128, 512], BF16) for i in range(NQ)]
        sg = [P.sb(f"sg{i}", [128, 512], BF16) for i in range(NQ)]
        e_sb = [P.sb(f"e{i}", [128, 512], F32) for i in range(2)]
        sp_sb = [P.sb(f"sp{i}", [128, 512], BF16) for i in range(3)]
        rcs = [P.sb(f"rcs{i}", [128, 512], F32) for i in range(3)]
        w_sb = [P.sb(f"w{i}", [128, 512], BF16) for i in range(2)]
        crep = [P.sb(f"crep{i}", [128, 512], F32) for i in range(2)]
        ogs = [P.sb(f"ogs{i}", [128, 512], BF16) for i in range(2)]
        units = []
        gi = 0
        for h in range(4):
            for g in range(NG):
                kbs = list(range(4 * g + 3, -1, -1))
                for n, kb in enumerate(kbs):
                    units.append(dict(h=h, g=g, kb=kb, first=(n == 0), last=(n == len(kbs) - 1),
                                      diag=(kb - 4 * g) if kb >= 4 * g else None, gi=gi))
                gi += 1
        NU = len(units)

        def load_head(h):
            s = h % 2
            P.dma("sp", kT[s], kT_d[h], r=[("kT_d", G) for G in range(NG)], w=[("kT", s)])
            P.dma("sp", vv[s], v_d.rearrange("(n p) (h d) -> p n h d", p=128, h=4)[:, :, h, :],
                  r=[("v_d", G) for G in range(NG)], w=[("vv", s)])

        def load_group(h, g, gi):
            s = gi % NQ
            cols = slice(g * 512, (g + 1) * 512)
            P.dma("sp", qg[s], qT_d[h][:, cols], r=[("qT_d", g)], w=[("qg", s)])
            P.dma("sp", sg[s], sgT_d[h][:, cols], r=[("sgT_d", g)], w=[("sg", s)])

        load_head(0)
        load_group(0, 0, 0)
        for step in range(NU + 2):
            if 3 <= step < NU + 3 and step - 3 < NU:
                u3 = units[step - 3]
                if u3["first"] and u3["g"] == 0 and u3["h"] + 1 < 4:
                    load_head(u3["h"] + 1)
            if step < NU:
                u = units[step]
                h, g, kb, gi_ = u["h"], u["g"], u["kb"], u["gi"]
                hsl, qs = h % 2, gi_ % NQ
                if u["first"]:
                    if step + (4 * g + 4) < NU:
                        un = units[step + 4 * g + 4]
                        load_group(un["h"], un["g"], un["gi"])
                zb = step % 2
                kblk = kT[hsl][:, kb * 128:(kb + 1) * 128]
                dg = u["diag"]
                P.op("pe", lambda e, zb=zb, kblk=kblk, qs=qs, dg=dg: e.matmul(
                    banks[zb], kblk, qg[qs], start=True, stop=(dg is None)),
                    r=[("kT", hsl), ("qg", qs)], w=[("bank", zb)])
                if dg is not None:
                    P.op("pe", lambda e, zb=zb, dg=dg: e.matmul(banks[zb], ident, negmask[:, dg, :], start=False, stop=True),
                         r=CONST_KEYS, w=[("bank", zb)])
                es, ss_ = step % 2, step % 3
                P.op("act", lambda e, zb=zb, es=es: e.activation(e_sb[es], banks[zb], AF.Exp),
                     r=[("bank", zb)], w=[("e", es)])
                P.op("act", lambda e, es=es, ss_=ss_: e.activation(sp_sb[ss_], e_sb[es], AF.Ln, bias=1.0),
                     r=[("e", es)], w=[("sp", ss_)])
            if 0 <= step - 1 < NU:
                s1 = step - 1
                u = units[s1]
                h, g, kb, gi_ = u["h"], u["g"], u["kb"], u["gi"]
                hsl, qs = h % 2, gi_ % NQ
                ss_ = s1 % 3
                rb, ab = 2 + s1 % 2, 4 + s1 % 2
                kblk = kT[hsl][:, kb * 128:(kb + 1) * 128]
                dg = u["diag"]
                P.op("pe", lambda e, rb=rb, ss_=ss_: e.matmul(banks[rb], trineg, sp_sb[ss_], start=True, stop=False),
                     r=[("sp", ss_), ("const", 1)], w=[("bank", rb)])
                P.op("pe", lambda e, rb=rb, kblk=kblk, qs=qs, dg=dg: e.matmul(
                    banks[rb], kblk, qg[qs], start=False, stop=(dg is None)),
                    r=[("kT", hsl), ("qg", qs)], w=[("bank", rb)])
                if dg is not None:
                    P.op("pe", lambda e, rb=rb, dg=dg: e.matmul(banks[rb], ident, negmask[:, dg, :], start=False, stop=True),
                         r=CONST_KEYS, w=[("bank", rb)])
                P.op("pe", lambda e, ab=ab, ss_=ss_: e.matmul(banks[ab], onesneg, sp_sb[ss_], start=True, stop=True),
                     r=[("sp", ss_), ("const", 2)], w=[("bank", ab)])
                cs_old, cs_new = s1 % 2, (s1 + 1) % 2
                rs = s1 % 3
                if u["first"]:
                    P.op("dve", lambda e, rs=rs, rb=rb: e.tensor_copy(rcs[rs], banks[rb]),
                         r=[("bank", rb)], w=[("rcs", rs)])
                    P.op("dve", lambda e, cs_new=cs_new, ab=ab: e.tensor_copy(crep[cs_new], banks[ab]),
                         r=[("bank", ab)], w=[("crep", cs_new)])
                else:
                    P.op("dve", lambda e, rs=rs, rb=rb, cs_old=cs_old: e.tensor_tensor(rcs[rs], banks[rb], crep[cs_old], ALU.add),
                         r=[("bank", rb), ("crep", cs_old)], w=[("rcs", rs)])
                    if not u["last"]:
                        P.op("dve", lambda e, cs_new=cs_new, cs_old=cs_old, ab=ab: e.tensor_tensor(
                            crep[cs_new], banks[ab], crep[cs_old], ALU.add),
                            r=[("bank", ab), ("crep", cs_old)], w=[("crep", cs_new)])
            if 0 <= step - 2 < NU:
                s2 = step - 2
                u = units[s2]
                h, g, kb, gi_ = u["h"], u["g"], u["kb"], u["gi"]
                hsl, qs = h % 2, gi_ % NQ
                rs, ws = s2 % 3, s2 % 2
                ob = 6 + gi_ % 2
                P.op("act", lambda e, rs=rs, ws=ws: e.activation(w_sb[ws], rcs[rs], AF.Exp),
                     r=[("rcs", rs)], w=[("w", ws)])
                P.op("pe", lambda e, ob=ob, hsl=hsl, kb=kb, ws=ws, u=u: e.matmul(
                    banks[ob], vv[hsl][:, kb, :], w_sb[ws], start=u["first"], stop=u["last"]),
                    r=[("vv", hsl), ("w", ws)], w=[("bank", ob)])
                if u["last"]:
                    os_ = gi_ % 2
                    P.op("dve", lambda e, ob=ob, qs=qs, os_=os_: e.tensor_tensor(ogs[os_], banks[ob], sg[qs], ALU.mult),
                         r=[("bank", ob), ("sg", qs)], w=[("ogs", os_)])
                    P.dma("sp", ogT_d[h][:, g * 512:(g + 1) * 512], ogs[os_], r=[("ogs", os_)], w=[("ogT_d", g)])
        rec.barrier()
        P.sb_off = PERSIST + 2 * 8192 + 4 * D_MODEL * 2
        out_phase(x_src, wob, last)
        rec.barrier()

    def even_layer(li, ei, x_src, last):
        rec = P.rec
        sb_reset()
        NCOL = 1664
        wbf = P.sb("wbf", [128, KC, NCOL], BF16)
        lnT = P.sb("lnT", [128, KC], F32)
        xin = [P.sb(f"xin{i}", [128, D_MODEL], F32) for i in range(2)]
        junk = P.sb("junk", [128, D_MODEL], BF16)
        ss = [P.sb(f"ss{i}", [128, 4], F32) for i in range(2)]
        rstd = [P.sb(f"rstd{i}", [128, 4], F32) for i in range(2)]
        xn = [P.sb(f"xn{i}", [128, D_MODEL], BF16) for i in range(2)]
        hT = [P.sb(f"hT{i}", [128, KC, 512], BF16) for i in range(2)]
        wqk = P.sb("wqk", [128, 320], F32)
        esink = P.sb("esink", [128, 4], F32)
        lbv = P.sb("lbv", [128, 4], F32)
        lb = P.sb("lb", [128, 2], F32)
        oml = P.sb("oml", [128, 2], F32)
        gn = P.sb("gn", [128, 1], F32)
        swam = P.sb("swam", [128, 2, 128], BF16)
        m16 = P.sb("m16", [128, 128], BF16)
        bmk = P.sb("bmk", [128, 8], BF16)
        onesf = P.sb("onesf", [128, 512], F32)
        ropeg = [P.sb(f"ropeg{i}", [128, 4, 64], F32) for i in range(2)]
        P.dma("sp", wqk, qkn[ei].partition_broadcast(128).rearrange("p a b -> p (a b)"), w=["wqk"])
        P.dma("sp", esink, snk[ei].partition_broadcast(128).rearrange("p a b -> p (a b)"), w=["esink"])
        P.dma("sp", lbv, lbt, w=["lbv"])
        P.dma("sp", gn, gnt[ei], w=["gn"])
        P.dma("sp", swam[:, 0, :], emask[0][:, 0:128], w=["swam"])
        P.dma("sp", swam[:, 1, :], emask[1][:, 0:128], w=["swam"])
        P.dma("sp", m16, emask[2][:, 0:128], w=["m16"])
        P.dma("sp", bmk, bmask, w=["bmk"])
        P.op("pool", lambda e: e.memset(onesf, 1.0), w=["onesf"])
        P.op("act", lambda e: e.activation(esink, esink, AF.Exp), r=["esink"], w=["esink"])
        if ei == 0:
            P.op("dve", lambda e: e.memset(lb, 0.0), w=["lb"])
            P.op("dve", lambda e: e.memset(oml, 1.0), w=["oml"])
        else:
            P.op("dve", lambda e: e.tensor_tensor(lb, lbv[:, 0:2], lbv[:, 2:4], ALU.subtract), r=["lbv"], w=["lb"])
            P.op("act", lambda e: e.activation(lb, lb, AF.Exp), r=["lb"], w=["lb"])
            P.op("dve", lambda e: e.tensor_scalar(lb, lb, 1.0, None, ALU.add), r=["lb"], w=["lb"])
            P.op("dve", lambda e: e.reciprocal(lb, lb), r=["lb"], w=["lb"])
            P.op("dve", lambda e: e.tensor_scalar(oml, lb, -1.0, 1.0, ALU.mult, ALU.add), r=["lb"], w=["oml"])
        mix_start = P.sb_off
        wst = [P.sb(f"wst{i}", [128, 2048], F32) for i in range(2)]
        load_weights(wie[ei], NCOL, lne[ei], wbf, wst, lnT)
        rec.barrier()
        P.sb_off = mix_start
        sga = [P.sb(f"sga{i}", [128, 256], F32) for i in range(4)]
        sq = P.sb("sq", [128, 320], F32)
        s5 = P.sb("s5", [128, 5], F32)
        qk = P.sb("qk", [128, 5, 64], F32)
        rt = [P.sb(f"rt{i}", [128, 5, 32], F32) for i in range(4)]
        qkr = P.sb("qkr", [128, 6, 64], BF16)
        qkT = [P.sb(f"qkT{i}", [128, 3, 128], BF16) for i in range(3)]
        vaug = [P.sb(f"vaug{i}", [128, 66], BF16) for i in range(3)]
        qz = [P.sb(f"qz{i}", [128, 4, 128], BF16) for i in range(3)]
        ex = [P.sb(f"ex{i}", [128, 2, 2, 128], F32) for i in range(2)]
        pm = [P.sb(f"pm{i}", [128, 2, 2, 128], BF16) for i in range(2)]
        den = P.sb("den", [128, 4], F32)
        ya1 = P.sb("ya1", [128, 4, 64], F32)
        ya = P.sb("ya", [128, 256], BF16)
        yaT = [P.sb("yaT0", [128, 2, 512], BF16)] * 2
        qs = [P.sb(f"qs{i}", [128, 512], F32) for i in range(2)]
        sgb = [P.sb(f"sgb{i}", [128, 512], F32) for i in range(2)]
        ef = [P.sb(f"ef{i}", [128, 512], F32) for i in range(2)]
        fbuf = P.sb("fbuf", [128, 512], F32)
        logf = P.sb("logf", [128, 512], F32)
        kk = P.sb("kk", [128, 512], F32)
        Gb = [[P.sb(f"Gb{h}", [128, 516], F32)] * 2 for h in range(2)]
        gcar = [P.sb(f"gcar{h}", [128, 1], F32) for h in range(2)]
        aa = P.sb("aa", [128, 512], F32)
        ea = P.sb("ea", [128, 512], F32)
        qt = [P.sb(f"qt{i}", [128, 512], BF16) for i in range(2)]
        kt = [P.sb(f"kt{i}", [128, 512], BF16) for i in range(2)]
        kh = [P.sb(f"kh{i}", [128, 512], BF16) for i in range(2)]
        dec = [P.sb(f"dec{i}", [128, 32], F32) for i in range(2)]
        vt = [P.sb(f"vt{i}", [128, 256], BF16) for i in range(4)]
        attT = [P.sb(f"attT{i}", [128, 128], BF16) for i in range(2)]
        kblk = [P.sb(f"kblk{i}", [128, 8, 128], BF16) for i in range(2)]
        Sf = [[P.sb(f"Sf{h}{i}", [128, 128], F32) for i in range(2)] for h in range(2)]
        NSB = 4
        Sb = [[P.sb(f"Sb{h}{i}", [128, 128], BF16) for i in range(NSB)] for h in range(2)]
        osq = P.sb("osq", [128, 128], BF16)
        rs = P.sb("rs", [128, 128], F32)
        otmp = P.sb("otmp", [128, 128], F32)
        ybT = [P.sb("ybT0", [128, 2, 512], BF16)] * 2
        for i in range(3):
            P.op("pool", lambda e, i=i: e.memset(vaug[i], 1.0), w=[("vaug", i)])
            P.op("pool", lambda e, i=i: e.memset(qz[i], 0.0), w=[("qkT", i)])
        for h in range(2):
            P.op("pool", lambda e, h=h: e.memset(Sf[h][0], 0.0), w=[("Sf", h, 0)])
            P.op("pool", lambda e, h=h: e.memset(Sb[h][0], 0.0), w=[("Sb", h, 0)])
            P.op("pool", lambda e, h=h: e.memset(gcar[h], 0.0), w=[("gcar", h)])
        ACC = [2, 3]
        sc_v = [pview(4, 0, 512).rearrange("p (a b c) -> p a b c", a=2, b=2), pview(5, 0, 512).rearrange("p (a b c) -> p a b c", a=2, b=2)]
        av_ps = pview(6, 0, 264, inner=66)
        qkT_ps = pview(6, 264, 192, BF16, inner=128)
        att_ps = pview(7, 0, 128)
        khT_ps = pview(7, 128, 64, BF16)
        o_ps = pview(7, 192, 128)
        yT_ps = pview(7, 320, 128, BF16, inner=128)
        ssq_ps = pview(7, 384, 128)
        sc_v = [pview(4, 0, 512).rearrange("p (a b c) -> p a b c", a=2, b=2)]
        U_ps = pview(5, 0, 512, inner=128)
        sstate = {"sb": [0, 0], "sf": [0, 0], "nacc": 0}

        def acc_bank():
            b = ACC[sstate["nacc"] % 2]
            sstate["nacc"] += 1
            return b

        def project_fm(col, hs):
            b = acc_bank()
            for kc in range(KC):
                P.op("pe", lambda e, b=b, kc=kc, col=col, hs=hs: e.matmul(
                    banks[b], wbf[:, kc, col:col + 128], hT[hs][:, kc, :], start=(kc == 0), stop=(kc == KC - 1)),
                    r=[("hT", hs), ("wbf", kc)], w=[("bank", b)])
            return b

        def project_tm(c0, c1, hs, t):
            b = acc_bank()
            for kc in range(KC):
                P.op("pe", lambda e, b=b, kc=kc, hs=hs, t=t: e.matmul(
                    banks[b][:, 0:c1 - c0], hT[hs][:, kc, t * 128:(t + 1) * 128], wbf[:, kc, c0:c1],
                    start=(kc == 0), stop=(kc == KC - 1)),
                    r=[("hT", hs), ("wbf", kc)], w=[("bank", b)])
            return b

        def swa_tile(G, t, hs):
            ti = G * 4 + t
            gp = G % 2
            cur, prv = ti % 3, (ti - 1) % 3
            b = project_tm(0, 384, hs, t)
            pa = banks[b]
            P.op("act", lambda e: e.activation(sq, pa[:, 0:320], AF.Square), r=[("bank", b)], w=["sq"])
            P.op("dve", lambda e: e.tensor_reduce(s5, sq.rearrange("p (a b) -> p a b", b=64), AX.X, ALU.add), r=["sq"], w=["s5"])
            P.op("act", lambda e: e.activation(s5, s5, AF.Ln, bias=EPS, scale=1.0 / 64), r=["s5"], w=["s5"])
            P.op("act", lambda e: e.activation(s5, s5, AF.Exp, scale=-0.5), r=["s5"], w=["s5"])
            P.op("dve", lambda e: e.tensor_tensor(qk, pa[:, 0:320].rearrange("p (a b) -> p a b", b=64),
                                                  s5.unsqueeze(2).to_broadcast([128, 5, 64]), ALU.mult),
                 r=[("bank", b), "s5"], w=["qk"])
            P.op("dve", lambda e, cur=cur: e.tensor_copy(vaug[cur][:, 0:64], pa[:, 320:384]), r=[("bank", b)], w=[("vaug", cur)])
            P.op("dve", lambda e: e.tensor_tensor(qk, qk, wqk.rearrange("p (a b) -> p a b", b=64), ALU.mult),
                 r=["qk", "wqk"], w=["qk"])
            cs = ropeg[gp][:, t, :]
            cosb = cs[:, 0:32].unsqueeze(1).to_broadcast([128, 5, 32])
            sinb = cs[:, 32:64].unsqueeze(1).to_broadcast([128, 5, 32])
            x1, x2 = qk[:, :, 0:32], qk[:, :, 32:64]
            P.op("dve", lambda e: e.tensor_tensor(rt[0], x1, cosb, ALU.mult), r=["qk", ("ropeg", gp)], w=[("rt", 0)])
            P.op("pool", lambda e: e.tensor_tensor(rt[1], x2, sinb, ALU.mult), r=["qk", ("ropeg", gp)], w=[("rt", 1)])
            P.op("dve", lambda e: e.tensor_tensor(rt[2], x2, cosb, ALU.mult), r=["qk", ("ropeg", gp)], w=[("rt", 2)])
            P.op("pool", lambda e: e.tensor_tensor(rt[3], x1, sinb, ALU.mult), r=["qk", ("ropeg", gp)], w=[("rt", 3)])
            P.op("dve", lambda e: e.tensor_tensor(qkr[:, 0:5, 0:32], rt[0], rt[1], ALU.subtract),
                 r=[("rt", 0), ("rt", 1)], w=["qkr"])
            P.op("dve", lambda e: e.tensor_tensor(qkr[:, 0:5, 32:64], rt[2], rt[3], ALU.add),
                 r=[("rt", 2), ("rt", 3)], w=["qkr"])
            P.op("dve", lambda e: e.tensor_copy(qkr[:, 5, :], qkr[:, 4, :]), r=["qkr"], w=["qkr"])
            for i in range(3):
                P.op("pe", lambda e, i=i: e.transpose(qkT_ps[:, i, :], qkr[:, 2 * i:2 * i + 2, :].rearrange("p a b -> p (a b)"), ident),
                     r=["qkr", ("const", 0)], w=[("bank", 6)])
            P.op("act", lambda e, cur=cur: e.activation(qkT[cur], qkT_ps, AF.Copy), r=[("bank", 6)], w=[("qkT", cur)])
            qz4 = qz[cur].rearrange("p (a b) t -> p a b t", b=2)
            P.op("dve", lambda e, qz4=qz4: e.tensor_copy(qz4[0:64, :, 0, :], qkT_ps[0:64, 0:2, :]), r=[("bank", 6)], w=[("qkT", cur)])
            P.op("dve", lambda e, qz4=qz4: e.tensor_copy(qz4[64:128, :, 1, :], qkT_ps[64:128, 0:2, :]), r=[("bank", 6)], w=[("qkT", cur)])
            import os
            lvl = int(os.environ.get("DBG_SWA", "9"))
            if lvl < 1:
                return
            has_prev = ti > 0
            nkb = 2 if has_prev else 1
            for hp in range(2):
                sc = sc_v[0]
                xs = hp
                for kb in range(nkb):
                    src = qkT[cur] if kb == 0 else qkT[prv]
                    P.op("pe", lambda e, sc=sc, kb=kb, src=src, hp=hp, cur=cur: e.matmul(
                        sc[:, kb, :, :], src[:, 2, :], qz[cur][:, 2 * hp:2 * hp + 2, :], start=True, stop=True),
                        r=[("qkT", cur), ("qkT", prv)], w=[("bank", 4)])
                if lvl < 2:
                    continue
                P.op("act", lambda e, sc=sc, xs=xs, nkb=nkb: e.activation(ex[xs][:, 0:nkb], sc[:, 0:nkb], AF.Exp, scale=0.125),
                     r=[("bank", 4)], w=[("ex", xs)])
                P.op("dve", lambda e, xs=xs, nkb=nkb: e.tensor_tensor(
                    pm[xs][:, 0:nkb], ex[xs][:, 0:nkb], swam[:, 0:nkb].unsqueeze(2).to_broadcast([128, nkb, 2, 128]), ALU.mult),
                    r=[("ex", xs), "swam"], w=[("pm", xs)])
                if lvl < 3:
                    continue
                for hi in range(2):
                    hh = 2 * hp + hi
                    for kb in range(nkb):
                        vsrc = vaug[cur] if kb == 0 else vaug[prv]
                        P.op("pe", lambda e, hh=hh, xs=xs, kb=kb, hi=hi, vsrc=vsrc, nkb=nkb: e.matmul(
                            av_ps[:, hh, 0:65], pm[xs][:, kb, hi, :], vsrc[:, 0:65], start=(kb == 0), stop=(kb == nkb - 1)),
                            r=[("pm", xs), ("vaug", cur), ("vaug", prv)], w=[("bank", 6)])
            if lvl < 4:
                return
            P.op("dve", lambda e: e.tensor_tensor(den, av_ps[:, :, 64], esink, ALU.add), r=[("bank", 6), "esink"], w=["den"])
            P.op("dve", lambda e: e.reciprocal(den, den), r=["den"], w=["den"])
            P.op("dve", lambda e: e.tensor_tensor(ya1, av_ps[:, :, 0:64], den.unsqueeze(2).to_broadcast([128, 4, 64]), ALU.mult),
                 r=[("bank", 6), "den"], w=["ya1"])
            P.op("pool", lambda e, t=t: e.tensor_tensor(ya, ya1.rearrange("p a b -> p (a b)"), sga[t], ALU.mult),
                 r=["ya1", ("sga", t)], w=["ya"])
            for i in range(2):
                P.op("pe", lambda e, i=i: e.transpose(yT_ps[:, i, :], ya[:, i * 128:(i + 1) * 128], ident),
                     r=["ya", ("const", 0)], w=[("bank", 7)])
            P.op("act", lambda e, gp=gp, t=t: e.activation(yaT[gp][:, :, t * 128:(t + 1) * 128], yT_ps, AF.Copy),
                 r=[("bank", 7)], w=[("yaT", 0)])

        def hgrn_group_prep(G, hh):
            gp = G % 2
            Gc, Gp = Gb[hh][gp], Gb[hh][1 - gp]
            P.op("dve", lambda e: e.tensor_scalar(fbuf, ef[hh], 1.0, None, ALU.add), r=[("ef", hh)], w=["fbuf"])
            P.op("dve", lambda e: e.reciprocal(fbuf, fbuf), r=["fbuf"], w=["fbuf"])
            P.op("dve", lambda e: e.tensor_scalar(fbuf, fbuf, oml[:, hh:hh + 1], lb[:, hh:hh + 1], ALU.mult, ALU.add),
                 r=["fbuf", "oml", "lb"], w=["fbuf"])
            P.op("act", lambda e: e.activation(logf, fbuf, AF.Ln), r=["fbuf"], w=["logf"])
            P.op("pool", lambda e: e.tensor_scalar(kk, fbuf, -1.0, 1.0, ALU.mult, ALU.add), r=["fbuf"], w=["kk"])
            P.op("dve", lambda e: e.tensor_copy(Gc[:, 0:1], gcar[hh]), r=[("gcar", hh)], w=[("Gb", hh)])
            P.op("dve", lambda e: e.tensor_tensor_scan(Gc[:, 1:513], onesf, logf, Gc[:, 0:1], ALU.mult, ALU.add),
                 r=["onesf", "logf", ("Gb", hh)], w=[("Gb", hh)])
            P.op("dve", lambda e: e.tensor_copy(gcar[hh], Gc[:, 512:513]), r=[("Gb", hh)], w=[("gcar", hh)])
            Gi = Gc[:, 1:513].rearrange("p (c i) -> p c i", i=16)
            Rc = Gc[:, 0:512].rearrange("p (c i) -> p c i", i=16)[:, :, 0:1]
            Ec = Gc[:, 1:513].rearrange("p (c i) -> p c i", i=16)[:, :, 15:16]
            a3 = aa.rearrange("p (c i) -> p c i", i=16)
            P.op("dve", lambda e: e.tensor_tensor(a3, Gi, Rc.to_broadcast([128, 32, 16]), ALU.subtract),
                 r=[("Gb", hh)], w=["aa"])
            P.op("act", lambda e: e.activation(ea, aa, AF.Exp), r=["aa"], w=["ea"])
            P.op("dve", lambda e: e.tensor_tensor(qt[hh], qs[hh], ea, ALU.mult), r=[("qs", hh), "ea"], w=[("qt", hh)])
            P.op("pool", lambda e: e.tensor_scalar(aa, aa, -1.0, 80.0, ALU.mult, ALU.min), r=["aa"], w=["aa"])
            P.op("act", lambda e: e.activation(ea, aa, AF.Exp), r=["aa"], w=["ea"])
            P.op("dve", lambda e: e.tensor_tensor(kt[hh], kk, ea, ALU.mult), r=["kk", "ea"], w=[("kt", hh)])
            P.op("dve", lambda e: e.tensor_tensor(a3, Ec.to_broadcast([128, 32, 16]), Gi, ALU.subtract),
                 r=[("Gb", hh)], w=["aa"])
            P.op("act", lambda e: e.activation(ea, aa, AF.Exp), r=["aa"], w=["ea"])
            P.op("dve", lambda e: e.tensor_tensor(kh[hh], kk, ea, ALU.mult), r=["kk", "ea"], w=[("kh", hh)])
            P.op("dve", lambda e: e.tensor_tensor(dec[hh], Ec.rearrange("p c i -> p (c i)"), Rc.rearrange("p c i -> p (c i)"), ALU.subtract),
                 r=[("Gb", hh)], w=[("dec", hh)])
            P.op("act", lambda e: e.activation(dec[hh], dec[hh], AF.Exp), r=[("dec", hh)], w=[("dec", hh)])

        def hgrn_tile(G, t, hh):
            gp = G % 2
            cols = slice(t * 128, (t + 1) * 128)
            vth = vt[t][:, hh * 128:(hh + 1) * 128]
            sa = (t * 2 + hh) % 2
            P.op("pe", lambda e: e.matmul(att_ps, kt[hh][:, cols], qt[hh][:, cols], start=True, stop=True),
                 r=[("kt", hh), ("qt", hh)], w=[("bank", 7)])
            P.op("dve", lambda e: e.tensor_tensor(attT[sa], att_ps, m16, ALU.mult), r=[("bank", 7), "m16"], w=[("attT", sa)])
            P.op("pe", lambda e: e.transpose(khT_ps, kh[hh][:, cols], ident), r=[("kh", hh), ("const", 0)], w=[("bank", 7)])
            P.op("dve", lambda e: e.tensor_tensor(
                kblk[sa], khT_ps.unsqueeze(1).to_broadcast([128, 8, 128]), bmk.unsqueeze(2).to_broadcast([128, 8, 128]), ALU.mult),
                r=[("bank", 7), "bmk"], w=[("kblk", sa)])
            P.op("pe", lambda e: e.matmul(o_ps, vth, attT[sa], start=True, stop=False),
                 r=[("vt", t), ("attT", sa)], w=[("bank", 7)])
            for c in range(8):
                if c % 4 == 0:
                    for c2 in range(c, c + 4):
                        P.op("pe", lambda e, c2=c2: e.matmul(U_ps[:, c2 % 4, :], kblk[sa][:, c2, :], vth, start=True, stop=True),
                             r=[("kblk", sa), ("vt", t)], w=[("bank", 5)])
                sbi = sstate["sb"][hh]
                sfi = sstate["sf"][hh]
                ccol = t * 128 + 16 * c
                P.op("pe", lambda e, sbi=sbi, ccol=ccol, c=c: e.matmul(
                    o_ps[:, 16 * c:16 * c + 16], Sb[hh][sbi], qt[hh][:, ccol:ccol + 16], start=False, stop=(c == 7)),
                    r=[("Sb", hh, sbi), ("qt", hh)], w=[("bank", 7)])
                nsf, nsb = 1 - sfi, (sbi + 1) % NSB
                dcol = dec[hh][:, 8 * t + c:8 * t + c + 1]
                P.op("dve", lambda e, sfi=sfi, nsf=nsf, dcol=dcol, c=c: e.scalar_tensor_tensor(
                    Sf[hh][nsf], Sf[hh][sfi], dcol, U_ps[:, c % 4, :], ALU.mult, ALU.add),
                    r=[("Sf", hh, sfi), ("dec", hh), ("bank", 5)], w=[("Sf", hh, nsf)])
                P.op("pool", lambda e, nsf=nsf, nsb=nsb: e.tensor_copy(Sb[hh][nsb], Sf[hh][nsf]),
                     r=[("Sf", hh, nsf)], w=[("Sb", hh, nsb)])
                sstate["sb"][hh] = nsb
                sstate["sf"][hh] = nsf
            P.op("act", lambda e: e.activation(osq, o_ps, AF.Square), r=[("bank", 7)], w=["osq"])
            P.op("pe", lambda e: e.matmul(ssq_ps, ones, osq, start=True, stop=True), r=["osq", ("const", 3)], w=[("bank", 7)])
            P.op("act", lambda e: e.activation(rs, ssq_ps, AF.Ln, bias=EPS, scale=1.0 / 128), r=[("bank", 7)], w=["rs"])
            P.op("act", lambda e: e.activation(rs, rs, AF.Exp, scale=-0.5), r=["rs"], w=["rs"])
            P.op("dve", lambda e: e.tensor_tensor(otmp, o_ps, rs, ALU.mult), r=[("bank", 7), "rs"], w=["otmp"])
            P.op("dve", lambda e: e.scalar_tensor_tensor(ybT[gp][:, hh, cols], otmp, gn[:, 0:1], sgb[hh][:, cols], ALU.mult, ALU.mult),
                 r=["otmp", "gn", ("sgb", hh)], w=[("ybT", 0)])

        def group_body(G):
            hs = G % 2
            gp = G % 2
            cols = slice(G * 512, (G + 1) * 512)
            P.dma("sp", ropeg[gp], rope[G * 512:(G + 1) * 512, :].rearrange("(t p) c -> p t c", p=128), w=[("ropeg", gp)])
            for hh in range(2):
                b = project_fm(640 + hh * 128, hs)
                P.op("act", lambda e, b=b, hh=hh: e.activation(qs[hh], banks[b], AF.Silu), r=[("bank", b)], w=[("qs", hh)])
            for hh in range(2):
                b = project_fm(1408 + hh * 128, hs)
                P.op("act", lambda e, b=b, hh=hh: e.activation(sgb[hh], banks[b], AF.Silu), r=[("bank", b)], w=[("sgb", hh)])
            for t in range(4):
                b = project_tm(384, 640, hs, t)
                P.op("act", lambda e, b=b, t=t: e.activation(sga[t], banks[b][:, 0:256], AF.Silu), r=[("bank", b)], w=[("sga", t)])
            for hh in range(2):
                b = project_fm(896 + hh * 128, hs)
                P.op("act", lambda e, b=b, hh=hh: e.activation(ef[hh], banks[b], AF.Exp, scale=-1.0), r=[("bank", b)], w=[("ef", hh)])
            for t in range(4):
                b = project_tm(1152, 1408, hs, t)
                P.op("dve", lambda e, b=b, t=t: e.tensor_copy(vt[t], banks[b][:, 0:256]), r=[("bank", b)], w=[("vt", t)])
            import os
            dbg = int(os.environ.get("DBG_EVEN", "7"))
            for hh in range(2):
                if dbg & 2:
                    hgrn_group_prep(G, hh)
            for t in range(4):
                if dbg & 1:
                    swa_tile(G, t, hs)
                for hh in range(2):
                    if dbg & 4:
                        hgrn_tile(G, t, hh)
            P.dma("sp", ogT_d[0:2].rearrange("c p t -> p c t")[:, :, cols], yaT[gp], r=[("yaT", 0)], w=[("ogT_d", G)])
            P.dma("sp", ogT_d[2:4].rearrange("c p t -> p c t")[:, :, cols], ybT[gp], r=[("ybT", 0)], w=[("ogT_d", G)])

        norm_transpose_group(0, x_src, xin, junk, ss, rstd, xn, hT, 0, tph, even=True)
        for G in range(NG):
            if G + 1 < NG:
                norm_transpose_group(G + 1, x_src, xin, junk, ss, rstd, xn, hT, (G + 1) % 2, tph, even=True)
            group_body(G)
        rec.barrier()
        sb_reset()
        wst2 = [P.sb(f"wste{i}", [128, 2048], F32) for i in range(2)]
        wob = P.sb("wob", [128, 4, D_MODEL], BF16)
        load_wout(woe[ei], wob, wst2)
        out_phase(x_src, wob, last)
        rec.barrier()

    cur = x_in
    no = ne = 0
    for li, typ in enumerate(layers):
        last = li == len(layers) - 1
        if typ == "o":
            odd_layer(li, no, cur, last)
            no += 1
        else:
            even_layer(li, ne, cur, last)
            ne += 1
        cur = xres
    P.rec.barrier()
    P.rec.emit(nc)
    return P


def _bf(a):
    return np.asarray(a, dtype=np.float32).astype(ml_dtypes.bfloat16)


def make_consts():
    j = np.arange(128)[:, None]
    s = np.arange(128)[None, :]
    ident = (j == s).astype(np.float32)
    trineg = -(j >= s).astype(np.float32)
    onesneg = -np.ones((128, 128), np.float32)
    ones = np.ones((128, 128), np.float32)
    cmat = _bf(np.stack([ident, trineg, onesneg, ones]))
    tri = np.where(j >= s, NEG, 0.0).astype(np.float32)
    cm = np.zeros((4, 128, 512), np.float32)
    for d in range(4):
        for i in range(4):
            blk = cm[d][:, i * 128:(i + 1) * 128]
            if i < d:
                blk[:] = NEG
            elif i == d:
                blk[:] = tri
    return cmat, _bf(cm)


def make_in_maps(inputs, S, layers):
    x = np.asarray(inputs["x"], np.float32)
    cmat, cmask = make_consts()
    half = 32
    inv_freq = (1.0 / (np.float32(10000.0) ** (np.arange(half, dtype=np.float32) / np.float32(half)))).astype(np.float32)
    ang = (np.arange(S, dtype=np.float32)[:, None] * inv_freq[None, :]).astype(np.float32)
    rope_tab = np.ascontiguousarray(np.concatenate([np.cos(ang), np.sin(ang)], axis=1).astype(np.float32))
    kk_ = np.arange(128)[:, None]
    qq_ = np.arange(128)[None, :]
    em = np.zeros((3, 128, 512), np.float32)
    em[0][:, :128] = (kk_ <= qq_)
    em[1][:, :128] = (kk_ > qq_)
    em[2][:, :128] = ((kk_ // 16) == (qq_ // 16)) & (kk_ <= qq_)
    emask = _bf(em)
    bmask = _bf((np.arange(128)[:, None] // 16) == np.arange(8)[None, :])
    maps = []
    for c in range(8):
        b, r = c // 4, c % 4
        m = {"x": np.ascontiguousarray(x[b, :S]), "cmat": cmat, "cmask": cmask}
        no = ne = 0
        for typ in layers:
            if typ == "o":
                w = np.asarray(inputs["w_in_odd"][no], np.float32)
                cols = []
                for part in range(4):
                    cols.append(w[:, part * 2048 + r * 512: part * 2048 + (r + 1) * 512])
                m[f"wio{no}"] = np.ascontiguousarray(np.concatenate(cols, axis=1))
                m[f"lno{no}"] = np.ascontiguousarray(np.asarray(inputs["ln_odd"][no], np.float32).reshape(KC, 128).T)
                m[f"woo{no}"] = np.ascontiguousarray(np.asarray(inputs["w_out_odd"][no], np.float32)[r * 512:(r + 1) * 512])
                no += 1
            else:
                w = np.asarray(inputs["w_in_even"][ne], np.float32)
                kv = r // 2
                cols = [w[:, r * 256:(r + 1) * 256],
                        w[:, 1024 + kv * 64:1024 + (kv + 1) * 64],
                        w[:, 1152 + kv * 64:1152 + (kv + 1) * 64],
                        w[:, 1280 + r * 256:1280 + (r + 1) * 256],
                        w[:, 2304 + r * 256:2304 + (r + 1) * 256],
                        w[:, 3328 + r * 256:3328 + (r + 1) * 256],
                        w[:, 4352 + r * 256:4352 + (r + 1) * 256],
                        w[:, 5376 + r * 256:5376 + (r + 1) * 256]]
                m[f"wie{ne}"] = np.ascontiguousarray(np.concatenate(cols, axis=1))
                m[f"lne{ne}"] = np.ascontiguousarray(np.asarray(inputs["ln_even"][ne], np.float32).reshape(KC, 128).T)
                wo = np.asarray(inputs["w_out_even"][ne], np.float32)
                m[f"woe{ne}"] = np.ascontiguousarray(np.concatenate([wo[r * 256:(r + 1) * 256], wo[1024 + r * 256:1024 + (r + 1) * 256]], axis=0))
                qn = np.asarray(inputs["q_norm_a"][ne], np.float32)
                kn = np.asarray(inputs["k_norm_a"][ne], np.float32)
                m[f"qkn{ne}"] = np.ascontiguousarray(np.concatenate([qn, qn, qn, qn, kn])[None, :])
                m[f"snk{ne}"] = np.ascontiguousarray(np.asarray(inputs["sinks_a"][ne], np.float32)[4 * r:4 * r + 4][None, :])
                m[f"gnt{ne}"] = np.ascontiguousarray(np.asarray(inputs["g_norm_b"][ne], np.float32)[:, None])
                ne += 1
        if "e" in layers:
            lbw = np.asarray(inputs["lower_bounds"], np.float32)
            m["lbt"] = np.ascontiguousarray(np.stack(
                [lbw[l, (2 * r + hh) * 128:(2 * r + hh + 1) * 128] for l in range(2) for hh in range(2)], axis=1))
            m["rope"] = rope_tab
            m["emask"] = emask
            m["bmask"] = bmask
        maps.append(m)
    return maps


_CACHE = {}


def run_layers(inputs, S, layers):
    key = (S, tuple(layers))
    if key not in _CACHE:
        _CACHE[key] = build(S, layers)
    P = _CACHE[key]
    maps = make_in_maps(inputs, S, layers)
    res = run_bass_kernel_spmd(P.nc, maps, core_ids=list(range(8)))
    out = np.stack([res.results[0]["y"], res.results[4]["y"]])
    return out


def kernel(**inputs):
    return run_layers(inputs, SEQ, ["e", "o", "e", "o"]).astype(np.float32)
```

```python
import numpy as np
import ml_dtypes
import concourse.bass as bass
import concourse.mybir as mybir
from concourse.bass_utils import run_bass_kernel_spmd

F32 = mybir.dt.float32
BF16 = mybir.dt.bfloat16
AF = mybir.ActivationFunctionType
ALU = mybir.AluOpType
AX = mybir.AxisListType

D_MODEL = 2048
SEQ = 8192
EPS = 1e-6
NEG = -30000.0
TP = 4
GROUPS = [[0, 1, 2, 3], [4, 5, 6, 7]]
KC = D_MODEL // 128


class Rec:
    COMPUTE = ("pe", "act", "dve", "pool")
    ND = 40
    ROT = 20000

    def __init__(self):
        self.ops = []
        self.lastw = {}
        self.readers = {}
        self.last_on = {}
        self.n_standalone_waits = 0

    def add(self, eng, fn, r=(), w=(), kind="c", extra=()):
        idx = len(self.ops)
        raw = set()
        oth = set()
        for k in r:
            if k in self.lastw:
                raw.add(self.lastw[k])
        for k in w:
            if k in self.lastw:
                oth.add(self.lastw[k])
            for j in self.readers.get(k, ()):
                oth.add(j)
        for j in extra:
            raw.add(j)
        self.ops.append(dict(eng=eng, fn=fn, raw=raw, oth=oth - raw, kind=kind))
        for k in r:
            self.readers.setdefault(k, []).append(idx)
        for k in w:
            self.lastw[k] = idx
            self.readers[k] = []
        self.last_on[eng] = idx
        return idx

    def barrier(self):
        pend = [i for i, o in enumerate(self.ops) if o["kind"] in ("d", "cc") and not o.get("barred")]
        lasts = [v for v in self.last_on.values()]
        for i in pend:
            self.ops[i]["barred"] = True
        deps = set(pend) | set(lasts)
        for eng in ("pe", "act", "dve", "pool", "sp"):
            self.add(eng, None, kind="n", extra=deps)
        self.lastw = {}
        self.readers = {}

    def emit(self, nc):
        ops = self.ops
        n = len(ops)
        for i, o in enumerate(ops):
            keep = set()
            for d in o["raw"] | o["oth"]:
                od = ops[d]
                if od["kind"] == "n":
                    pass
                same = od["eng"] == o["eng"] and od["kind"] in ("c", "n") and o["kind"] in ("c", "n")
                if same:
                    if o["eng"] == "pe":
                        continue
                keep.add(d)
            o["deps"] = keep
        needed = [False] * n
        for o in ops:
            for d in o["deps"]:
                needed[d] = True
        sig = {}
        cnt = {e: 0 for e in ("pe", "act", "dve", "pool", "sp")}
        ndma = 0
        ncc = 0
        dma_prev = {}
        for i, o in enumerate(ops):
            if o["kind"] == "d":
                slot = ndma % self.ND
                val = 16 * (ndma // self.ND + 1)
                sig[i] = ("dma", slot, val)
                if ndma >= self.ND:
                    o["deps"].add(dma_prev[slot])
                    needed[dma_prev[slot]] = True
                dma_prev[slot] = i
                ndma += 1
            elif o["kind"] == "cc":
                ncc += 1
                sig[i] = ("cc", 0, ncc)
            elif needed[i]:
                e = o["eng"]
                cnt[e] += 1
                sig[i] = (e, (cnt[e] - 1) // self.ROT, (cnt[e] - 1) % self.ROT + 1)
        nrot = {e: max(1, (cnt[e] + self.ROT - 1) // self.ROT) for e in cnt}
        from contextlib import ExitStack
        with ExitStack() as st:
            sems = {}
            for e in cnt:
                for k in range(nrot[e]):
                    sems[(e, k)] = st.enter_context(nc.semaphore(f"s_{e}{k}"))
            for k in range(min(self.ND, max(ndma, 1))):
                sems[("dma", k)] = st.enter_context(nc.semaphore(f"s_d{k}"))
            sems[("cc", 0)] = st.enter_context(nc.semaphore("s_cc"))
            block = st.enter_context(nc.Block())
            engs = {"pe": "tensor", "act": "scalar", "dve": "vector", "pool": "gpsimd", "sp": "sync"}
            per = {e: [i for i, o in enumerate(ops) if o["eng"] == e] for e in engs}

            order = ("pe", "act", "dve", "pool", "sp")
            eidx = {e: k for k, e in enumerate(order)}
            know = [None] * n
            kE = {e: [-1] * 5 for e in order}
            seenA = {e: set() for e in order}
            plan = [None] * n
            for i, o in enumerate(ops):
                E = o["eng"]
                cur = kE[E]
                waits = []
                for d in sorted(o["deps"], reverse=True):
                    od = ops[d]
                    if od["kind"] in ("d", "cc"):
                        if d in seenA[E]:
                            continue
                        seenA[E].add(d)
                        waits.append(d)
                    else:
                        if cur[eidx[od["eng"]]] >= d:
                            continue
                        waits.append(d)
                    kd = know[d]
                    for k in range(5):
                        if kd[k] > cur[k]:
                            cur[k] = kd[k]
                plan[i] = waits
                mine = list(cur)
                if o["kind"] in ("c", "n"):
                    mine[eidx[E]] = i
                know[i] = mine

            def run(ename, eobj):
                for i in per[ename]:
                    o = ops[i]
                    waits = plan[i]
                    last_wait = None
                    if o["fn"] is not None and waits:
                        last_wait = waits[-1]
                        waits = waits[:-1]
                    for d in waits:
                        s = sig[d]
                        eobj.wait_ge(sems[(s[0], s[1])], s[2])
                    self.n_standalone_waits += len(waits)
                    if o["fn"] is None:
                        if i in sig:
                            s = sig[i]
                            eobj.nop().then_inc(sems[(s[0], s[1])], 1)
                        continue
                    ins = o["fn"](eobj)
                    if last_wait is not None:
                        s = sig[last_wait]
                        ins._wait_ge(sems[(s[0], s[1])], s[2])
                    if i in sig:
                        s = sig[i]
                        if o["kind"] == "d":
                            ins.then_inc(sems[(s[0], s[1])], 16)
                        else:
                            ins.then_inc(sems[(s[0], s[1])], 1)

            for ename, attr in engs.items():
                if not per[ename]:
                    continue
                deco = getattr(block, attr)

                def body(eobj, _en=ename):
                    run(_en, eobj)
                deco(body)


class Prog:
    def __init__(self, S, layers):
        self.S = S
        self.layers = layers
        self.NT = S // 128
        self.NG = S // 512
        self.nc = bass.Bass("TRN2", target_bir_lowering=False)
        self.rec = Rec()
        self.sb_off = 0
        self.arena = None
        self.ARENA = 206 * 1024
        self.ps_banks = []
        self.inputs = {}
        self._uid = 0

    def dram_in(self, name, shape, dt=F32):
        t = self.nc.dram_tensor(name, list(shape), dt, kind="ExternalInput")
        self.inputs[name] = t
        return t.ap()

    def dram(self, name, shape, dt):
        return self.nc.dram_tensor(name, list(shape), dt).ap()

    def sb(self, name, shape, dt):
        if self.arena is None:
            self.arena = self.nc.alloc_sbuf_tensor("arena", [128, self.ARENA], mybir.dt.uint8).ap()
        esz = 2 if dt == BF16 else 4
        nbytes = int(np.prod(shape[1:])) * esz
        v = self.arena[:, self.sb_off:self.sb_off + nbytes].bitcast(dt)
        if len(shape) == 3:
            v = v.rearrange("p (a b) -> p a b", b=shape[2])
        elif len(shape) == 4:
            v = v.rearrange("p (a b c) -> p a b c", b=shape[2], c=shape[3])
        self.sb_off += (nbytes + 31) // 32 * 32
        assert self.sb_off <= self.ARENA, (name, self.sb_off)
        return v

    def op(self, eng, fn, r=(), w=(), kind="c"):
        return self.rec.add(eng, fn, r, w, kind)

    def dma(self, q, out, in_, r=(), w=()):
        return self.rec.add(q, lambda e: e.dma_start(out=out, in_=in_), r, w, kind="d")


def build(S, layers, final_full=True):
    P = Prog(S, layers)
    nc = P.nc
    NT, NG = P.NT, P.NG
    n_even = sum(1 for l in layers if l == "e")
    n_odd = sum(1 for l in layers if l == "o")

    x_in = P.dram_in("x", [S, D_MODEL])
    y_out = nc.dram_tensor("y", [S, D_MODEL], F32, kind="ExternalOutput").ap()
    cmask = P.dram_in("cmask", [4, 128, 512], BF16)
    cmat = P.dram_in("cmat", [4, 128, 128], BF16)
    wio = [P.dram_in(f"wio{i}", [D_MODEL, 2048]) for i in range(n_odd)]
    lno = [P.dram_in(f"lno{i}", [128, KC]) for i in range(n_odd)]
    woo = [P.dram_in(f"woo{i}", [512, D_MODEL]) for i in range(n_odd)]
    wie = [P.dram_in(f"wie{i}", [D_MODEL, 1664]) for i in range(n_even)]
    lne = [P.dram_in(f"lne{i}", [128, KC]) for i in range(n_even)]
    woe = [P.dram_in(f"woe{i}", [512, D_MODEL]) for i in range(n_even)]
    if n_even:
        qkn = [P.dram_in(f"qkn{i}", [1, 320]) for i in range(n_even)]
        snk = [P.dram_in(f"snk{i}", [1, 4]) for i in range(n_even)]
        lbt = P.dram_in("lbt", [128, 4])
        gnt = [P.dram_in(f"gnt{i}", [128, 1]) for i in range(n_even)]
        rope = P.dram_in("rope", [S, 64])
        emask = P.dram_in("emask", [3, 128, 512], BF16)
        bmask = P.dram_in("bmask", [128, 8], BF16)

    xres = P.dram("xres", [S, D_MODEL], F32)
    ypart = P.dram("ypart", [S, D_MODEL], F32)
    qT_d = P.dram("qT_d", [4, 128, S], BF16)
    kT_d = P.dram("kT_d", [4, 128, S], BF16)
    sgT_d = P.dram("sgT_d", [4, 128, S], BF16)
    v_d = P.dram("v_d", [S, 512], BF16)
    ogT_d = P.dram("ogT_d", [4, 128, S], BF16)

    ident = P.sb("ident", [128, 128], BF16)
    trineg = P.sb("trineg", [128, 128], BF16)
    onesneg = P.sb("onesneg", [128, 128], BF16)
    ones = P.sb("ones", [128, 128], BF16)
    negmask = P.sb("negmask", [128, 4, 512], BF16)
    PERSIST = P.sb_off
    for i, t in enumerate((ident, trineg, onesneg, ones)):
        P.dma("sp", t, cmat[i], w=[("const", i)])
    P.dma("sp", negmask, cmask.rearrange("j p t -> p j t"), w=[("const", "negmask")])
    CONST_KEYS = [("const", i) for i in range(4)] + [("const", "negmask")]

    psum = nc.alloc_psum_tensor("psum", [128, 4096], F32).ap()
    banks = [psum[:, i * 512:(i + 1) * 512] for i in range(8)]
    tpb = [psum[:, i * 1024:(i + 1) * 1024].bitcast(BF16).rearrange("p (k t) -> p k t", t=128) for i in range(2)]
    tph = [psum[:, i * 512:(i + 1) * 512].bitcast(BF16).rearrange("p (k t) -> p k t", t=128) for i in range(2)]

    def pview(bank, off, n, dt=F32, inner=None):
        v = psum[:, bank * 512 + off: bank * 512 + off + n]
        if dt == BF16:
            v = v.bitcast(BF16)
        if inner is not None:
            v = v.rearrange("p (a b) -> p a b", b=inner)
        return v

    def sb_reset():
        P.sb_off = PERSIST

    def load_weights(w_dram, ncols, ln_dram, wbf, wst, lnT):
        P.dma("sp", lnT, ln_dram, w=["lnT"])
        for kc in range(KC):
            s = kc % 2
            P.dma("sp", wst[s][:, :ncols], w_dram[kc * 128:(kc + 1) * 128, :], w=[("wst", s)])
            eng = "pool" if kc % 2 else "dve"
            P.op(eng, lambda e, kc=kc, s=s: e.tensor_scalar(
                wbf[:, kc, :], wst[s][:, :ncols], lnT[:, kc:kc + 1], None, ALU.mult),
                r=[("wst", s), "lnT"], w=[("wbf", kc)])

    def load_wout(w_dram, wob, wst):
        for c in range(4):
            s = c % len(wst)
            P.dma("sp", wst[s][:, :D_MODEL], w_dram[c * 128:(c + 1) * 128, :], w=[("wst", s)])
            eng = "pool" if c % 2 else "dve"
            P.op(eng, lambda e, c=c, s=s: e.tensor_copy(wob[:, c, :], wst[s][:, :D_MODEL]),
                 r=[("wst", s)], w=[("wob", c)])

    def norm_transpose_group(G, x_src, xin, junk, ss, rstd, xn, hT, hs, tpb, even=False):
        gp = G % 2
        nx = len(xin)
        for t in range(4):
            tile_i = G * 4 + t
            xs_ = t % nx
            P.dma("sp", xin[xs_], x_src[tile_i * 128:(tile_i + 1) * 128, :], r=[("xres", G)], w=[("xin", xs_)])
            P.op("act", lambda e, t=t, gp=gp, xs_=xs_: e.activation(junk, xin[xs_], AF.Square, accum_out=ss[gp][:, t:t + 1]),
                 r=[("xin", xs_)], w=["junk", ("ss", gp)])
        if even:
            P.op("act", lambda e, gp=gp: e.activation(rstd[gp], ss[gp], AF.Ln, bias=EPS, scale=1.0 / D_MODEL),
                 r=[("ss", gp)], w=[("rstd", gp)])
            P.op("act", lambda e, gp=gp: e.activation(rstd[gp], rstd[gp], AF.Exp, scale=-0.5),
                 r=[("rstd", gp)], w=[("rstd", gp)])
        else:
            P.op("act", lambda e, gp=gp: e.activation(rstd[gp], ss[gp], AF.Sqrt, bias=EPS, scale=1.0 / D_MODEL),
                 r=[("ss", gp)], w=[("rstd", gp)])
            P.op("dve", lambda e, gp=gp: e.reciprocal(rstd[gp], rstd[gp]), r=[("rstd", gp)], w=[("rstd", gp)])
        nh = 2 if even else 1
        kh = KC // nh

        def tpk(sl):
            return ("bank", sl) if even else ("tp", sl)
        for t in range(4):
            s2 = t % 2
            xs_ = t % nx
            if nx < 4:
                tile_i = G * 4 + t
                P.dma("sp", xin[xs_], x_src[tile_i * 128:(tile_i + 1) * 128, :], r=[("xres", G)], w=[("xin", xs_)])
            P.op("dve", lambda e, s2=s2, t=t, gp=gp, xs_=xs_: e.tensor_scalar(xn[s2], xin[xs_], rstd[gp][:, t:t + 1], None, ALU.mult),
                 r=[("xin", xs_), ("rstd", gp)], w=[("xn", s2)])
            for hf in range(nh):
                sl = (t * nh + hf) % 2
                tp = tpb[sl]
                for k in range(kh):
                    kc = hf * kh + k
                    P.op("pe", lambda e, kc=kc, k=k, s2=s2, tp=tp: e.transpose(tp[:, k, :], xn[s2][:, kc * 128:(kc + 1) * 128], ident),
                         r=[("xn", s2), ("const", 0)], w=[tpk(sl)])
                dst = hT[hs][:, hf * kh:(hf + 1) * kh, t * 128:(t + 1) * 128]
                if (t * nh + hf) % 2 == 0:
                    P.op("act", lambda e, dst=dst, tp=tp: e.activation(dst, tp, AF.Copy),
                         r=[tpk(sl)], w=[("hT", hs)])
                else:
                    P.op("dve", lambda e, dst=dst, tp=tp: e.tensor_copy(dst, tp),
                         r=[tpk(sl)], w=[("hT", hs)])

    def out_phase(x_src, wob, last):
        og = [P.sb(f"og{i}", [128, 4, 512], BF16) for i in range(2)]
        xt = [P.sb(f"xt{i}", [128, D_MODEL], F32) for i in range(2)]
        ys = [P.sb(f"ys{i}", [128, D_MODEL], F32) for i in range(2)]
        for G in range(NG):
            gs = G % 2
            P.dma("sp", og[gs], ogT_d.rearrange("c p t -> p c t")[:, :, G * 512:(G + 1) * 512],
                  r=[("ogT_d", G)], w=[("og", gs)])
            for t in range(4):
                ti = G * 4 + t
                s2 = ti % 2
                P.dma("sp", xt[s2], x_src[ti * 128:(ti + 1) * 128, :], r=[("xres", G)], w=[("xt", s2)])
                for jg in range(4):
                    b = (ti * 4 + jg) % 4
                    for c in range(4):
                        P.op("pe", lambda e, b=b, c=c, jg=jg, t=t, gs=gs: e.matmul(
                            banks[b], og[gs][:, c, t * 128:(t + 1) * 128], wob[:, c, jg * 512:(jg + 1) * 512],
                            start=(c == 0), stop=(c == 3)),
                            r=[("og", gs), ("wob", c)], w=[("bank", b)])
                    P.op("dve", lambda e, b=b, jg=jg, s2=s2: e.scalar_tensor_tensor(
                        ys[s2][:, jg * 512:(jg + 1) * 512], xt[s2][:, jg * 512:(jg + 1) * 512], 0.25, banks[b],
                        ALU.mult, ALU.add),
                        r=[("bank", b), ("xt", s2)], w=[("ys", s2)])
                P.dma("sp", ypart[ti * 128:(ti + 1) * 128, :], ys[s2], r=[("ys", s2)], w=[("ypart", G)])
            rows = slice(G * 512, (G + 1) * 512)
            P.rec.add("pool", lambda e, rows=rows: e.collective_compute(
                "AllReduce", ALU.add, replica_groups=GROUPS, ins=[ypart[rows, :]], outs=[xres[rows, :]]),
                r=[("ypart", G)], w=[("xres", G)], kind="cc")
            if last:
                P.dma("sp", y_out[rows, :], xres[rows, :], r=[("xres", G)], w=[("yout", G)])

    def odd_layer(li, oi, x_src, last):
        rec = P.rec
        sb_reset()
        wbf = P.sb("wbf", [128, KC, 2048], BF16)
        wst = [P.sb(f"wst{i}", [128, 2048], F32) for i in range(2)]
        lnT = P.sb("lnT", [128, KC], F32)
        xin = [P.sb(f"xin{i}", [128, D_MODEL], F32) for i in range(4)]
        junk = P.sb("junk", [128, D_MODEL], BF16)
        ss = [P.sb(f"ss{i}", [128, 4], F32) for i in range(2)]
        rstd = [P.sb(f"rstd{i}", [128, 4], F32) for i in range(2)]
        xn = [P.sb(f"xn{i}", [128, D_MODEL], BF16) for i in range(2)]
        hT = [P.sb(f"hT{i}", [128, KC, 512], BF16) for i in range(2)]
        stq = [P.sb(f"stq{i}", [128, 4, 512], BF16) for i in range(2)]
        stk = [P.sb(f"stk{i}", [128, 4, 512], BF16) for i in range(2)]
        stg = [P.sb(f"stg{i}", [128, 4, 512], BF16) for i in range(2)]
        stv = [P.sb(f"stv{i}", [128, 4, 512], BF16) for i in range(2)]
        load_weights(wio[oi], 2048, lno[oi], wbf, wst, lnT)
        scale = 128 ** -0.5
        norm_transpose_group(0, x_src, xin, junk, ss, rstd, xn, hT, 0, tpb)
        for G in range(NG):
            hs = G % 2
            if G + 1 < NG:
                norm_transpose_group(G + 1, x_src, xin, junk, ss, rstd, xn, hT, (G + 1) % 2, tpb)
            n_acc = 0
            for typ in range(4):
                if typ == 2:
                    for t in range(4):
                        b = 4 + (n_acc % 4)
                        n_acc += 1
                        for kc in range(KC):
                            P.op("pe", lambda e, b=b, kc=kc, t=t, hs=hs: e.matmul(
                                banks[b], hT[hs][:, kc, t * 128:(t + 1) * 128], wbf[:, kc, 1024:1536],
                                start=(kc == 0), stop=(kc == KC - 1)),
                                r=[("hT", hs), ("wbf", kc)], w=[("bank", b)])
                        P.op("dve", lambda e, b=b, t=t, hs=hs: e.tensor_copy(stv[hs][:, t, :], banks[b]),
                             r=[("bank", b)], w=[("stv", hs)])
                    continue
                for h in range(4):
                    b = 4 + (n_acc % 4)
                    n_acc += 1
                    col = typ * 512 + h * 128
                    for kc in range(KC):
                        P.op("pe", lambda e, b=b, kc=kc, col=col, hs=hs: e.matmul(
                            banks[b], wbf[:, kc, col:col + 128], hT[hs][:, kc, :],
                            start=(kc == 0), stop=(kc == KC - 1)),
                            r=[("hT", hs), ("wbf", kc)], w=[("bank", b)])
                    if typ == 0:
                        P.op("act", lambda e, b=b, h=h, hs=hs: e.activation(stq[hs][:, h, :], banks[b], AF.Copy, scale=scale),
                             r=[("bank", b)], w=[("stq", hs)])
                    elif typ == 1:
                        P.op("dve", lambda e, b=b, h=h, hs=hs: e.tensor_copy(stk[hs][:, h, :], banks[b]),
                             r=[("bank", b)], w=[("stk", hs)])
                    else:
                        P.op("act", lambda e, b=b, h=h, hs=hs: e.activation(stg[hs][:, h, :], banks[b], AF.Silu),
                             r=[("bank", b)], w=[("stg", hs)])
            cols = slice(G * 512, (G + 1) * 512)
            P.dma("sp", qT_d.rearrange("h p t -> p h t")[:, :, cols], stq[hs], r=[("stq", hs)], w=[("qT_d", G)])
            P.dma("sp", kT_d.rearrange("h p t -> p h t")[:, :, cols], stk[hs], r=[("stk", hs)], w=[("kT_d", G)])
            P.dma("sp", sgT_d.rearrange("h p t -> p h t")[:, :, cols], stg[hs], r=[("stg", hs)], w=[("sgT_d", G)])
            P.dma("sp", v_d[G * 512:(G + 1) * 512, :].rearrange("(t p) c -> p t c", p=128), stv[hs],
                  r=[("stv", hs)], w=[("v_d", G)])
        rec.barrier()
        sb_reset()
        wst2 = [P.sb("wsto0", [128, 2048], F32)]
        wob = P.sb("wob", [128, 4, D_MODEL], BF16)
        load_wout(woo[oi], wob, wst2)
        kT = [P.sb(f"kT{i}", [128, S], BF16) for i in range(4)]
        vv = [P.sb(f"vv{i}", [128, NT, 128], BF16) for i in range(4)]
        NQ = 3
        qg = [P.sb(f"qg{i}", [128, 512], BF16) for i in range(NQ)]
        sg = [P.sb(f"sg{i}", [128, 512], BF16) for i in range(NQ)]
        e_sb = [P.sb(f"e{i}", [128, 512], F32) for i in range(2)]
        sp_sb = [P.sb(f"sp{i}", [128, 512], BF16) for i in range(3)]
        rcs = [P.sb(f"rcs{i}", [128, 512], F32) for i in range(3)]
        w_sb = [P.sb(f"w{i}", [128, 512], BF16) for i in range(2)]
        crep = [P.sb(f"crep{i}", [128, 512], F32) for i in range(2)]
        ogs = [P.sb(f"ogs{i}", [128, 4, 512], BF16) for i in range(2)]
        xt = [P.sb(f"xt{i}", [128, D_MODEL], F32) for i in range(2)]
        for h in range(4):
            P.dma("sp", kT[h], kT_d[h], w=[("kT", h)])
            P.dma("sp", vv[h], v_d.rearrange("(n p) (h d) -> p n h d", p=128, h=4)[:, :, h, :], w=[("vv", h)])
        units = []
        gi = 0
        for g in range(NG):
            for h in range(4):
                kbs = list(range(4 * g + 3, -1, -1))
                for n, kb in enumerate(kbs):
                    units.append(dict(h=h, g=g, kb=kb, first=(n == 0), last=(n == len(kbs) - 1),
                                      diag=(kb - 4 * g) if kb >= 4 * g else None, gi=gi))
                gi += 1
        NU = len(units)

        def load_group(h, g, gi):
            s = gi % NQ
            cols = slice(g * 512, (g + 1) * 512)
            P.dma("sp", qg[s], qT_d[h][:, cols], w=[("qg", s)])
            P.dma("sp", sg[s], sgT_d[h][:, cols], w=[("sg", s)])

        def out_group(g):
            os_ = g % 2
            for t in range(4):
                ti = g * 4 + t
                s2 = ti % 2
                P.dma("sp", xt[s2], x_src[ti * 128:(ti + 1) * 128, :], r=[("xres", g)], w=[("xt", s2)])
                for jg in range(4):
                    b = 4 + jg % 2
                    for c in range(4):
                        P.op("pe", lambda e, b=b, c=c, jg=jg, t=t, os_=os_: e.matmul(
                            banks[b], ogs[os_][:, c, t * 128:(t + 1) * 128], wob[:, c, jg * 512:(jg + 1) * 512],
                            start=(c == 0), stop=(c == 3)),
                            r=[("ogs", os_), ("wob", c)], w=[("bank", b)])
                    xs_ = xt[s2][:, jg * 512:(jg + 1) * 512]
                    P.op("dve", lambda e, b=b, xs_=xs_: e.scalar_tensor_tensor(xs_, xs_, 0.25, banks[b], ALU.mult, ALU.add),
                         r=[("bank", b), ("xt", s2)], w=[("xt", s2)])
                P.dma("sp", ypart[ti * 128:(ti + 1) * 128, :], xt[s2], r=[("xt", s2)], w=[("ypart", g)])
            rows = slice(g * 512, (g + 1) * 512)
            P.rec.add("pool", lambda e, rows=rows: e.collective_compute(
                "AllReduce", ALU.add, replica_groups=GROUPS, ins=[ypart[rows, :]], outs=[xres[rows, :]]),
                r=[("ypart", g)], w=[("xres", g)], kind="cc")
            if last:
                P.dma("sp", y_out[rows, :], xres[rows, :], r=[("xres", g)], w=[("yout", g)])

        load_group(0, 0, 0)
        for step in range(NU + 2):
            if step < NU:
                u = units[step]
                h, g, kb, gi_ = u["h"], u["g"], u["kb"], u["gi"]
                qs = gi_ % NQ
                if u["first"]:
                    if step + (4 * g + 4) < NU:
                        un = units[step + 4 * g + 4]
                        load_group(un["h"], un["g"], un["gi"])
                zb = step % 2
                kblk = kT[h][:, kb * 128:(kb + 1) * 128]
                dg = u["diag"]
                P.op("pe", lambda e, zb=zb, kblk=kblk, qs=qs, dg=dg: e.matmul(
                    banks[zb], kblk, qg[qs], start=True, stop=(dg is None)),
                    r=[("kT", h), ("qg", qs)], w=[("bank", zb)])
                if dg is not None:
                    P.op("pe", lambda e, zb=zb, dg=dg: e.matmul(banks[zb], ident, negmask[:, dg, :], start=False, stop=True),
                         r=CONST_KEYS, w=[("bank", zb)])
                es, ss_ = step % 2, step % 3
                P.op("act", lambda e, zb=zb, es=es: e.activation(e_sb[es], banks[zb], AF.Exp),
                     r=[("bank", zb)], w=[("e", es)])
                P.op("act", lambda e, es=es, ss_=ss_: e.activation(sp_sb[ss_], e_sb[es], AF.Ln, bias=1.0),
                     r=[("e", es)], w=[("sp", ss_)])
            if 0 <= step - 1 < NU:
                s1 = step - 1
                u = units[s1]
                h, g, kb, gi_ = u["h"], u["g"], u["kb"], u["gi"]
                qs = gi_ % NQ
                ss_ = s1 % 3
                rb, ab = 2 + s1 % 2, 4 + s1 % 2
                kblk = kT[h][:, kb * 128:(kb + 1) * 128]
                dg = u["diag"]
                P.op("pe", lambda e, rb=rb, ss_=ss_: e.matmul(banks[rb], trineg, sp_sb[ss_], start=True, stop=False),
                     r=[("sp", ss_), ("const", 1)], w=[("bank", rb)])
                P.op("pe", lambda e, rb=rb, kblk=kblk, qs=qs, dg=dg: e.matmul(
                    banks[rb], kblk, qg[qs], start=False, stop=(dg is None)),
                    r=[("kT", h), ("qg", qs)], w=[("bank", rb)])
                if dg is not None:
                    P.op("pe", lambda e, rb=rb, dg=dg: e.matmul(banks[rb], ident, negmask[:, dg, :], start=False, stop=True),
                         r=CONST_KEYS, w=[("bank", rb)])
                cs_old, cs_new = s1 % 2, (s1 + 1) % 2
                rs = s1 % 3
                if not u["last"]:
                    P.op("pe", lambda e, ab=ab, ss_=ss_: e.matmul(banks[ab], onesneg, sp_sb[ss_], start=True, stop=True),
                         r=[("sp", ss_), ("const", 2)], w=[("bank", ab)])
                if u["first"]:
                    P.op("dve", lambda e, rs=rs, rb=rb: e.tensor_copy(rcs[rs], banks[rb]),
                         r=[("bank", rb)], w=[("rcs", rs)])
                    if not u["last"]:
                        P.op("dve", lambda e, cs_new=cs_new, ab=ab: e.tensor_copy(crep[cs_new], banks[ab]),
                             r=[("bank", ab)], w=[("crep", cs_new)])
                else:
                    P.op("dve", lambda e, rs=rs, rb=rb, cs_old=cs_old: e.tensor_tensor(rcs[rs], banks[rb], crep[cs_old], ALU.add),
                         r=[("bank", rb), ("crep", cs_old)], w=[("rcs", rs)])
                    if not u["last"]:
                        P.op("dve", lambda e, cs_new=cs_new, cs_old=cs_old, ab=ab: e.tensor_tensor(
                            crep[cs_new], banks[ab], crep[cs_old], ALU.add),
                            r=[("bank", ab), ("crep", cs_old)], w=[("crep", cs_new)])
            if 0 <= step - 2 < NU:
                s2 = step - 2
                u = units[s2]
                h, g, kb, gi_ = u["h"], u["g"], u["kb"], u["gi"]
                qs = gi_ % NQ
                rs, ws = s2 % 3, s2 % 2
                ob = 6 + gi_ % 2
                P.op("act", lambda e, rs=rs, ws=ws: e.activation(w_sb[ws], rcs[rs], AF.Exp),
                     r=[("rcs", rs)], w=[("w", ws)])
                P.op("pe", lambda e, ob=ob, h=h, kb=kb, ws=ws, u=u: e.matmul(
                    banks[ob], vv[h][:, kb, :], w_sb[ws], start=u["first"], stop=u["last"]),
                    r=[("vv", h), ("w", ws)], w=[("bank", ob)])
                if u["last"]:
                    os_ = g % 2
                    P.op("dve", lambda e, ob=ob, qs=qs, os_=os_, h=h: e.tensor_tensor(ogs[os_][:, h, :], banks[ob], sg[qs], ALU.mult),
                         r=[("bank", ob), ("sg", qs)], w=[("ogs", os_)])
                    if h == 3:
                        out_group(g)
        rec.barrier()

    def even_layer(li, ei, x_src, last):
        rec = P.rec
        sb_reset()
        NCOL = 1664
        wbf = P.sb("wbf", [128, KC, NCOL], BF16)
        lnT = P.sb("lnT", [128, KC], F32)
        xin = [P.sb(f"xin{i}", [128, D_MODEL], F32) for i in range(2)]
        junk = P.sb("junk", [128, D_MODEL], BF16)
        ss = [P.sb(f"ss{i}", [128, 4], F32) for i in range(2)]
        rstd = [P.sb(f"rstd{i}", [128, 4], F32) for i in range(2)]
        xn = [P.sb(f"xn{i}", [128, D_MODEL], BF16) for i in range(2)]
        hT = [P.sb(f"hT{i}", [128, KC, 512], BF16) for i in range(2)]
        wqk = P.sb("wqk", [128, 320], F32)
        esink = P.sb("esink", [128, 4], F32)
        lbv = P.sb("lbv", [128, 4], F32)
        lb = P.sb("lb", [128, 2], F32)
        oml = P.sb("oml", [128, 2], F32)
        gn = P.sb("gn", [128, 1], F32)
        swam = P.sb("swam", [128, 2, 128], BF16)
        m16 = P.sb("m16", [128, 128], BF16)
        bmk = P.sb("bmk", [128, 8], BF16)
        onesf = P.sb("onesf", [128, 512], F32)
        ropeg = [P.sb(f"ropeg{i}", [128, 4, 64], F32) for i in range(2)]
        P.dma("sp", wqk, qkn[ei].partition_broadcast(128).rearrange("p a b -> p (a b)"), w=["wqk"])
        P.dma("sp", esink, snk[ei].partition_broadcast(128).rearrange("p a b -> p (a b)"), w=["esink"])
        P.dma("sp", lbv, lbt, w=["lbv"])
        P.dma("sp", gn, gnt[ei], w=["gn"])
        P.dma("sp", swam[:, 0, :], emask[0][:, 0:128], w=["swam"])
        P.dma("sp", swam[:, 1, :], emask[1][:, 0:128], w=["swam"])
        P.dma("sp", m16, emask[2][:, 0:128], w=["m16"])
        P.dma("sp", bmk, bmask, w=["bmk"])
        P.op("pool", lambda e: e.memset(onesf, 1.0), w=["onesf"])
        P.op("act", lambda e: e.activation(esink, esink, AF.Exp), r=["esink"], w=["esink"])
        if ei == 0:
            P.op("dve", lambda e: e.memset(lb, 0.0), w=["lb"])
            P.op("dve", lambda e: e.memset(oml, 1.0), w=["oml"])
        else:
            P.op("dve", lambda e: e.tensor_tensor(lb, lbv[:, 0:2], lbv[:, 2:4], ALU.subtract), r=["lbv"], w=["lb"])
            P.op("act", lambda e: e.activation(lb, lb, AF.Exp), r=["lb"], w=["lb"])
            P.op("dve", lambda e: e.tensor_scalar(lb, lb, 1.0, None, ALU.add), r=["lb"], w=["lb"])
            P.op("dve", lambda e: e.reciprocal(lb, lb), r=["lb"], w=["lb"])
            P.op("dve", lambda e: e.tensor_scalar(oml, lb, -1.0, 1.0, ALU.mult, ALU.add), r=["lb"], w=["oml"])
        mix_start = P.sb_off
        wst = [P.sb(f"wst{i}", [128, 2048], F32) for i in range(2)]
        load_weights(wie[ei], NCOL, lne[ei], wbf, wst, lnT)
        rec.barrier()
        P.sb_off = mix_start
        sga = [P.sb(f"sga{i}", [128, 256], F32) for i in range(4)]
        sq = P.sb("sq", [128, 320], F32)
        s5 = P.sb("s5", [128, 5], F32)
        qk = P.sb("qk", [128, 5, 64], F32)
        rt = [P.sb(f"rt{i}", [128, 5, 32], F32) for i in range(4)]
        qkr = P.sb("qkr", [128, 6, 64], BF16)
        qkT = [P.sb(f"qkT{i}", [128, 3, 128], BF16) for i in range(3)]
        vaug = [P.sb(f"vaug{i}", [128, 66], BF16) for i in range(3)]
        qz = [P.sb(f"qz{i}", [128, 4, 128], BF16) for i in range(3)]
        ex = [P.sb(f"ex{i}", [128, 2, 2, 128], F32) for i in range(2)]
        pm = [P.sb(f"pm{i}", [128, 2, 2, 128], BF16) for i in range(2)]
        den = P.sb("den", [128, 4], F32)
        ya1 = P.sb("ya1", [128, 4, 64], F32)
        ya = P.sb("ya", [128, 256], BF16)
        yaT = [P.sb("yaT0", [128, 2, 512], BF16)] * 2
        qs = [P.sb(f"qs{i}", [128, 512], F32) for i in range(2)]
        sgb = [P.sb(f"sgb{i}", [128, 512], F32) for i in range(2)]
        ef = [P.sb(f"ef{i}", [128, 512], F32) for i in range(2)]
        fbuf = P.sb("fbuf", [128, 512], F32)
        logf = P.sb("logf", [128, 512], F32)
        kk = P.sb("kk", [128, 512], F32)
        Gb = [[P.sb(f"Gb{h}", [128, 516], F32)] * 2 for h in range(2)]
        gcar = [P.sb(f"gcar{h}", [128, 1], F32) for h in range(2)]
        aa = P.sb("aa", [128, 512], F32)
        ea = P.sb("ea", [128, 512], F32)
        qt = [P.sb(f"qt{i}", [128, 512], BF16) for i in range(2)]
        kt = [P.sb(f"kt{i}", [128, 512], BF16) for i in range(2)]
        kh = [P.sb(f"kh{i}", [128, 512], BF16) for i in range(2)]
        dec = [P.sb(f"dec{i}", [128, 32], F32) for i in range(2)]
        vt = [P.sb(f"vt{i}", [128, 256], BF16) for i in range(4)]
        attT = [P.sb(f"attT{i}", [128, 128], BF16) for i in range(2)]
        kblk = [P.sb(f"kblk{i}", [128, 8, 128], BF16) for i in range(2)]
        Sf = [[P.sb(f"Sf{h}{i}", [128, 128], F32) for i in range(2)] for h in range(2)]
        NSB = 4
        Sb = [[P.sb(f"Sb{h}{i}", [128, 128], BF16) for i in range(NSB)] for h in range(2)]
        osq = P.sb("osq", [128, 128], BF16)
        rs = P.sb("rs", [128, 128], F32)
        otmp = P.sb("otmp", [128, 128], F32)
        ybT = [P.sb("ybT0", [128, 2, 512], BF16)] * 2
        for i in range(3):
            P.op("pool", lambda e, i=i: e.memset(vaug[i], 1.0), w=[("vaug", i)])
            P.op("pool", lambda e, i=i: e.memset(qz[i], 0.0), w=[("qkT", i)])
        for h in range(2):
            P.op("pool", lambda e, h=h: e.memset(Sf[h][0], 0.0), w=[("Sf", h, 0)])
            P.op("pool", lambda e, h=h: e.memset(Sb[h][0], 0.0), w=[("Sb", h, 0)])
            P.op("pool", lambda e, h=h: e.memset(gcar[h], 0.0), w=[("gcar", h)])
        ACC = [2, 3]
        sc_v = [pview(4, 0, 512).rearrange("p (a b c) -> p a b c", a=2, b=2), pview(5, 0, 512).rearrange("p (a b c) -> p a b c", a=2, b=2)]
        av_ps = pview(6, 0, 264, inner=66)
        qkT_ps = pview(6, 264, 192, BF16, inner=128)
        att_ps = pview(7, 0, 128)
        khT_ps = pview(7, 128, 64, BF16)
        o_ps = pview(7, 192, 128)
        yT_ps = pview(7, 320, 128, BF16, inner=128)
        ssq_ps = pview(7, 384, 128)
        sc_v = [pview(4, 0, 512).rearrange("p (a b c) -> p a b c", a=2, b=2)]
        U_ps = pview(5, 0, 512, inner=128)
        sstate = {"sb": [0, 0], "sf": [0, 0], "nacc": 0}

        def acc_bank():
            b = ACC[sstate["nacc"] % 2]
            sstate["nacc"] += 1
            return b

        def project_fm(col, hs):
            b = acc_bank()
            for kc in range(KC):
                P.op("pe", lambda e, b=b, kc=kc, col=col, hs=hs: e.matmul(
                    banks[b], wbf[:, kc, col:col + 128], hT[hs][:, kc, :], start=(kc == 0), stop=(kc == KC - 1)),
                    r=[("hT", hs), ("wbf", kc)], w=[("bank", b)])
            return b

        def project_tm(c0, c1, hs, t):
            b = acc_bank()
            for kc in range(KC):
                P.op("pe", lambda e, b=b, kc=kc, hs=hs, t=t: e.matmul(
                    banks[b][:, 0:c1 - c0], hT[hs][:, kc, t * 128:(t + 1) * 128], wbf[:, kc, c0:c1],
                    start=(kc == 0), stop=(kc == KC - 1)),
                    r=[("hT", hs), ("wbf", kc)], w=[("bank", b)])
            return b

        def swa_tile(G, t, hs):
            ti = G * 4 + t
            gp = G % 2
            cur, prv = ti % 3, (ti - 1) % 3
            b = project_tm(0, 384, hs, t)
            pa = banks[b]
            P.op("act", lambda e: e.activation(sq, pa[:, 0:320], AF.Square), r=[("bank", b)], w=["sq"])
            P.op("dve", lambda e: e.tensor_reduce(s5, sq.rearrange("p (a b) -> p a b", b=64), AX.X, ALU.add), r=["sq"], w=["s5"])
            P.op("act", lambda e: e.activation(s5, s5, AF.Ln, bias=EPS, scale=1.0 / 64), r=["s5"], w=["s5"])
            P.op("act", lambda e: e.activation(s5, s5, AF.Exp, scale=-0.5), r=["s5"], w=["s5"])
            P.op("dve", lambda e: e.tensor_tensor(qk, pa[:, 0:320].rearrange("p (a b) -> p a b", b=64),
                                                  s5.unsqueeze(2).to_broadcast([128, 5, 64]), ALU.mult),
                 r=[("bank", b), "s5"], w=["qk"])
            P.op("dve", lambda e, cur=cur: e.tensor_copy(vaug[cur][:, 0:64], pa[:, 320:384]), r=[("bank", b)], w=[("vaug", cur)])
            P.op("dve", lambda e: e.tensor_tensor(qk, qk, wqk.rearrange("p (a b) -> p a b", b=64), ALU.mult),
                 r=["qk", "wqk"], w=["qk"])
            cs = ropeg[gp][:, t, :]
            cosb = cs[:, 0:32].unsqueeze(1).to_broadcast([128, 5, 32])
            sinb = cs[:, 32:64].unsqueeze(1).to_broadcast([128, 5, 32])
            x1, x2 = qk[:, :, 0:32], qk[:, :, 32:64]
            P.op("dve", lambda e: e.tensor_tensor(rt[0], x1, cosb, ALU.mult), r=["qk", ("ropeg", gp)], w=[("rt", 0)])
            P.op("pool", lambda e: e.tensor_tensor(rt[1], x2, sinb, ALU.mult), r=["qk", ("ropeg", gp)], w=[("rt", 1)])
            P.op("dve", lambda e: e.tensor_tensor(rt[2], x2, cosb, ALU.mult), r=["qk", ("ropeg", gp)], w=[("rt", 2)])
            P.op("pool", lambda e: e.tensor_tensor(rt[3], x1, sinb, ALU.mult), r=["qk", ("ropeg", gp)], w=[("rt", 3)])
            P.op("dve", lambda e: e.tensor_tensor(qkr[:, 0:5, 0:32], rt[0], rt[1], ALU.subtract),
                 r=[("rt", 0), ("rt", 1)], w=["qkr"])
            P.op("dve", lambda e: e.tensor_tensor(qkr[:, 0:5, 32:64], rt[2], rt[3], ALU.add),
                 r=[("rt", 2), ("rt", 3)], w=["qkr"])
            P.op("dve", lambda e: e.tensor_copy(qkr[:, 5, :], qkr[:, 4, :]), r=["qkr"], w=["qkr"])
            for i in range(3):
                P.op("pe", lambda e, i=i: e.transpose(qkT_ps[:, i, :], qkr[:, 2 * i:2 * i + 2, :].rearrange("p a b -> p (a b)"), ident),
                     r=["qkr", ("const", 0)], w=[("bank", 6)])
            P.op("act", lambda e, cur=cur: e.activation(qkT[cur], qkT_ps, AF.Copy), r=[("bank", 6)], w=[("qkT", cur)])
            qz4 = qz[cur].rearrange("p (a b) t -> p a b t", b=2)
            P.op("dve", lambda e, qz4=qz4: e.tensor_copy(qz4[0:64, :, 0, :], qkT_ps[0:64, 0:2, :]), r=[("bank", 6)], w=[("qkT", cur)])
            P.op("dve", lambda e, qz4=qz4: e.tensor_copy(qz4[64:128, :, 1, :], qkT_ps[64:128, 0:2, :]), r=[("bank", 6)], w=[("qkT", cur)])
            import os
            lvl = int(os.environ.get("DBG_SWA", "9"))
            if lvl < 1:
                return
            has_prev = ti > 0
            nkb = 2 if has_prev else 1
            for hp in range(2):
                sc = sc_v[0]
                xs = hp
                for kb in range(nkb):
                    src = qkT[cur] if kb == 0 else qkT[prv]
                    P.op("pe", lambda e, sc=sc, kb=kb, src=src, hp=hp, cur=cur: e.matmul(
                        sc[:, kb, :, :], src[:, 2, :], qz[cur][:, 2 * hp:2 * hp + 2, :], start=True, stop=True),
                        r=[("qkT", cur), ("qkT", prv)], w=[("bank", 4)])
                if lvl < 2:
                    continue
                P.op("act", lambda e, sc=sc, xs=xs, nkb=nkb: e.activation(ex[xs][:, 0:nkb], sc[:, 0:nkb], AF.Exp, scale=0.125),
                     r=[("bank", 4)], w=[("ex", xs)])
                P.op("dve", lambda e, xs=xs, nkb=nkb: e.tensor_tensor(
                    pm[xs][:, 0:nkb], ex[xs][:, 0:nkb], swam[:, 0:nkb].unsqueeze(2).to_broadcast([128, nkb, 2, 128]), ALU.mult),
                    r=[("ex", xs), "swam"], w=[("pm", xs)])
                if lvl < 3:
                    continue
                for hi in range(2):
                    hh = 2 * hp + hi
                    for kb in range(nkb):
                        vsrc = vaug[cur] if kb == 0 else vaug[prv]
                        P.op("pe", lambda e, hh=hh, xs=xs, kb=kb, hi=hi, vsrc=vsrc, nkb=nkb: e.matmul(
                            av_ps[:, hh, 0:65], pm[xs][:, kb, hi, :], vsrc[:, 0:65], start=(kb == 0), stop=(kb == nkb - 1)),
                            r=[("pm", xs), ("vaug", cur), ("vaug", prv)], w=[("bank", 6)])
            if lvl < 4:
                return
            P.op("dve", lambda e: e.tensor_tensor(den, av_ps[:, :, 64], esink, ALU.add), r=[("bank", 6), "esink"], w=["den"])
            P.op("dve", lambda e: e.reciprocal(den, den), r=["den"], w=["den"])
            P.op("dve", lambda e: e.tensor_tensor(ya1, av_ps[:, :, 0:64], den.unsqueeze(2).to_broadcast([128, 4, 64]), ALU.mult),
                 r=[("bank", 6), "den"], w=["ya1"])
            P.op("pool", lambda e, t=t: e.tensor_tensor(ya, ya1.rearrange("p a b -> p (a b)"), sga[t], ALU.mult),
                 r=["ya1", ("sga", t)], w=["ya"])
            for i in range(2):
                P.op("pe", lambda e, i=i: e.transpose(yT_ps[:, i, :], ya[:, i * 128:(i + 1) * 128], ident),
                     r=["ya", ("const", 0)], w=[("bank", 7)])
            P.op("act", lambda e, gp=gp, t=t: e.activation(yaT[gp][:, :, t * 128:(t + 1) * 128], yT_ps, AF.Copy),
                 r=[("bank", 7)], w=[("yaT", 0)])

        def hgrn_group_prep(G, hh):
            gp = G % 2
            Gc, Gp = Gb[hh][gp], Gb[hh][1 - gp]
            P.op("dve", lambda e: e.tensor_scalar(fbuf, ef[hh], 1.0, None, ALU.add), r=[("ef", hh)], w=["fbuf"])
            P.op("dve", lambda e: e.reciprocal(fbuf, fbuf), r=["fbuf"], w=["fbuf"])
            P.op("dve", lambda e: e.tensor_scalar(fbuf, fbuf, oml[:, hh:hh + 1], lb[:, hh:hh + 1], ALU.mult, ALU.add),
                 r=["fbuf", "oml", "lb"], w=["fbuf"])
            P.op("act", lambda e: e.activation(logf, fbuf, AF.Ln), r=["fbuf"], w=["logf"])
            P.op("pool", lambda e: e.tensor_scalar(kk, fbuf, -1.0, 1.0, ALU.mult, ALU.add), r=["fbuf"], w=["kk"])
            P.op("dve", lambda e: e.tensor_copy(Gc[:, 0:1], gcar[hh]), r=[("gcar", hh)], w=[("Gb", hh)])
            P.op("dve", lambda e: e.tensor_tensor_scan(Gc[:, 1:513], onesf, logf, Gc[:, 0:1], ALU.mult, ALU.add),
                 r=["onesf", "logf", ("Gb", hh)], w=[("Gb", hh)])
            P.op("dve", lambda e: e.tensor_copy(gcar[hh], Gc[:, 512:513]), r=[("Gb", hh)], w=[("gcar", hh)])
            Gi = Gc[:, 1:513].rearrange("p (c i) -> p c i", i=16)
            Rc = Gc[:, 0:512].rearrange("p (c i) -> p c i", i=16)[:, :, 0:1]
            Ec = Gc[:, 1:513].rearrange("p (c i) -> p c i", i=16)[:, :, 15:16]
            a3 = aa.rearrange("p (c i) -> p c i", i=16)
            P.op("dve", lambda e: e.tensor_tensor(a3, Gi, Rc.to_broadcast([128, 32, 16]), ALU.subtract),
                 r=[("Gb", hh)], w=["aa"])
            P.op("act", lambda e: e.activation(ea, aa, AF.Exp), r=["aa"], w=["ea"])
            P.op("dve", lambda e: e.tensor_tensor(qt[hh], qs[hh], ea, ALU.mult), r=[("qs", hh), "ea"], w=[("qt", hh)])
            P.op("pool", lambda e: e.tensor_scalar(aa, aa, -1.0, 80.0, ALU.mult, ALU.min), r=["aa"], w=["aa"])
            P.op("act", lambda e: e.activation(ea, aa, AF.Exp), r=["aa"], w=["ea"])
            P.op("dve", lambda e: e.tensor_tensor(kt[hh], kk, ea, ALU.mult), r=["kk", "ea"], w=[("kt", hh)])
            P.op("dve", lambda e: e.tensor_tensor(a3, Ec.to_broadcast([128, 32, 16]), Gi, ALU.subtract),
                 r=[("Gb", hh)], w=["aa"])
            P.op("act", lambda e: e.activation(ea, aa, AF.Exp), r=["aa"], w=["ea"])
            P.op("dve", lambda e: e.tensor_tensor(kh[hh], kk, ea, ALU.mult), r=["kk", "ea"], w=[("kh", hh)])
            P.op("dve", lambda e: e.tensor_tensor(dec[hh], Ec.rearrange("p c i -> p (c i)"), Rc.rearrange("p c i -> p (c i)"), ALU.subtract),
                 r=[("Gb", hh)], w=[("dec", hh)])
            P.op("act", lambda e: e.activation(dec[hh], dec[hh], AF.Exp), r=[("dec", hh)], w=[("dec", hh)])

        def hgrn_tile(G, t, hh):
            gp = G % 2
            cols = slice(t * 128, (t + 1) * 128)
            vth = vt[t][:, hh * 128:(hh + 1) * 128]
            sa = (t * 2 + hh) % 2
            P.op("pe", lambda e: e.matmul(att_ps, kt[hh][:, cols], qt[hh][:, cols], start=True, stop=True),
                 r=[("kt", hh), ("qt", hh)], w=[("bank", 7)])
            P.op("dve", lambda e: e.tensor_tensor(attT[sa], att_ps, m16, ALU.mult), r=[("bank", 7), "m16"], w=[("attT", sa)])
            P.op("pe", lambda e: e.transpose(khT_ps, kh[hh][:, cols], ident), r=[("kh", hh), ("const", 0)], w=[("bank", 7)])
            P.op("dve", lambda e: e.tensor_tensor(
                kblk[sa], khT_ps.unsqueeze(1).to_broadcast([128, 8, 128]), bmk.unsqueeze(2).to_broadcast([128, 8, 128]), ALU.mult),
                r=[("bank", 7), "bmk"], w=[("kblk", sa)])
            P.op("pe", lambda e: e.matmul(o_ps, vth, attT[sa], start=True, stop=False),
                 r=[("vt", t), ("attT", sa)], w=[("bank", 7)])
            for c in range(8):
                if c % 4 == 0:
                    for c2 in range(c, c + 4):
                        P.op("pe", lambda e, c2=c2: e.matmul(U_ps[:, c2 % 4, :], kblk[sa][:, c2, :], vth, start=True, stop=True),
                             r=[("kblk", sa), ("vt", t)], w=[("bank", 5)])
                sbi = sstate["sb"][hh]
                sfi = sstate["sf"][hh]
                ccol = t * 128 + 16 * c
                P.op("pe", lambda e, sbi=sbi, ccol=ccol, c=c: e.matmul(
                    o_ps[:, 16 * c:16 * c + 16], Sb[hh][sbi], qt[hh][:, ccol:ccol + 16], start=False, stop=(c == 7)),
                    r=[("Sb", hh, sbi), ("qt", hh)], w=[("bank", 7)])
                nsf, nsb = 1 - sfi, (sbi + 1) % NSB
                dcol = dec[hh][:, 8 * t + c:8 * t + c + 1]
                P.op("dve", lambda e, sfi=sfi, nsf=nsf, dcol=dcol, c=c: e.scalar_tensor_tensor(
                    Sf[hh][nsf], Sf[hh][sfi], dcol, U_ps[:, c % 4, :], ALU.mult, ALU.add),
                    r=[("Sf", hh, sfi), ("dec", hh), ("bank", 5)], w=[("Sf", hh, nsf)])
                P.op("pool", lambda e, nsf=nsf, nsb=nsb: e.tensor_copy(Sb[hh][nsb], Sf[hh][nsf]),
                     r=[("Sf", hh, nsf)], w=[("Sb", hh, nsb)])
                sstate["sb"][hh] = nsb
                sstate["sf"][hh] = nsf
            P.op("act", lambda e: e.activation(osq, o_ps, AF.Square), r=[("bank", 7)], w=["osq"])
            P.op("pe", lambda e: e.matmul(ssq_ps, ones, osq, start=True, stop=True), r=["osq", ("const", 3)], w=[("bank", 7)])
            P.op("act", lambda e: e.activation(rs, ssq_ps, AF.Ln, bias=EPS, scale=1.0 / 128), r=[("bank", 7)], w=["rs"])
            P.op("act", lambda e: e.activation(rs, rs, AF.Exp, scale=-0.5), r=["rs"], w=["rs"])
            P.op("dve", lambda e: e.tensor_tensor(otmp, o_ps, rs, ALU.mult), r=[("bank", 7), "rs"], w=["otmp"])
            P.op("dve", lambda e: e.scalar_tensor_tensor(ybT[gp][:, hh, cols], otmp, gn[:, 0:1], sgb[hh][:, cols], ALU.mult, ALU.mult),
                 r=["otmp", "gn", ("sgb", hh)], w=[("ybT", 0)])

        def group_body(G):
            hs = G % 2
            gp = G % 2
            cols = slice(G * 512, (G + 1) * 512)
            P.dma("sp", ropeg[gp], rope[G * 512:(G + 1) * 512, :].rearrange("(t p) c -> p t c", p=128), w=[("ropeg", gp)])
            for hh in range(2):
                b = project_fm(640 + hh * 128, hs)
                P.op("act", lambda e, b=b, hh=hh: e.activation(qs[hh], banks[b], AF.Silu), r=[("bank", b)], w=[("qs", hh)])
            for hh in range(2):
                b = project_fm(1408 + hh * 128, hs)
                P.op("act", lambda e, b=b, hh=hh: e.activation(sgb[hh], banks[b], AF.Silu), r=[("bank", b)], w=[("sgb", hh)])
            for t in range(4):
                b = project_tm(384, 640, hs, t)
                P.op("act", lambda e, b=b, t=t: e.activation(sga[t], banks[b][:, 0:256], AF.Silu), r=[("bank", b)], w=[("sga", t)])
            for hh in range(2):
                b = project_fm(896 + hh * 128, hs)
                P.op("act", lambda e, b=b, hh=hh: e.activation(ef[hh], banks[b], AF.Exp, scale=-1.0), r=[("bank", b)], w=[("ef", hh)])
            for t in range(4):
                b = project_tm(1152, 1408, hs, t)
                P.op("dve", lambda e, b=b, t=t: e.tensor_copy(vt[t], banks[b][:, 0:256]), r=[("bank", b)], w=[("vt", t)])
            import os
            dbg = int(os.environ.get("DBG_EVEN", "7"))
            for hh in range(2):
                if dbg & 2:
                    hgrn_group_prep(G, hh)
            for t in range(4):
                if dbg & 1:
                    swa_tile(G, t, hs)
                for hh in range(2):
                    if dbg & 4:
                        hgrn_tile(G, t, hh)
            P.dma("sp", ogT_d[0:2].rearrange("c p t -> p c t")[:, :, cols], yaT[gp], r=[("yaT", 0)], w=[("ogT_d", G)])
            P.dma("sp", ogT_d[2:4].rearrange("c p t -> p c t")[:, :, cols], ybT[gp], r=[("ybT", 0)], w=[("ogT_d", G)])

        norm_transpose_group(0, x_src, xin, junk, ss, rstd, xn, hT, 0, tph, even=True)
        for G in range(NG):
            if G + 1 < NG:
                norm_transpose_group(G + 1, x_src, xin, junk, ss, rstd, xn, hT, (G + 1) % 2, tph, even=True)
            group_body(G)
        rec.barrier()
        sb_reset()
        wst2 = [P.sb(f"wste{i}", [128, 2048], F32) for i in range(2)]
        wob = P.sb("wob", [128, 4, D_MODEL], BF16)
        load_wout(woe[ei], wob, wst2)
        out_phase(x_src, wob, last)
        rec.barrier()

    cur = x_in
    no = ne = 0
    for li, typ in enumerate(layers):
        last = li == len(layers) - 1
        if typ == "o":
            odd_layer(li, no, cur, last)
            no += 1
        else:
            even_layer(li, ne, cur, last)
            ne += 1
        cur = xres
    P.rec.barrier()
    P.rec.emit(nc)
    return P


def _bf(a):
    return np.asarray(a, dtype=np.float32).astype(ml_dtypes.bfloat16)


def make_consts():
    j = np.arange(128)[:, None]
    s = np.arange(128)[None, :]
    ident = (j == s).astype(np.float32)
    trineg = -(j >= s).astype(np.float32)
    onesneg = -np.ones((128, 128), np.float32)
    ones = np.ones((128, 128), np.float32)
    cmat = _bf(np.stack([ident, trineg, onesneg, ones]))
    tri = np.where(j >= s, NEG, 0.0).astype(np.float32)
    cm = np.zeros((4, 128, 512), np.float32)
    for d in range(4):
        for i in range(4):
            blk = cm[d][:, i * 128:(i + 1) * 128]
            if i < d:
                blk[:] = NEG
            elif i == d:
                blk[:] = tri
    return cmat, _bf(cm)


def make_in_maps(inputs, S, layers):
    x = np.asarray(inputs["x"], np.float32)
    cmat, cmask = make_consts()
    half = 32
    inv_freq = (1.0 / (np.float32(10000.0) ** (np.arange(half, dtype=np.float32) / np.float32(half)))).astype(np.float32)
    ang = (np.arange(S, dtype=np.float32)[:, None] * inv_freq[None, :]).astype(np.float32)
    rope_tab = np.ascontiguousarray(np.concatenate([np.cos(ang), np.sin(ang)], axis=1).astype(np.float32))
    kk_ = np.arange(128)[:, None]
    qq_ = np.arange(128)[None, :]
    em = np.zeros((3, 128, 512), np.float32)
    em[0][:, :128] = (kk_ <= qq_)
    em[1][:, :128] = (kk_ > qq_)
    em[2][:, :128] = ((kk_ // 16) == (qq_ // 16)) & (kk_ <= qq_)
    emask = _bf(em)
    bmask = _bf((np.arange(128)[:, None] // 16) == np.arange(8)[None, :])
    maps = []
    for c in range(8):
        b, r = c // 4, c % 4
        m = {"x": np.ascontiguousarray(x[b, :S]), "cmat": cmat, "cmask": cmask}
        no = ne = 0
        for typ in layers:
            if typ == "o":
                w = np.asarray(inputs["w_in_odd"][no], np.float32)
                cols = []
                for part in range(4):
                    cols.append(w[:, part * 2048 + r * 512: part * 2048 + (r + 1) * 512])
                m[f"wio{no}"] = np.ascontiguousarray(np.concatenate(cols, axis=1))
                m[f"lno{no}"] = np.ascontiguousarray(np.asarray(inputs["ln_odd"][no], np.float32).reshape(KC, 128).T)
                m[f"woo{no}"] = np.ascontiguousarray(np.asarray(inputs["w_out_odd"][no], np.float32)[r * 512:(r + 1) * 512])
                no += 1
            else:
                w = np.asarray(inputs["w_in_even"][ne], np.float32)
                kv = r // 2
                cols = [w[:, r * 256:(r + 1) * 256],
                        w[:, 1024 + kv * 64:1024 + (kv + 1) * 64],
                        w[:, 1152 + kv * 64:1152 + (kv + 1) * 64],
                        w[:, 1280 + r * 256:1280 + (r + 1) * 256],
                        w[:, 2304 + r * 256:2304 + (r + 1) * 256],
                        w[:, 3328 + r * 256:3328 + (r + 1) * 256],
                        w[:, 4352 + r * 256:4352 + (r + 1) * 256],
                        w[:, 5376 + r * 256:5376 + (r + 1) * 256]]
                m[f"wie{ne}"] = np.ascontiguousarray(np.concatenate(cols, axis=1))
                m[f"lne{ne}"] = np.ascontiguousarray(np.asarray(inputs["ln_even"][ne], np.float32).reshape(KC, 128).T)
                wo = np.asarray(inputs["w_out_even"][ne], np.float32)
                m[f"woe{ne}"] = np.ascontiguousarray(np.concatenate([wo[r * 256:(r + 1) * 256], wo[1024 + r * 256:1024 + (r + 1) * 256]], axis=0))
                qn = np.asarray(inputs["q_norm_a"][ne], np.float32)
                kn = np.asarray(inputs["k_norm_a"][ne], np.float32)
                m[f"qkn{ne}"] = np.ascontiguousarray(np.concatenate([qn, qn, qn, qn, kn])[None, :])
                m[f"snk{ne}"] = np.ascontiguousarray(np.asarray(inputs["sinks_a"][ne], np.float32)[4 * r:4 * r + 4][None, :])
                m[f"gnt{ne}"] = np.ascontiguousarray(np.asarray(inputs["g_norm_b"][ne], np.float32)[:, None])
                ne += 1
        if "e" in layers:
            lbw = np.asarray(inputs["lower_bounds"], np.float32)
            m["lbt"] = np.ascontiguousarray(np.stack(
                [lbw[l, (2 * r + hh) * 128:(2 * r + hh + 1) * 128] for l in range(2) for hh in range(2)], axis=1))
            m["rope"] = rope_tab
            m["emask"] = emask
            m["bmask"] = bmask
        maps.append(m)
    return maps


_CACHE = {}


def run_layers(inputs, S, layers):
    key = (S, tuple(layers))
    if key not in _CACHE:
        _CACHE[key] = build(S, layers)
    P = _CACHE[key]
    maps = make_in_maps(inputs, S, layers)
    res = run_bass_kernel_spmd(P.nc, maps, core_ids=list(range(8)))
    out = np.stack([res.results[0]["y"], res.results[4]["y"]])
    return out


def kernel(**inputs):
    return run_layers(inputs, SEQ, ["e", "o", "e", "o"]).astype(np.float32)
```

```python
import numpy as np
import ml_dtypes
import concourse.bass as bass
import concourse.mybir as mybir
from concourse.bass_utils import run_bass_kernel_spmd

F32 = mybir.dt.float32
BF16 = mybir.dt.bfloat16
AF = mybir.ActivationFunctionType
ALU = mybir.AluOpType
AX = mybir.AxisListType

D_MODEL = 2048
SEQ = 8192
EPS = 1e-6
NEG = -30000.0
TP = 4
GROUPS = [[0, 1, 2, 3], [4, 5, 6, 7]]
KC = D_MODEL // 128


class Rec:
    COMPUTE = ("pe", "act", "dve", "pool")
    ND = 40
    ROT = 20000

    def __init__(self):
        self.ops = []
        self.lastw = {}
        self.readers = {}
        self.last_on = {}
        self.n_standalone_waits = 0

    def add(self, eng, fn, r=(), w=(), kind="c", extra=()):
        idx = len(self.ops)
        raw = set()
        oth = set()
        for k in r:
            if k in self.lastw:
                raw.add(self.lastw[k])
        for k in w:
            if k in self.lastw:
                oth.add(self.lastw[k])
            for j in self.readers.get(k, ()):
                oth.add(j)
        for j in extra:
            raw.add(j)
        self.ops.append(dict(eng=eng, fn=fn, raw=raw, oth=oth - raw, kind=kind))
        for k in r:
            self.readers.setdefault(k, []).append(idx)
        for k in w:
            self.lastw[k] = idx
            self.readers[k] = []
        self.last_on[eng] = idx
        return idx

    def barrier(self):
        pend = [i for i, o in enumerate(self.ops) if o["kind"] in ("d", "cc") and not o.get("barred")]
        lasts = [v for v in self.last_on.values()]
        for i in pend:
            self.ops[i]["barred"] = True
        deps = set(pend) | set(lasts)
        for eng in ("pe", "act", "dve", "pool", "sp"):
            self.add(eng, None, kind="n", extra=deps)
        self.lastw = {}
        self.readers = {}

    def emit(self, nc):
        ops = self.ops
        n = len(ops)
        for i, o in enumerate(ops):
            keep = set()
            for d in o["raw"] | o["oth"]:
                od = ops[d]
                if od["kind"] == "n":
                    pass
                same = od["eng"] == o["eng"] and od["kind"] in ("c", "n") and o["kind"] in ("c", "n")
                if same:
                    if o["eng"] == "pe":
                        continue
                keep.add(d)
            o["deps"] = keep
        needed = [False] * n
        for o in ops:
            for d in o["deps"]:
                needed[d] = True
        sig = {}
        cnt = {e: 0 for e in ("pe", "act", "dve", "pool", "sp")}
        ndma = 0
        ncc = 0
        dma_prev = {}
        for i, o in enumerate(ops):
            if o["kind"] == "d":
                slot = ndma % self.ND
                val = 16 * (ndma // self.ND + 1)
                sig[i] = ("dma", slot, val)
                if ndma >= self.ND:
                    o["deps"].add(dma_prev[slot])
                    needed[dma_prev[slot]] = True
                dma_prev[slot] = i
                ndma += 1
            elif o["kind"] == "cc":
                ncc += 1
                sig[i] = ("cc", 0, ncc)
            elif needed[i]:
                e = o["eng"]
                cnt[e] += 1
                sig[i] = (e, (cnt[e] - 1) // self.ROT, (cnt[e] - 1) % self.ROT + 1)
        nrot = {e: max(1, (cnt[e] + self.ROT - 1) // self.ROT) for e in cnt}
        from contextlib import ExitStack
        with ExitStack() as st:
            sems = {}
            for e in cnt:
                for k in range(nrot[e]):
                    sems[(e, k)] = st.enter_context(nc.semaphore(f"s_{e}{k}"))
            for k in range(min(self.ND, max(ndma, 1))):
                sems[("dma", k)] = st.enter_context(nc.semaphore(f"s_d{k}"))
            sems[("cc", 0)] = st.enter_context(nc.semaphore("s_cc"))
            block = st.enter_context(nc.Block())
            engs = {"pe": "tensor", "act": "scalar", "dve": "vector", "pool": "gpsimd", "sp": "sync"}
            per = {e: [i for i, o in enumerate(ops) if o["eng"] == e] for e in engs}

            order = ("pe", "act", "dve", "pool", "sp")
            eidx = {e: k for k, e in enumerate(order)}
            know = [None] * n
            kE = {e: [-1] * 5 for e in order}
            seenA = {e: set() for e in order}
            plan = [None] * n
            for i, o in enumerate(ops):
                E = o["eng"]
                cur = kE[E]
                waits = []
                for d in sorted(o["deps"], reverse=True):
                    od = ops[d]
                    if od["kind"] in ("d", "cc"):
                        if d in seenA[E]:
                            continue
                        seenA[E].add(d)
                        waits.append(d)
                    else:
                        if cur[eidx[od["eng"]]] >= d:
                            continue
                        waits.append(d)
                    kd = know[d]
                    for k in range(5):
                        if kd[k] > cur[k]:
                            cur[k] = kd[k]
                plan[i] = waits
                mine = list(cur)
                if o["kind"] in ("c", "n"):
                    mine[eidx[E]] = i
                know[i] = mine

            def run(ename, eobj):
                for i in per[ename]:
                    o = ops[i]
                    waits = plan[i]
                    last_wait = None
                    if o["fn"] is not None and waits:
                        last_wait = waits[-1]
                        waits = waits[:-1]
                    for d in waits:
                        s = sig[d]
                        eobj.wait_ge(sems[(s[0], s[1])], s[2])
                    self.n_standalone_waits += len(waits)
                    if o["fn"] is None:
                        if i in sig:
                            s = sig[i]
                            eobj.nop().then_inc(sems[(s[0], s[1])], 1)
                        continue
                    ins = o["fn"](eobj)
                    if last_wait is not None:
                        s = sig[last_wait]
                        ins._wait_ge(sems[(s[0], s[1])], s[2])
                    if i in sig:
                        s = sig[i]
                        if o["kind"] == "d":
                            ins.then_inc(sems[(s[0], s[1])], 16)
                        else:
                            ins.then_inc(sems[(s[0], s[1])], 1)

            for ename, attr in engs.items():
                if not per[ename]:
                    continue
                deco = getattr(block, attr)

                def body(eobj, _en=ename):
                    run(_en, eobj)
                deco(body)


class Prog:
    def __init__(self, S, layers):
        self.S = S
        self.layers = layers
        self.NT = S // 128
        self.NG = S // 512
        self.nc = bass.Bass("TRN2", target_bir_lowering=False)
        self.rec = Rec()
        self.sb_off = 0
        self.arena = None
        self.ARENA = 206 * 1024
        self.ps_banks = []
        self.inputs = {}
        self._uid = 0

    def dram_in(self, name, shape, dt=F32):
        t = self.nc.dram_tensor(name, list(shape), dt, kind="ExternalInput")
        self.inputs[name] = t
        return t.ap()

    def dram(self, name, shape, dt):
        return self.nc.dram_tensor(name, list(shape), dt).ap()

    def sb(self, name, shape, dt):
        if self.arena is None:
            self.arena = self.nc.alloc_sbuf_tensor("arena", [128, self.ARENA], mybir.dt.uint8).ap()
        esz = 2 if dt == BF16 else 4
        nbytes = int(np.prod(shape[1:])) * esz
        v = self.arena[:, self.sb_off:self.sb_off + nbytes].bitcast(dt)
        if len(shape) == 3:
            v = v.rearrange("p (a b) -> p a b", b=shape[2])
        elif len(shape) == 4:
            v = v.rearrange("p (a b c) -> p a b c", b=shape[2], c=shape[3])
        self.sb_off += (nbytes + 31) // 32 * 32
        assert self.sb_off <= self.ARENA, (name, self.sb_off)
        return v

    cap = None

    def op(self, eng, fn, r=(), w=(), kind="c"):
        if self.cap is not None:
            self.cap.append((eng, fn, tuple(r), tuple(w), kind))
            return None
        return self.rec.add(eng, fn, r, w, kind)

    def dma(self, q, out, in_, r=(), w=()):
        return self.op(q, lambda e: e.dma_start(out=out, in_=in_), r, w, kind="d")

    def merge(self, a, b):
        na, nb = len(a), len(b)
        i = j = 0
        while i < na or j < nb:
            if j >= nb or (i < na and i * nb <= j * na):
                o = a[i]
                i += 1
            else:
                o = b[j]
                j += 1
            self.rec.add(o[0], o[1], o[2], o[3], o[4])


def build(S, layers, final_full=True):
    P = Prog(S, layers)
    nc = P.nc
    NT, NG = P.NT, P.NG
    n_even = sum(1 for l in layers if l == "e")
    n_odd = sum(1 for l in layers if l == "o")

    x_in = P.dram_in("x", [S, D_MODEL])
    y_out = nc.dram_tensor("y", [S, D_MODEL], F32, kind="ExternalOutput").ap()
    cmask = P.dram_in("cmask", [4, 128, 512], BF16)
    cmat = P.dram_in("cmat", [4, 128, 128], BF16)
    wio = [P.dram_in(f"wio{i}", [D_MODEL, 2048]) for i in range(n_odd)]
    lno = [P.dram_in(f"lno{i}", [128, KC]) for i in range(n_odd)]
    woo = [P.dram_in(f"woo{i}", [512, D_MODEL]) for i in range(n_odd)]
    wie = [P.dram_in(f"wie{i}", [D_MODEL, 1664]) for i in range(n_even)]
    lne = [P.dram_in(f"lne{i}", [128, KC]) for i in range(n_even)]
    woe = [P.dram_in(f"woe{i}", [512, D_MODEL]) for i in range(n_even)]
    if n_even:
        qkn = [P.dram_in(f"qkn{i}", [1, 320]) for i in range(n_even)]
        snk = [P.dram_in(f"snk{i}", [1, 4]) for i in range(n_even)]
        lbt = P.dram_in("lbt", [128, 4])
        gnt = [P.dram_in(f"gnt{i}", [128, 1]) for i in range(n_even)]
        rope = P.dram_in("rope", [S, 64])
        emask = P.dram_in("emask", [3, 128, 512], BF16)
        bmask = P.dram_in("bmask", [128, 8], BF16)

    xres = P.dram("xres", [S, D_MODEL], F32)
    ypart = P.dram("ypart", [S, D_MODEL], F32)
    qT_d = P.dram("qT_d", [4, 128, S], BF16)
    kT_d = P.dram("kT_d", [4, 128, S], BF16)
    sgT_d = P.dram("sgT_d", [4, 128, S], BF16)
    v_d = P.dram("v_d", [S, 512], BF16)
    ogT_d = P.dram("ogT_d", [4, 128, S], BF16)

    ident = P.sb("ident", [128, 128], BF16)
    trineg = P.sb("trineg", [128, 128], BF16)
    onesneg = P.sb("onesneg", [128, 128], BF16)
    ones = P.sb("ones", [128, 128], BF16)
    negmask = P.sb("negmask", [128, 4, 512], BF16)
    PERSIST = P.sb_off
    for i, t in enumerate((ident, trineg, onesneg, ones)):
        P.dma("sp", t, cmat[i], w=[("const", i)])
    P.dma("sp", negmask, cmask.rearrange("j p t -> p j t"), w=[("const", "negmask")])
    CONST_KEYS = [("const", i) for i in range(4)] + [("const", "negmask")]

    psum = nc.alloc_psum_tensor("psum", [128, 4096], F32).ap()
    banks = [psum[:, i * 512:(i + 1) * 512] for i in range(8)]
    tpb = [psum[:, i * 1024:(i + 1) * 1024].bitcast(BF16).rearrange("p (k t) -> p k t", t=128) for i in range(2)]
    tph = [psum[:, i * 512:(i + 1) * 512].bitcast(BF16).rearrange("p (k t) -> p k t", t=128) for i in range(2)]

    def pview(bank, off, n, dt=F32, inner=None):
        v = psum[:, bank * 512 + off: bank * 512 + off + n]
        if dt == BF16:
            v = v.bitcast(BF16)
        if inner is not None:
            v = v.rearrange("p (a b) -> p a b", b=inner)
        return v

    def sb_reset():
        P.sb_off = PERSIST

    def load_weights(w_dram, ncols, ln_dram, wbf, wst, lnT):
        P.dma("sp", lnT, ln_dram, w=["lnT"])
        for kc in range(KC):
            s = kc % 2
            P.dma("sp", wst[s][:, :ncols], w_dram[kc * 128:(kc + 1) * 128, :], w=[("wst", s)])
            eng = "pool" if kc % 2 else "dve"
            P.op(eng, lambda e, kc=kc, s=s: e.tensor_scalar(
                wbf[:, kc, :], wst[s][:, :ncols], lnT[:, kc:kc + 1], None, ALU.mult),
                r=[("wst", s), "lnT"], w=[("wbf", kc)])

    def load_wout(w_dram, wob, wst):
        for c in range(4):
            s = c % len(wst)
            P.dma("sp", wst[s][:, :D_MODEL], w_dram[c * 128:(c + 1) * 128, :], w=[("wst", s)])
            eng = "pool" if c % 2 else "dve"
            P.op(eng, lambda e, c=c, s=s: e.tensor_copy(wob[:, c, :], wst[s][:, :D_MODEL]),
                 r=[("wst", s)], w=[("wob", c)])

    def norm_transpose_group(G, x_src, xin, junk, ss, rstd, xn, hT, hs, tpb, even=False):
        gp = G % 2
        nx = len(xin)
        for t in range(4):
            tile_i = G * 4 + t
            xs_ = t % nx
            P.dma("sp", xin[xs_], x_src[tile_i * 128:(tile_i + 1) * 128, :], r=[("xres", G)], w=[("xin", xs_)])
            P.op("act", lambda e, t=t, gp=gp, xs_=xs_: e.activation(junk, xin[xs_], AF.Square, accum_out=ss[gp][:, t:t + 1]),
                 r=[("xin", xs_)], w=["junk", ("ss", gp)])
        if even:
            P.op("act", lambda e, gp=gp: e.activation(rstd[gp], ss[gp], AF.Ln, bias=EPS, scale=1.0 / D_MODEL),
                 r=[("ss", gp)], w=[("rstd", gp)])
            P.op("act", lambda e, gp=gp: e.activation(rstd[gp], rstd[gp], AF.Exp, scale=-0.5),
                 r=[("rstd", gp)], w=[("rstd", gp)])
        else:
            P.op("act", lambda e, gp=gp: e.activation(rstd[gp], ss[gp], AF.Sqrt, bias=EPS, scale=1.0 / D_MODEL),
                 r=[("ss", gp)], w=[("rstd", gp)])
            P.op("dve", lambda e, gp=gp: e.reciprocal(rstd[gp], rstd[gp]), r=[("rstd", gp)], w=[("rstd", gp)])
        nh = 2 if even else 1
        kh = KC // nh

        def tpk(sl):
            return ("bank", sl) if even else ("tp", sl)
        for t in range(4):
            s2 = t % 2
            xs_ = t % nx
            if nx < 4:
                tile_i = G * 4 + t
                P.dma("sp", xin[xs_], x_src[tile_i * 128:(tile_i + 1) * 128, :], r=[("xres", G)], w=[("xin", xs_)])
            P.op("dve", lambda e, s2=s2, t=t, gp=gp, xs_=xs_: e.tensor_scalar(xn[s2], xin[xs_], rstd[gp][:, t:t + 1], None, ALU.mult),
                 r=[("xin", xs_), ("rstd", gp)], w=[("xn", s2)])
            for hf in range(nh):
                sl = (t * nh + hf) % 2
                tp = tpb[sl]
                for k in range(kh):
                    kc = hf * kh + k
                    P.op("pe", lambda e, kc=kc, k=k, s2=s2, tp=tp: e.transpose(tp[:, k, :], xn[s2][:, kc * 128:(kc + 1) * 128], ident),
                         r=[("xn", s2), ("const", 0)], w=[tpk(sl)])
                dst = hT[hs][:, hf * kh:(hf + 1) * kh, t * 128:(t + 1) * 128]
                if (t * nh + hf) % 2 == 0:
                    P.op("act", lambda e, dst=dst, tp=tp: e.activation(dst, tp, AF.Copy),
                         r=[tpk(sl)], w=[("hT", hs)])
                else:
                    P.op("dve", lambda e, dst=dst, tp=tp: e.tensor_copy(dst, tp),
                         r=[tpk(sl)], w=[("hT", hs)])

    def out_phase(x_src, wob, last):
        og = [P.sb(f"og{i}", [128, 4, 512], BF16) for i in range(2)]
        xt = [P.sb(f"xt{i}", [128, D_MODEL], F32) for i in range(2)]
        ys = [P.sb(f"ys{i}", [128, D_MODEL], F32) for i in range(2)]
        for G in range(NG):
            gs = G % 2
            P.dma("sp", og[gs], ogT_d.rearrange("c p t -> p c t")[:, :, G * 512:(G + 1) * 512],
                  r=[("ogT_d", G)], w=[("og", gs)])
            for t in range(4):
                ti = G * 4 + t
                s2 = ti % 2
                P.dma("sp", xt[s2], x_src[ti * 128:(ti + 1) * 128, :], r=[("xres", G)], w=[("xt", s2)])
                for jg in range(4):
                    b = (ti * 4 + jg) % 4
                    for c in range(4):
                        P.op("pe", lambda e, b=b, c=c, jg=jg, t=t, gs=gs: e.matmul(
                            banks[b], og[gs][:, c, t * 128:(t + 1) * 128], wob[:, c, jg * 512:(jg + 1) * 512],
                            start=(c == 0), stop=(c == 3)),
                            r=[("og", gs), ("wob", c)], w=[("bank", b)])
                    P.op("dve", lambda e, b=b, jg=jg, s2=s2: e.scalar_tensor_tensor(
                        ys[s2][:, jg * 512:(jg + 1) * 512], xt[s2][:, jg * 512:(jg + 1) * 512], 0.25, banks[b],
                        ALU.mult, ALU.add),
                        r=[("bank", b), ("xt", s2)], w=[("ys", s2)])
                P.dma("sp", ypart[ti * 128:(ti + 1) * 128, :], ys[s2], r=[("ys", s2)], w=[("ypart", G)])
            rows = slice(G * 512, (G + 1) * 512)
            P.rec.add("pool", lambda e, rows=rows: e.collective_compute(
                "AllReduce", ALU.add, replica_groups=GROUPS, ins=[ypart[rows, :]], outs=[xres[rows, :]]),
                r=[("ypart", G)], w=[("xres", G)], kind="cc")
            if last:
                P.dma("sp", y_out[rows, :], xres[rows, :], r=[("xres", G)], w=[("yout", G)])

    def odd_layer(li, oi, x_src, last):
        rec = P.rec
        sb_reset()
        wbf = P.sb("wbf", [128, KC, 2048], BF16)
        wst = [P.sb(f"wst{i}", [128, 2048], F32) for i in range(2)]
        lnT = P.sb("lnT", [128, KC], F32)
        xin = [P.sb(f"xin{i}", [128, D_MODEL], F32) for i in range(4)]
        junk = P.sb("junk", [128, D_MODEL], BF16)
        ss = [P.sb(f"ss{i}", [128, 4], F32) for i in range(2)]
        rstd = [P.sb(f"rstd{i}", [128, 4], F32) for i in range(2)]
        xn = [P.sb(f"xn{i}", [128, D_MODEL], BF16) for i in range(2)]
        hT = [P.sb(f"hT{i}", [128, KC, 512], BF16) for i in range(2)]
        stq = [P.sb(f"stq{i}", [128, 4, 512], BF16) for i in range(2)]
        stk = [P.sb(f"stk{i}", [128, 4, 512], BF16) for i in range(2)]
        stg = [P.sb(f"stg{i}", [128, 4, 512], BF16) for i in range(2)]
        stv = [P.sb(f"stv{i}", [128, 4, 512], BF16) for i in range(2)]
        load_weights(wio[oi], 2048, lno[oi], wbf, wst, lnT)
        scale = 128 ** -0.5
        norm_transpose_group(0, x_src, xin, junk, ss, rstd, xn, hT, 0, tpb)
        for G in range(NG):
            hs = G % 2
            if G + 1 < NG:
                norm_transpose_group(G + 1, x_src, xin, junk, ss, rstd, xn, hT, (G + 1) % 2, tpb)
            n_acc = 0
            for typ in range(4):
                if typ == 2:
                    for t in range(4):
                        b = 4 + (n_acc % 4)
                        n_acc += 1
                        for kc in range(KC):
                            P.op("pe", lambda e, b=b, kc=kc, t=t, hs=hs: e.matmul(
                                banks[b], hT[hs][:, kc, t * 128:(t + 1) * 128], wbf[:, kc, 1024:1536],
                                start=(kc == 0), stop=(kc == KC - 1)),
                                r=[("hT", hs), ("wbf", kc)], w=[("bank", b)])
                        P.op("dve", lambda e, b=b, t=t, hs=hs: e.tensor_copy(stv[hs][:, t, :], banks[b]),
                             r=[("bank", b)], w=[("stv", hs)])
                    continue
                for h in range(4):
                    b = 4 + (n_acc % 4)
                    n_acc += 1
                    col = typ * 512 + h * 128
                    for kc in range(KC):
                        P.op("pe", lambda e, b=b, kc=kc, col=col, hs=hs: e.matmul(
                            banks[b], wbf[:, kc, col:col + 128], hT[hs][:, kc, :],
                            start=(kc == 0), stop=(kc == KC - 1)),
                            r=[("hT", hs), ("wbf", kc)], w=[("bank", b)])
                    if typ == 0:
                        P.op("act", lambda e, b=b, h=h, hs=hs: e.activation(stq[hs][:, h, :], banks[b], AF.Copy, scale=scale),
                             r=[("bank", b)], w=[("stq", hs)])
                    elif typ == 1:
                        P.op("dve", lambda e, b=b, h=h, hs=hs: e.tensor_copy(stk[hs][:, h, :], banks[b]),
                             r=[("bank", b)], w=[("stk", hs)])
                    else:
                        P.op("act", lambda e, b=b, h=h, hs=hs: e.activation(stg[hs][:, h, :], banks[b], AF.Silu),
                             r=[("bank", b)], w=[("stg", hs)])
            cols = slice(G * 512, (G + 1) * 512)
            P.dma("sp", qT_d.rearrange("h p t -> p h t")[:, :, cols], stq[hs], r=[("stq", hs)], w=[("qT_d", G)])
            P.dma("sp", kT_d.rearrange("h p t -> p h t")[:, :, cols], stk[hs], r=[("stk", hs)], w=[("kT_d", G)])
            P.dma("sp", sgT_d.rearrange("h p t -> p h t")[:, :, cols], stg[hs], r=[("stg", hs)], w=[("sgT_d", G)])
            P.dma("sp", v_d[G * 512:(G + 1) * 512, :].rearrange("(t p) c -> p t c", p=128), stv[hs],
                  r=[("stv", hs)], w=[("v_d", G)])
        rec.barrier()
        sb_reset()
        wst2 = [P.sb("wsto0", [128, 2048], F32)]
        wob = P.sb("wob", [128, 4, D_MODEL], BF16)
        load_wout(woo[oi], wob, wst2)
        kT = [P.sb(f"kT{i}", [128, S], BF16) for i in range(4)]
        vv = [P.sb(f"vv{i}", [128, NT, 128], BF16) for i in range(4)]
        NQ = 3
        qg = [P.sb(f"qg{i}", [128, 512], BF16) for i in range(NQ)]
        sg = [P.sb(f"sg{i}", [128, 512], BF16) for i in range(NQ)]
        e_sb = [P.sb(f"e{i}", [128, 512], F32) for i in range(2)]
        sp_sb = [P.sb(f"sp{i}", [128, 512], BF16) for i in range(3)]
        rcs = [P.sb(f"rcs{i}", [128, 512], F32) for i in range(3)]
        w_sb = [P.sb(f"w{i}", [128, 512], BF16) for i in range(2)]
        crep = [P.sb(f"crep{i}", [128, 512], F32) for i in range(2)]
        ogs = [P.sb(f"ogs{i}", [128, 4, 512], BF16) for i in range(2)]
        xt = [P.sb(f"xt{i}", [128, D_MODEL], F32) for i in range(2)]
        for h in range(4):
            P.dma("sp", kT[h], kT_d[h], w=[("kT", h)])
            P.dma("sp", vv[h], v_d.rearrange("(n p) (h d) -> p n h d", p=128, h=4)[:, :, h, :], w=[("vv", h)])
        units = []
        gi = 0
        for g in range(NG):
            for h in range(4):
                kbs = list(range(4 * g + 3, -1, -1))
                for n, kb in enumerate(kbs):
                    units.append(dict(h=h, g=g, kb=kb, first=(n == 0), last=(n == len(kbs) - 1),
                                      diag=(kb - 4 * g) if kb >= 4 * g else None, gi=gi))
                gi += 1
        NU = len(units)

        def load_group(h, g, gi):
            s = gi % NQ
            cols = slice(g * 512, (g + 1) * 512)
            P.dma("sp", qg[s], qT_d[h][:, cols], w=[("qg", s)])
            P.dma("sp", sg[s], sgT_d[h][:, cols], w=[("sg", s)])

        def out_group(g):
            os_ = g % 2
            for t in range(4):
                ti = g * 4 + t
                s2 = ti % 2
                P.dma("sp", xt[s2], x_src[ti * 128:(ti + 1) * 128, :], r=[("xres", g)], w=[("xt", s2)])
                for jg in range(4):
                    b = 4 + jg % 2
                    for c in range(4):
                        P.op("pe", lambda e, b=b, c=c, jg=jg, t=t, os_=os_: e.matmul(
                            banks[b], ogs[os_][:, c, t * 128:(t + 1) * 128], wob[:, c, jg * 512:(jg + 1) * 512],
                            start=(c == 0), stop=(c == 3)),
                            r=[("ogs", os_), ("wob", c)], w=[("bank", b)])
                    xs_ = xt[s2][:, jg * 512:(jg + 1) * 512]
                    P.op("dve", lambda e, b=b, xs_=xs_: e.scalar_tensor_tensor(xs_, xs_, 0.25, banks[b], ALU.mult, ALU.add),
                         r=[("bank", b), ("xt", s2)], w=[("xt", s2)])
                P.dma("sp", ypart[ti * 128:(ti + 1) * 128, :], xt[s2], r=[("xt", s2)], w=[("ypart", g)])
            rows = slice(g * 512, (g + 1) * 512)
            P.rec.add("pool", lambda e, rows=rows: e.collective_compute(
                "AllReduce", ALU.add, replica_groups=GROUPS, ins=[ypart[rows, :]], outs=[xres[rows, :]]),
                r=[("ypart", g)], w=[("xres", g)], kind="cc")
            if last:
                P.dma("sp", y_out[rows, :], xres[rows, :], r=[("xres", g)], w=[("yout", g)])

        load_group(0, 0, 0)
        for step in range(NU + 2):
            if step < NU:
                u = units[step]
                h, g, kb, gi_ = u["h"], u["g"], u["kb"], u["gi"]
                qs = gi_ % NQ
                if u["first"]:
                    if step + (4 * g + 4) < NU:
                        un = units[step + 4 * g + 4]
                        load_group(un["h"], un["g"], un["gi"])
                zb = step % 2
                kblk = kT[h][:, kb * 128:(kb + 1) * 128]
                dg = u["diag"]
                P.op("pe", lambda e, zb=zb, kblk=kblk, qs=qs, dg=dg: e.matmul(
                    banks[zb], kblk, qg[qs], start=True, stop=(dg is None)),
                    r=[("kT", h), ("qg", qs)], w=[("bank", zb)])
                if dg is not None:
                    P.op("pe", lambda e, zb=zb, dg=dg: e.matmul(banks[zb], ident, negmask[:, dg, :], start=False, stop=True),
                         r=CONST_KEYS, w=[("bank", zb)])
                es, ss_ = step % 2, step % 3
                P.op("act", lambda e, zb=zb, es=es: e.activation(e_sb[es], banks[zb], AF.Exp),
                     r=[("bank", zb)], w=[("e", es)])
                P.op("act", lambda e, es=es, ss_=ss_: e.activation(sp_sb[ss_], e_sb[es], AF.Ln, bias=1.0),
                     r=[("e", es)], w=[("sp", ss_)])
            if 0 <= step - 1 < NU:
                s1 = step - 1
                u = units[s1]
                h, g, kb, gi_ = u["h"], u["g"], u["kb"], u["gi"]
                qs = gi_ % NQ
                ss_ = s1 % 3
                rb, ab = 2 + s1 % 2, 4 + s1 % 2
                kblk = kT[h][:, kb * 128:(kb + 1) * 128]
                dg = u["diag"]
                P.op("pe", lambda e, rb=rb, ss_=ss_: e.matmul(banks[rb], trineg, sp_sb[ss_], start=True, stop=False),
                     r=[("sp", ss_), ("const", 1)], w=[("bank", rb)])
                P.op("pe", lambda e, rb=rb, kblk=kblk, qs=qs, dg=dg: e.matmul(
                    banks[rb], kblk, qg[qs], start=False, stop=(dg is None)),
                    r=[("kT", h), ("qg", qs)], w=[("bank", rb)])
                if dg is not None:
                    P.op("pe", lambda e, rb=rb, dg=dg: e.matmul(banks[rb], ident, negmask[:, dg, :], start=False, stop=True),
                         r=CONST_KEYS, w=[("bank", rb)])
                cs_old, cs_new = s1 % 2, (s1 + 1) % 2
                rs = s1 % 3
                if not u["last"]:
                    P.op("pe", lambda e, ab=ab, ss_=ss_: e.matmul(banks[ab], onesneg, sp_sb[ss_], start=True, stop=True),
                         r=[("sp", ss_), ("const", 2)], w=[("bank", ab)])
                if u["first"]:
                    P.op("dve", lambda e, rs=rs, rb=rb: e.tensor_copy(rcs[rs], banks[rb]),
                         r=[("bank", rb)], w=[("rcs", rs)])
                    if not u["last"]:
                        P.op("dve", lambda e, cs_new=cs_new, ab=ab: e.tensor_copy(crep[cs_new], banks[ab]),
                             r=[("bank", ab)], w=[("crep", cs_new)])
                else:
                    P.op("dve", lambda e, rs=rs, rb=rb, cs_old=cs_old: e.tensor_tensor(rcs[rs], banks[rb], crep[cs_old], ALU.add),
                         r=[("bank", rb), ("crep", cs_old)], w=[("rcs", rs)])
                    if not u["last"]:
                        P.op("dve", lambda e, cs_new=cs_new, cs_old=cs_old, ab=ab: e.tensor_tensor(
                            crep[cs_new], banks[ab], crep[cs_old], ALU.add),
                            r=[("bank", ab), ("crep", cs_old)], w=[("crep", cs_new)])
            if 0 <= step - 2 < NU:
                s2 = step - 2
                u = units[s2]
                h, g, kb, gi_ = u["h"], u["g"], u["kb"], u["gi"]
                qs = gi_ % NQ
                rs, ws = s2 % 3, s2 % 2
                ob = 6 + gi_ % 2
                P.op("act", lambda e, rs=rs, ws=ws: e.activation(w_sb[ws], rcs[rs], AF.Exp),
                     r=[("rcs", rs)], w=[("w", ws)])
                P.op("pe", lambda e, ob=ob, h=h, kb=kb, ws=ws, u=u: e.matmul(
                    banks[ob], vv[h][:, kb, :], w_sb[ws], start=u["first"], stop=u["last"]),
                    r=[("vv", h), ("w", ws)], w=[("bank", ob)])
                if u["last"]:
                    os_ = g % 2
                    P.op("dve", lambda e, ob=ob, qs=qs, os_=os_, h=h: e.tensor_tensor(ogs[os_][:, h, :], banks[ob], sg[qs], ALU.mult),
                         r=[("bank", ob), ("sg", qs)], w=[("ogs", os_)])
                    if h == 3:
                        out_group(g)
        rec.barrier()

    def even_layer(li, ei, x_src, last):
        rec = P.rec
        sb_reset()
        NCOL = 1664
        wbf = P.sb("wbf", [128, KC, NCOL], BF16)
        lnT = P.sb("lnT", [128, KC], F32)
        xin = [P.sb(f"xin{i}", [128, D_MODEL], F32) for i in range(2)]
        junk = P.sb("junk", [128, D_MODEL], BF16)
        ss = [P.sb(f"ss{i}", [128, 4], F32) for i in range(2)]
        rstd = [P.sb(f"rstd{i}", [128, 4], F32) for i in range(2)]
        xn = [P.sb(f"xn{i}", [128, D_MODEL], BF16) for i in range(2)]
        hT = [P.sb(f"hT{i}", [128, KC, 512], BF16) for i in range(2)]
        wqk = P.sb("wqk", [128, 320], F32)
        esink = P.sb("esink", [128, 4], F32)
        lbv = P.sb("lbv", [128, 4], F32)
        lb = P.sb("lb", [128, 2], F32)
        oml = P.sb("oml", [128, 2], F32)
        gn = P.sb("gn", [128, 1], F32)
        swam = P.sb("swam", [128, 2, 128], BF16)
        m16 = P.sb("m16", [128, 128], BF16)
        bmk = P.sb("bmk", [128, 8], BF16)
        onesf = P.sb("onesf", [128, 512], F32)
        ropeg = [P.sb(f"ropeg{i}", [128, 4, 64], F32) for i in range(2)]
        P.dma("sp", wqk, qkn[ei].partition_broadcast(128).rearrange("p a b -> p (a b)"), w=["wqk"])
        P.dma("sp", esink, snk[ei].partition_broadcast(128).rearrange("p a b -> p (a b)"), w=["esink"])
        P.dma("sp", lbv, lbt, w=["lbv"])
        P.dma("sp", gn, gnt[ei], w=["gn"])
        P.dma("sp", swam[:, 0, :], emask[0][:, 0:128], w=["swam"])
        P.dma("sp", swam[:, 1, :], emask[1][:, 0:128], w=["swam"])
        P.dma("sp", m16, emask[2][:, 0:128], w=["m16"])
        P.dma("sp", bmk, bmask, w=["bmk"])
        P.op("pool", lambda e: e.memset(onesf, 1.0), w=["onesf"])
        P.op("act", lambda e: e.activation(esink, esink, AF.Exp), r=["esink"], w=["esink"])
        if ei == 0:
            P.op("dve", lambda e: e.memset(lb, 0.0), w=["lb"])
            P.op("dve", lambda e: e.memset(oml, 1.0), w=["oml"])
        else:
            P.op("dve", lambda e: e.tensor_tensor(lb, lbv[:, 0:2], lbv[:, 2:4], ALU.subtract), r=["lbv"], w=["lb"])
            P.op("act", lambda e: e.activation(lb, lb, AF.Exp), r=["lb"], w=["lb"])
            P.op("dve", lambda e: e.tensor_scalar(lb, lb, 1.0, None, ALU.add), r=["lb"], w=["lb"])
            P.op("dve", lambda e: e.reciprocal(lb, lb), r=["lb"], w=["lb"])
            P.op("dve", lambda e: e.tensor_scalar(oml, lb, -1.0, 1.0, ALU.mult, ALU.add), r=["lb"], w=["oml"])
        mix_start = P.sb_off
        wst = [P.sb(f"wst{i}", [128, 2048], F32) for i in range(2)]
        load_weights(wie[ei], NCOL, lne[ei], wbf, wst, lnT)
        rec.barrier()
        P.sb_off = mix_start
        sga = [P.sb(f"sga{i}", [128, 256], F32) for i in range(4)]
        sq = P.sb("sq", [128, 320], F32)
        s5 = P.sb("s5", [128, 5], F32)
        qk = P.sb("qk", [128, 5, 64], F32)
        rt = [P.sb(f"rt{i}", [128, 5, 32], F32) for i in range(4)]
        qkr = P.sb("qkr", [128, 6, 64], BF16)
        qkT = [P.sb(f"qkT{i}", [128, 3, 128], BF16) for i in range(3)]
        vaug = [P.sb(f"vaug{i}", [128, 66], BF16) for i in range(3)]
        qz = [P.sb(f"qz{i}", [128, 4, 128], BF16) for i in range(3)]
        ex = [P.sb(f"ex{i}", [128, 2, 2, 128], F32) for i in range(2)]
        pm = [P.sb(f"pm{i}", [128, 2, 2, 128], BF16) for i in range(2)]
        den = P.sb("den", [128, 4], F32)
        ya1 = P.sb("ya1", [128, 4, 64], F32)
        ya = P.sb("ya", [128, 256], BF16)
        yaT = [P.sb("yaT0", [128, 2, 512], BF16)] * 2
        qs = [P.sb(f"qs{i}", [128, 512], F32) for i in range(2)]
        sgb = [P.sb(f"sgb{i}", [128, 512], F32) for i in range(2)]
        ef = [P.sb(f"ef{i}", [128, 512], F32) for i in range(2)]
        fbuf = P.sb("fbuf", [128, 512], F32)
        logf = P.sb("logf", [128, 512], F32)
        kk = P.sb("kk", [128, 512], F32)
        Gb = [[P.sb(f"Gb{h}", [128, 516], F32)] * 2 for h in range(2)]
        gcar = [P.sb(f"gcar{h}", [128, 1], F32) for h in range(2)]
        aa = P.sb("aa", [128, 512], F32)
        ea = P.sb("ea", [128, 512], F32)
        qt = [P.sb(f"qt{i}", [128, 512], BF16) for i in range(2)]
        kt = [P.sb(f"kt{i}", [128, 512], BF16) for i in range(2)]
        kh = [P.sb(f"kh{i}", [128, 512], BF16) for i in range(2)]
        dec = [P.sb(f"dec{i}", [128, 32], F32) for i in range(2)]
        vt = [P.sb(f"vt{i}", [128, 256], BF16) for i in range(4)]
        attT = [P.sb(f"attT{i}", [128, 128], BF16) for i in range(2)]
        kblk = [P.sb(f"kblk{i}", [128, 8, 128], BF16) for i in range(2)]
        Sf = [[P.sb(f"Sf{h}{i}", [128, 128], F32) for i in range(2)] for h in range(2)]
        NSB = 4
        Sb = [[P.sb(f"Sb{h}{i}", [128, 128], BF16) for i in range(NSB)] for h in range(2)]
        osq = P.sb("osq", [128, 128], BF16)
        rs = P.sb("rs", [128, 128], F32)
        otmp = P.sb("otmp", [128, 128], F32)
        ybT = [P.sb("ybT0", [128, 2, 512], BF16)] * 2
        for i in range(3):
            P.op("pool", lambda e, i=i: e.memset(vaug[i], 1.0), w=[("vaug", i)])
            P.op("pool", lambda e, i=i: e.memset(qz[i], 0.0), w=[("qkT", i)])
        for h in range(2):
            P.op("pool", lambda e, h=h: e.memset(Sf[h][0], 0.0), w=[("Sf", h, 0)])
            P.op("pool", lambda e, h=h: e.memset(Sb[h][0], 0.0), w=[("Sb", h, 0)])
            P.op("pool", lambda e, h=h: e.memset(gcar[h], 0.0), w=[("gcar", h)])
        ACC = [2, 3]
        sc_v = [pview(4, 0, 512).rearrange("p (a b c) -> p a b c", a=2, b=2), pview(5, 0, 512).rearrange("p (a b c) -> p a b c", a=2, b=2)]
        av_ps = pview(6, 0, 264, inner=66)
        qkT_ps = pview(6, 264, 192, BF16, inner=128)
        att_ps = pview(7, 0, 128)
        khT_ps = pview(7, 128, 64, BF16)
        o_ps = pview(7, 192, 128)
        yT_ps = pview(7, 320, 128, BF16, inner=128)
        ssq_ps = pview(7, 384, 128)
        sc_v = [pview(4, 0, 512).rearrange("p (a b c) -> p a b c", a=2, b=2)]
        U_ps = pview(5, 0, 512, inner=128)
        sstate = {"sb": [0, 0], "sf": [0, 0], "nacc": 0}

        def acc_bank():
            b = ACC[sstate["nacc"] % 2]
            sstate["nacc"] += 1
            return b

        def project_fm(col, hs):
            b = acc_bank()
            for kc in range(KC):
                P.op("pe", lambda e, b=b, kc=kc, col=col, hs=hs: e.matmul(
                    banks[b], wbf[:, kc, col:col + 128], hT[hs][:, kc, :], start=(kc == 0), stop=(kc == KC - 1)),
                    r=[("hT", hs), ("wbf", kc)], w=[("bank", b)])
            return b

        def project_tm(c0, c1, hs, t):
            b = acc_bank()
            for kc in range(KC):
                P.op("pe", lambda e, b=b, kc=kc, hs=hs, t=t: e.matmul(
                    banks[b][:, 0:c1 - c0], hT[hs][:, kc, t * 128:(t + 1) * 128], wbf[:, kc, c0:c1],
                    start=(kc == 0), stop=(kc == KC - 1)),
                    r=[("hT", hs), ("wbf", kc)], w=[("bank", b)])
            return b

        def swa_tile(G, t, hs):
            ti = G * 4 + t
            gp = G % 2
            cur, prv = ti % 3, (ti - 1) % 3
            b = project_tm(0, 384, hs, t)
            pa = banks[b]
            P.op("act", lambda e: e.activation(sq, pa[:, 0:320], AF.Square), r=[("bank", b)], w=["sq"])
            P.op("dve", lambda e: e.tensor_reduce(s5, sq.rearrange("p (a b) -> p a b", b=64), AX.X, ALU.add), r=["sq"], w=["s5"])
            P.op("act", lambda e: e.activation(s5, s5, AF.Ln, bias=EPS, scale=1.0 / 64), r=["s5"], w=["s5"])
            P.op("act", lambda e: e.activation(s5, s5, AF.Exp, scale=-0.5), r=["s5"], w=["s5"])
            P.op("dve", lambda e: e.tensor_tensor(qk, pa[:, 0:320].rearrange("p (a b) -> p a b", b=64),
                                                  s5.unsqueeze(2).to_broadcast([128, 5, 64]), ALU.mult),
                 r=[("bank", b), "s5"], w=["qk"])
            P.op("dve", lambda e, cur=cur: e.tensor_copy(vaug[cur][:, 0:64], pa[:, 320:384]), r=[("bank", b)], w=[("vaug", cur)])
            P.op("dve", lambda e: e.tensor_tensor(qk, qk, wqk.rearrange("p (a b) -> p a b", b=64), ALU.mult),
                 r=["qk", "wqk"], w=["qk"])
            cs = ropeg[gp][:, t, :]
            cosb = cs[:, 0:32].unsqueeze(1).to_broadcast([128, 5, 32])
            sinb = cs[:, 32:64].unsqueeze(1).to_broadcast([128, 5, 32])
            x1, x2 = qk[:, :, 0:32], qk[:, :, 32:64]
            P.op("dve", lambda e: e.tensor_tensor(rt[0], x1, cosb, ALU.mult), r=["qk", ("ropeg", gp)], w=[("rt", 0)])
            P.op("pool", lambda e: e.tensor_tensor(rt[1], x2, sinb, ALU.mult), r=["qk", ("ropeg", gp)], w=[("rt", 1)])
            P.op("dve", lambda e: e.tensor_tensor(rt[2], x2, cosb, ALU.mult), r=["qk", ("ropeg", gp)], w=[("rt", 2)])
            P.op("pool", lambda e: e.tensor_tensor(rt[3], x1, sinb, ALU.mult), r=["qk", ("ropeg", gp)], w=[("rt", 3)])
            P.op("dve", lambda e: e.tensor_tensor(qkr[:, 0:5, 0:32], rt[0], rt[1], ALU.subtract),
                 r=[("rt", 0), ("rt", 1)], w=["qkr"])
            P.op("dve", lambda e: e.tensor_tensor(qkr[:, 0:5, 32:64], rt[2], rt[3], ALU.add),
                 r=[("rt", 2), ("rt", 3)], w=["qkr"])
            P.op("dve", lambda e: e.tensor_copy(qkr[:, 5, :], qkr[:, 4, :]), r=["qkr"], w=["qkr"])
            for i in range(3):
                P.op("pe", lambda e, i=i: e.transpose(qkT_ps[:, i, :], qkr[:, 2 * i:2 * i + 2, :].rearrange("p a b -> p (a b)"), ident),
                     r=["qkr", ("const", 0)], w=[("bank", 6)])
            P.op("act", lambda e, cur=cur: e.activation(qkT[cur], qkT_ps, AF.Copy), r=[("bank", 6)], w=[("qkT", cur)])
            qz4 = qz[cur].rearrange("p (a b) t -> p a b t", b=2)
            P.op("dve", lambda e, qz4=qz4: e.tensor_copy(qz4[0:64, :, 0, :], qkT_ps[0:64, 0:2, :]), r=[("bank", 6)], w=[("qkT", cur)])
            P.op("dve", lambda e, qz4=qz4: e.tensor_copy(qz4[64:128, :, 1, :], qkT_ps[64:128, 0:2, :]), r=[("bank", 6)], w=[("qkT", cur)])
            import os
            lvl = int(os.environ.get("DBG_SWA", "9"))
            if lvl < 1:
                return
            has_prev = ti > 0
            nkb = 2 if has_prev else 1
            for hp in range(2):
                sc = sc_v[0]
                xs = hp
                for kb in range(nkb):
                    src = qkT[cur] if kb == 0 else qkT[prv]
                    P.op("pe", lambda e, sc=sc, kb=kb, src=src, hp=hp, cur=cur: e.matmul(
                        sc[:, kb, :, :], src[:, 2, :], qz[cur][:, 2 * hp:2 * hp + 2, :], start=True, stop=True),
                        r=[("qkT", cur), ("qkT", prv)], w=[("bank", 4)])
                if lvl < 2:
                    continue
                P.op("act", lambda e, sc=sc, xs=xs, nkb=nkb: e.activation(ex[xs][:, 0:nkb], sc[:, 0:nkb], AF.Exp, scale=0.125),
                     r=[("bank", 4)], w=[("ex", xs)])
                P.op("dve", lambda e, xs=xs, nkb=nkb: e.tensor_tensor(
                    pm[xs][:, 0:nkb], ex[xs][:, 0:nkb], swam[:, 0:nkb].unsqueeze(2).to_broadcast([128, nkb, 2, 128]), ALU.mult),
                    r=[("ex", xs), "swam"], w=[("pm", xs)])
                if lvl < 3:
                    continue
                for hi in range(2):
                    hh = 2 * hp + hi
                    for kb in range(nkb):
                        vsrc = vaug[cur] if kb == 0 else vaug[prv]
                        P.op("pe", lambda e, hh=hh, xs=xs, kb=kb, hi=hi, vsrc=vsrc, nkb=nkb: e.matmul(
                            av_ps[:, hh, 0:65], pm[xs][:, kb, hi, :], vsrc[:, 0:65], start=(kb == 0), stop=(kb == nkb - 1)),
                            r=[("pm", xs), ("vaug", cur), ("vaug", prv)], w=[("bank", 6)])
            if lvl < 4:
                return
            P.op("dve", lambda e: e.tensor_tensor(den, av_ps[:, :, 64], esink, ALU.add), r=[("bank", 6), "esink"], w=["den"])
            P.op("dve", lambda e: e.reciprocal(den, den), r=["den"], w=["den"])
            P.op("dve", lambda e: e.tensor_tensor(ya1, av_ps[:, :, 0:64], den.unsqueeze(2).to_broadcast([128, 4, 64]), ALU.mult),
                 r=[("bank", 6), "den"], w=["ya1"])
            P.op("pool", lambda e, t=t: e.tensor_tensor(ya, ya1.rearrange("p a b -> p (a b)"), sga[t], ALU.mult),
                 r=["ya1", ("sga", t)], w=["ya"])
            yb_ = acc_bank()
            yTp = pview(yb_, 0, 128, BF16, inner=128)
            for i in range(2):
                P.op("pe", lambda e, i=i, yTp=yTp: e.transpose(yTp[:, i, :], ya[:, i * 128:(i + 1) * 128], ident),
                     r=["ya", ("const", 0)], w=[("bank", yb_)])
            P.op("act", lambda e, gp=gp, t=t, yTp=yTp: e.activation(yaT[gp][:, :, t * 128:(t + 1) * 128], yTp, AF.Copy),
                 r=[("bank", yb_)], w=[("yaT", 0)])

        def hgrn_group_prep(G, hh):
            gp = G % 2
            Gc, Gp = Gb[hh][gp], Gb[hh][1 - gp]
            P.op("dve", lambda e: e.tensor_scalar(fbuf, ef[hh], 1.0, None, ALU.add), r=[("ef", hh)], w=["fbuf"])
            P.op("dve", lambda e: e.reciprocal(fbuf, fbuf), r=["fbuf"], w=["fbuf"])
            P.op("dve", lambda e: e.tensor_scalar(fbuf, fbuf, oml[:, hh:hh + 1], lb[:, hh:hh + 1], ALU.mult, ALU.add),
                 r=["fbuf", "oml", "lb"], w=["fbuf"])
            P.op("act", lambda e: e.activation(logf, fbuf, AF.Ln), r=["fbuf"], w=["logf"])
            P.op("pool", lambda e: e.tensor_scalar(kk, fbuf, -1.0, 1.0, ALU.mult, ALU.add), r=["fbuf"], w=["kk"])
            P.op("dve", lambda e: e.tensor_copy(Gc[:, 0:1], gcar[hh]), r=[("gcar", hh)], w=[("Gb", hh)])
            P.op("dve", lambda e: e.tensor_tensor_scan(Gc[:, 1:513], onesf, logf, Gc[:, 0:1], ALU.mult, ALU.add),
                 r=["onesf", "logf", ("Gb", hh)], w=[("Gb", hh)])
            P.op("dve", lambda e: e.tensor_copy(gcar[hh], Gc[:, 512:513]), r=[("Gb", hh)], w=[("gcar", hh)])
            Gi = Gc[:, 1:513].rearrange("p (c i) -> p c i", i=16)
            Rc = Gc[:, 0:512].rearrange("p (c i) -> p c i", i=16)[:, :, 0:1]
            Ec = Gc[:, 1:513].rearrange("p (c i) -> p c i", i=16)[:, :, 15:16]
            a3 = aa.rearrange("p (c i) -> p c i", i=16)
            P.op("dve", lambda e: e.tensor_tensor(a3, Gi, Rc.to_broadcast([128, 32, 16]), ALU.subtract),
                 r=[("Gb", hh)], w=["aa"])
            P.op("act", lambda e: e.activation(ea, aa, AF.Exp), r=["aa"], w=["ea"])
            P.op("dve", lambda e: e.tensor_tensor(qt[hh], qs[hh], ea, ALU.mult), r=[("qs", hh), "ea"], w=[("qt", hh)])
            P.op("pool", lambda e: e.tensor_scalar(aa, aa, -1.0, 80.0, ALU.mult, ALU.min), r=["aa"], w=["aa"])
            P.op("act", lambda e: e.activation(ea, aa, AF.Exp), r=["aa"], w=["ea"])
            P.op("dve", lambda e: e.tensor_tensor(kt[hh], kk, ea, ALU.mult), r=["kk", "ea"], w=[("kt", hh)])
            P.op("dve", lambda e: e.tensor_tensor(a3, Ec.to_broadcast([128, 32, 16]), Gi, ALU.subtract),
                 r=[("Gb", hh)], w=["aa"])
            P.op("act", lambda e: e.activation(ea, aa, AF.Exp), r=["aa"], w=["ea"])
            P.op("dve", lambda e: e.tensor_tensor(kh[hh], kk, ea, ALU.mult), r=["kk", "ea"], w=[("kh", hh)])
            P.op("dve", lambda e: e.tensor_tensor(dec[hh], Ec.rearrange("p c i -> p (c i)"), Rc.rearrange("p c i -> p (c i)"), ALU.subtract),
                 r=[("Gb", hh)], w=[("dec", hh)])
            P.op("act", lambda e: e.activation(dec[hh], dec[hh], AF.Exp), r=[("dec", hh)], w=[("dec", hh)])

        def hgrn_tile(G, t, hh):
            gp = G % 2
            cols = slice(t * 128, (t + 1) * 128)
            vth = vt[t][:, hh * 128:(hh + 1) * 128]
            sa = (t * 2 + hh) % 2
            P.op("pe", lambda e: e.matmul(att_ps, kt[hh][:, cols], qt[hh][:, cols], start=True, stop=True),
                 r=[("kt", hh), ("qt", hh)], w=[("bank", 7)])
            P.op("dve", lambda e: e.tensor_tensor(attT[sa], att_ps, m16, ALU.mult), r=[("bank", 7), "m16"], w=[("attT", sa)])
            P.op("pe", lambda e: e.transpose(khT_ps, kh[hh][:, cols], ident), r=[("kh", hh), ("const", 0)], w=[("bank", 7)])
            P.op("dve", lambda e: e.tensor_tensor(
                kblk[sa], khT_ps.unsqueeze(1).to_broadcast([128, 8, 128]), bmk.unsqueeze(2).to_broadcast([128, 8, 128]), ALU.mult),
                r=[("bank", 7), "bmk"], w=[("kblk", sa)])
            P.op("pe", lambda e: e.matmul(o_ps, vth, attT[sa], start=True, stop=False),
                 r=[("vt", t), ("attT", sa)], w=[("bank", 7)])
            for c in range(8):
                if c % 4 == 0:
                    for c2 in range(c, c + 4):
                        P.op("pe", lambda e, c2=c2: e.matmul(U_ps[:, c2 % 4, :], kblk[sa][:, c2, :], vth, start=True, stop=True),
                             r=[("kblk", sa), ("vt", t)], w=[("bank", 5)])
                sbi = sstate["sb"][hh]
                sfi = sstate["sf"][hh]
                ccol = t * 128 + 16 * c
                P.op("pe", lambda e, sbi=sbi, ccol=ccol, c=c: e.matmul(
                    o_ps[:, 16 * c:16 * c + 16], Sb[hh][sbi], qt[hh][:, ccol:ccol + 16], start=False, stop=(c == 7)),
                    r=[("Sb", hh, sbi), ("qt", hh)], w=[("bank", 7)])
                nsf, nsb = 1 - sfi, (sbi + 1) % NSB
                dcol = dec[hh][:, 8 * t + c:8 * t + c + 1]
                P.op("dve", lambda e, sfi=sfi, nsf=nsf, dcol=dcol, c=c: e.scalar_tensor_tensor(
                    Sf[hh][nsf], Sf[hh][sfi], dcol, U_ps[:, c % 4, :], ALU.mult, ALU.add),
                    r=[("Sf", hh, sfi), ("dec", hh), ("bank", 5)], w=[("Sf", hh, nsf)])
                P.op("pool", lambda e, nsf=nsf, nsb=nsb: e.tensor_copy(Sb[hh][nsb], Sf[hh][nsf]),
                     r=[("Sf", hh, nsf)], w=[("Sb", hh, nsb)])
                sstate["sb"][hh] = nsb
                sstate["sf"][hh] = nsf
            P.op("act", lambda e: e.activation(osq, o_ps, AF.Square), r=[("bank", 7)], w=["osq"])
            P.op("pe", lambda e: e.matmul(ssq_ps, ones, osq, start=True, stop=True), r=["osq", ("const", 3)], w=[("bank", 7)])
            P.op("act", lambda e: e.activation(rs, ssq_ps, AF.Ln, bias=EPS, scale=1.0 / 128), r=[("bank", 7)], w=["rs"])
            P.op("act", lambda e: e.activation(rs, rs, AF.Exp, scale=-0.5), r=["rs"], w=["rs"])
            P.op("dve", lambda e: e.tensor_tensor(otmp, o_ps, rs, ALU.mult), r=[("bank", 7), "rs"], w=["otmp"])
            P.op("dve", lambda e: e.scalar_tensor_tensor(ybT[gp][:, hh, cols], otmp, gn[:, 0:1], sgb[hh][:, cols], ALU.mult, ALU.mult),
                 r=["otmp", "gn", ("sgb", hh)], w=[("ybT", 0)])

        def group_body(G):
            hs = G % 2
            gp = G % 2
            cols = slice(G * 512, (G + 1) * 512)
            P.dma("sp", ropeg[gp], rope[G * 512:(G + 1) * 512, :].rearrange("(t p) c -> p t c", p=128), w=[("ropeg", gp)])
            for hh in range(2):
                b = project_fm(640 + hh * 128, hs)
                P.op("act", lambda e, b=b, hh=hh: e.activation(qs[hh], banks[b], AF.Silu), r=[("bank", b)], w=[("qs", hh)])
            for hh in range(2):
                b = project_fm(1408 + hh * 128, hs)
                P.op("act", lambda e, b=b, hh=hh: e.activation(sgb[hh], banks[b], AF.Silu), r=[("bank", b)], w=[("sgb", hh)])
            for t in range(4):
                b = project_tm(384, 640, hs, t)
                P.op("act", lambda e, b=b, t=t: e.activation(sga[t], banks[b][:, 0:256], AF.Silu), r=[("bank", b)], w=[("sga", t)])
            for hh in range(2):
                b = project_fm(896 + hh * 128, hs)
                P.op("act", lambda e, b=b, hh=hh: e.activation(ef[hh], banks[b], AF.Exp, scale=-1.0), r=[("bank", b)], w=[("ef", hh)])
            for t in range(4):
                b = project_tm(1152, 1408, hs, t)
                P.op("dve", lambda e, b=b, t=t: e.tensor_copy(vt[t], banks[b][:, 0:256]), r=[("bank", b)], w=[("vt", t)])
            import os
            dbg = int(os.environ.get("DBG_EVEN", "7"))
            for hh in range(2):
                if dbg & 2:
                    hgrn_group_prep(G, hh)
            for t in range(4):
                P.cap = []
                if dbg & 1:
                    swa_tile(G, t, hs)
                ca = P.cap
                P.cap = []
                for hh in range(2):
                    if dbg & 4:
                        hgrn_tile(G, t, hh)
                cb = P.cap
                P.cap = None
                P.merge(ca, cb)
            P.dma("sp", ogT_d[0:2].rearrange("c p t -> p c t")[:, :, cols], yaT[gp], r=[("yaT", 0)], w=[("ogT_d", G)])
            P.dma("sp", ogT_d[2:4].rearrange("c p t -> p c t")[:, :, cols], ybT[gp], r=[("ybT", 0)], w=[("ogT_d", G)])

        norm_transpose_group(0, x_src, xin, junk, ss, rstd, xn, hT, 0, tph, even=True)
        for G in range(NG):
            if G + 1 < NG:
                norm_transpose_group(G + 1, x_src, xin, junk, ss, rstd, xn, hT, (G + 1) % 2, tph, even=True)
            group_body(G)
        rec.barrier()
        sb_reset()
        wst2 = [P.sb(f"wste{i}", [128, 2048], F32) for i in range(2)]
        wob = P.sb("wob", [128, 4, D_MODEL], BF16)
        load_wout(woe[ei], wob, wst2)
        out_phase(x_src, wob, last)
        rec.barrier()

    cur = x_in
    no = ne = 0
    for li, typ in enumerate(layers):
        last = li == len(layers) - 1
        if typ == "o":
            odd_layer(li, no, cur, last)
            no += 1
        else:
            even_layer(li, ne, cur, last)
            ne += 1
        cur = xres
    P.rec.barrier()
    P.rec.emit(nc)
    return P


def _bf(a):
    return np.asarray(a, dtype=np.float32).astype(ml_dtypes.bfloat16)


def make_consts():
    j = np.arange(128)[:, None]
    s = np.arange(128)[None, :]
    ident = (j == s).astype(np.float32)
    trineg = -(j >= s).astype(np.float32)
    onesneg = -np.ones((128, 128), np.float32)
    ones = np.ones((128, 128), np.float32)
    cmat = _bf(np.stack([ident, trineg, onesneg, ones]))
    tri = np.where(j >= s, NEG, 0.0).astype(np.float32)
    cm = np.zeros((4, 128, 512), np.float32)
    for d in range(4):
        for i in range(4):
            blk = cm[d][:, i * 128:(i + 1) * 128]
            if i < d:
                blk[:] = NEG
            elif i == d:
                blk[:] = tri
    return cmat, _bf(cm)


def make_in_maps(inputs, S, layers):
    x = np.asarray(inputs["x"], np.float32)
    cmat, cmask = make_consts()
    half = 32
    inv_freq = (1.0 / (np.float32(10000.0) ** (np.arange(half, dtype=np.float32) / np.float32(half)))).astype(np.float32)
    ang = (np.arange(S, dtype=np.float32)[:, None] * inv_freq[None, :]).astype(np.float32)
    rope_tab = np.ascontiguousarray(np.concatenate([np.cos(ang), np.sin(ang)], axis=1).astype(np.float32))
    kk_ = np.arange(128)[:, None]
    qq_ = np.arange(128)[None, :]
    em = np.zeros((3, 128, 512), np.float32)
    em[0][:, :128] = (kk_ <= qq_)
    em[1][:, :128] = (kk_ > qq_)
    em[2][:, :128] = ((kk_ // 16) == (qq_ // 16)) & (kk_ <= qq_)
    emask = _bf(em)
    bmask = _bf((np.arange(128)[:, None] // 16) == np.arange(8)[None, :])
    maps = []
    for c in range(8):
        b, r = c // 4, c % 4
        m = {"x": np.ascontiguousarray(x[b, :S]), "cmat": cmat, "cmask": cmask}
        no = ne = 0
        for typ in layers:
            if typ == "o":
                w = np.asarray(inputs["w_in_odd"][no], np.float32)
                cols = []
                for part in range(4):
                    cols.append(w[:, part * 2048 + r * 512: part * 2048 + (r + 1) * 512])
                m[f"wio{no}"] = np.ascontiguousarray(np.concatenate(cols, axis=1))
                m[f"lno{no}"] = np.ascontiguousarray(np.asarray(inputs["ln_odd"][no], np.float32).reshape(KC, 128).T)
                m[f"woo{no}"] = np.ascontiguousarray(np.asarray(inputs["w_out_odd"][no], np.float32)[r * 512:(r + 1) * 512])
                no += 1
            else:
                w = np.asarray(inputs["w_in_even"][ne], np.float32)
                kv = r // 2
                cols = [w[:, r * 256:(r + 1) * 256],
                        w[:, 1024 + kv * 64:1024 + (kv + 1) * 64],
                        w[:, 1152 + kv * 64:1152 + (kv + 1) * 64],
                        w[:, 1280 + r * 256:1280 + (r + 1) * 256],
                        w[:, 2304 + r * 256:2304 + (r + 1) * 256],
                        w[:, 3328 + r * 256:3328 + (r + 1) * 256],
                        w[:, 4352 + r * 256:4352 + (r + 1) * 256],
                        w[:, 5376 + r * 256:5376 + (r + 1) * 256]]
                m[f"wie{ne}"] = np.ascontiguousarray(np.concatenate(cols, axis=1))
                m[f"lne{ne}"] = np.ascontiguousarray(np.asarray(inputs["ln_even"][ne], np.float32).reshape(KC, 128).T)
                wo = np.asarray(inputs["w_out_even"][ne], np.float32)
                m[f"woe{ne}"] = np.ascontiguousarray(np.concatenate([wo[r * 256:(r + 1) * 256], wo[1024 + r * 256:1024 + (r + 1) * 256]], axis=0))
                qn = np.asarray(inputs["q_norm_a"][ne], np.float32)
                kn = np.asarray(inputs["k_norm_a"][ne], np.float32)
                m[f"qkn{ne}"] = np.ascontiguousarray(np.concatenate([qn, qn, qn, qn, kn])[None, :])
                m[f"snk{ne}"] = np.ascontiguousarray(np.asarray(inputs["sinks_a"][ne], np.float32)[4 * r:4 * r + 4][None, :])
                m[f"gnt{ne}"] = np.ascontiguousarray(np.asarray(inputs["g_norm_b"][ne], np.float32)[:, None])
                ne += 1
        if "e" in layers:
            lbw = np.asarray(inputs["lower_bounds"], np.float32)
            m["lbt"] = np.ascontiguousarray(np.stack(
                [lbw[l, (2 * r + hh) * 128:(2 * r + hh + 1) * 128] for l in range(2) for hh in range(2)], axis=1))
            m["rope"] = rope_tab
            m["emask"] = emask
            m["bmask"] = bmask
        maps.append(m)
    return maps


_CACHE = {}


def run_layers(inputs, S, layers):
    key = (S, tuple(layers))
    if key not in _CACHE:
        _CACHE[key] = build(S, layers)
    P = _CACHE[key]
    maps = make_in_maps(inputs, S, layers)
    res = run_bass_kernel_spmd(P.nc, maps, core_ids=list(range(8)))
    out = np.stack([res.results[0]["y"], res.results[4]["y"]])
    return out


def kernel(**inputs):
    return run_layers(inputs, SEQ, ["e", "o", "e", "o"]).astype(np.float32)
```

```python
import numpy as np
import ml_dtypes
import concourse.bass as bass
import concourse.mybir as mybir
from concourse.bass_utils import run_bass_kernel_spmd

F32 = mybir.dt.float32
BF16 = mybir.dt.bfloat16
AF = mybir.ActivationFunctionType
ALU = mybir.AluOpType
AX = mybir.AxisListType

D_MODEL = 2048
SEQ = 8192
EPS = 1e-6
NEG = -30000.0
TP = 4
GROUPS = [[0, 1, 2, 3], [4, 5, 6, 7]]
KC = D_MODEL // 128


class Rec:
    COMPUTE = ("pe", "act", "dve", "pool")
    ND = 40
    ROT = 20000

    def __init__(self):
        self.ops = []
        self.lastw = {}
        self.readers = {}
        self.last_on = {}
        self.n_standalone_waits = 0

    def add(self, eng, fn, r=(), w=(), kind="c", extra=()):
        idx = len(self.ops)
        raw = set()
        oth = set()
        for k in r:
            if k in self.lastw:
                raw.add(self.lastw[k])
        for k in w:
            if k in self.lastw:
                oth.add(self.lastw[k])
            for j in self.readers.get(k, ()):
                oth.add(j)
        for j in extra:
            raw.add(j)
        self.ops.append(dict(eng=eng, fn=fn, raw=raw, oth=oth - raw, kind=kind))
        for k in r:
            self.readers.setdefault(k, []).append(idx)
        for k in w:
            self.lastw[k] = idx
            self.readers[k] = []
        self.last_on[eng] = idx
        return idx

    def barrier(self):
        pend = [i for i, o in enumerate(self.ops) if o["kind"] in ("d", "cc") and not o.get("barred")]
        lasts = [v for v in self.last_on.values()]
        for i in pend:
            self.ops[i]["barred"] = True
        deps = set(pend) | set(lasts)
        for eng in ("pe", "act", "dve", "pool", "sp"):
            self.add(eng, None, kind="n", extra=deps)
        self.lastw = {}
        self.readers = {}

    def emit(self, nc):
        ops = self.ops
        n = len(ops)
        for i, o in enumerate(ops):
            keep = set()
            for d in o["raw"] | o["oth"]:
                od = ops[d]
                if od["kind"] == "n":
                    pass
                same = od["eng"] == o["eng"] and od["kind"] in ("c", "n") and o["kind"] in ("c", "n")
                if same:
                    if o["eng"] == "pe":
                        continue
                keep.add(d)
            o["deps"] = keep
        needed = [False] * n
        for o in ops:
            for d in o["deps"]:
                needed[d] = True
        sig = {}
        cnt = {e: 0 for e in ("pe", "act", "dve", "pool", "sp")}
        ndma = 0
        ncc = 0
        dma_prev = {}
        for i, o in enumerate(ops):
            if o["kind"] == "d":
                slot = ndma % self.ND
                val = 16 * (ndma // self.ND + 1)
                sig[i] = ("dma", slot, val)
                if ndma >= self.ND:
                    o["deps"].add(dma_prev[slot])
                    needed[dma_prev[slot]] = True
                dma_prev[slot] = i
                ndma += 1
            elif o["kind"] == "cc":
                ncc += 1
                sig[i] = ("cc", 0, ncc)
            elif needed[i]:
                e = o["eng"]
                cnt[e] += 1
                sig[i] = (e, (cnt[e] - 1) // self.ROT, (cnt[e] - 1) % self.ROT + 1)
        nrot = {e: max(1, (cnt[e] + self.ROT - 1) // self.ROT) for e in cnt}
        from contextlib import ExitStack
        with ExitStack() as st:
            sems = {}
            for e in cnt:
                for k in range(nrot[e]):
                    sems[(e, k)] = st.enter_context(nc.semaphore(f"s_{e}{k}"))
            for k in range(min(self.ND, max(ndma, 1))):
                sems[("dma", k)] = st.enter_context(nc.semaphore(f"s_d{k}"))
            sems[("cc", 0)] = st.enter_context(nc.semaphore("s_cc"))
            block = st.enter_context(nc.Block())
            engs = {"pe": "tensor", "act": "scalar", "dve": "vector", "pool": "gpsimd", "sp": "sync"}
            per = {e: [i for i, o in enumerate(ops) if o["eng"] == e] for e in engs}

            order = ("pe", "act", "dve", "pool", "sp")
            eidx = {e: k for k, e in enumerate(order)}
            know = [None] * n
            kE = {e: [-1] * 5 for e in order}
            seenA = {e: set() for e in order}
            plan = [None] * n
            for i, o in enumerate(ops):
                E = o["eng"]
                cur = kE[E]
                waits = []
                for d in sorted(o["deps"], reverse=True):
                    od = ops[d]
                    if od["kind"] in ("d", "cc"):
                        if d in seenA[E]:
                            continue
                        seenA[E].add(d)
                        waits.append(d)
                    else:
                        if cur[eidx[od["eng"]]] >= d:
                            continue
                        waits.append(d)
                    kd = know[d]
                    for k in range(5):
                        if kd[k] > cur[k]:
                            cur[k] = kd[k]
                plan[i] = waits
                mine = list(cur)
                if o["kind"] in ("c", "n"):
                    mine[eidx[E]] = i
                know[i] = mine

            def run(ename, eobj):
                for i in per[ename]:
                    o = ops[i]
                    waits = plan[i]
                    last_wait = None
                    if o["fn"] is not None and waits:
                        last_wait = waits[-1]
                        waits = waits[:-1]
                    for d in waits:
                        s = sig[d]
                        eobj.wait_ge(sems[(s[0], s[1])], s[2])
                    self.n_standalone_waits += len(waits)
                    if o["fn"] is None:
                        if i in sig:
                            s = sig[i]
                            eobj.nop().then_inc(sems[(s[0], s[1])], 1)
                        continue
                    ins = o["fn"](eobj)
                    if last_wait is not None:
                        s = sig[last_wait]
                        ins._wait_ge(sems[(s[0], s[1])], s[2])
                    if i in sig:
                        s = sig[i]
                        if o["kind"] == "d":
                            ins.then_inc(sems[(s[0], s[1])], 16)
                        else:
                            ins.then_inc(sems[(s[0], s[1])], 1)

            for ename, attr in engs.items():
                if not per[ename]:
                    continue
                deco = getattr(block, attr)

                def body(eobj, _en=ename):
                    run(_en, eobj)
                deco(body)


class Prog:
    def __init__(self, S, layers):
        self.S = S
        self.layers = layers
        self.NT = S // 128
        self.NG = S // 512
        self.nc = bass.Bass("TRN2", target_bir_lowering=False)
        self.rec = Rec()
        self.sb_off = 0
        self.arena = None
        self.ARENA = 206 * 1024
        self.ps_banks = []
        self.inputs = {}
        self._uid = 0

    def dram_in(self, name, shape, dt=F32):
        t = self.nc.dram_tensor(name, list(shape), dt, kind="ExternalInput")
        self.inputs[name] = t
        return t.ap()

    def dram(self, name, shape, dt):
        return self.nc.dram_tensor(name, list(shape), dt).ap()

    def sb(self, name, shape, dt):
        if self.arena is None:
            self.arena = self.nc.alloc_sbuf_tensor("arena", [128, self.ARENA], mybir.dt.uint8).ap()
        esz = 2 if dt == BF16 else 4
        nbytes = int(np.prod(shape[1:])) * esz
        v = self.arena[:, self.sb_off:self.sb_off + nbytes].bitcast(dt)
        if len(shape) == 3:
            v = v.rearrange("p (a b) -> p a b", b=shape[2])
        elif len(shape) == 4:
            v = v.rearrange("p (a b c) -> p a b c", b=shape[2], c=shape[3])
        self.sb_off += (nbytes + 31) // 32 * 32
        assert self.sb_off <= self.ARENA, (name, self.sb_off)
        return v

    cap = None

    def op(self, eng, fn, r=(), w=(), kind="c"):
        if self.cap is not None:
            self.cap.append((eng, fn, tuple(r), tuple(w), kind))
            return None
        return self.rec.add(eng, fn, r, w, kind)

    def dma(self, q, out, in_, r=(), w=()):
        return self.op(q, lambda e: e.dma_start(out=out, in_=in_), r, w, kind="d")

    def merge(self, a, b):
        na, nb = len(a), len(b)
        i = j = 0
        while i < na or j < nb:
            if j >= nb or (i < na and i * nb <= j * na):
                o = a[i]
                i += 1
            else:
                o = b[j]
                j += 1
            self.rec.add(o[0], o[1], o[2], o[3], o[4])


def build(S, layers, final_full=True):
    P = Prog(S, layers)
    nc = P.nc
    NT, NG = P.NT, P.NG
    n_even = sum(1 for l in layers if l == "e")
    n_odd = sum(1 for l in layers if l == "o")

    x_in = P.dram_in("x", [S, D_MODEL])
    y_out = nc.dram_tensor("y", [S, D_MODEL], F32, kind="ExternalOutput").ap()
    cmask = P.dram_in("cmask", [4, 128, 512], BF16)
    cmat = P.dram_in("cmat", [4, 128, 128], BF16)
    wio = [P.dram_in(f"wio{i}", [D_MODEL, 2048]) for i in range(n_odd)]
    lno = [P.dram_in(f"lno{i}", [128, KC]) for i in range(n_odd)]
    woo = [P.dram_in(f"woo{i}", [512, D_MODEL]) for i in range(n_odd)]
    wie = [P.dram_in(f"wie{i}", [D_MODEL, 1664]) for i in range(n_even)]
    lne = [P.dram_in(f"lne{i}", [128, KC]) for i in range(n_even)]
    woe = [P.dram_in(f"woe{i}", [512, D_MODEL]) for i in range(n_even)]
    if n_even:
        qkn = [P.dram_in(f"qkn{i}", [1, 320]) for i in range(n_even)]
        snk = [P.dram_in(f"snk{i}", [1, 4]) for i in range(n_even)]
        lbt = P.dram_in("lbt", [128, 4])
        gnt = [P.dram_in(f"gnt{i}", [128, 1]) for i in range(n_even)]
        rope = P.dram_in("rope", [S, 64])
        emask = P.dram_in("emask", [3, 128, 512], BF16)
        bmask = P.dram_in("bmask", [128, 8], BF16)

    xres = P.dram("xres", [S, D_MODEL], F32)
    ypart = P.dram("ypart", [S, D_MODEL], F32)
    qT_d = P.dram("qT_d", [4, 128, S], BF16)
    kT_d = P.dram("kT_d", [4, 128, S], BF16)
    sgT_d = P.dram("sgT_d", [4, 128, S], BF16)
    v_d = P.dram("v_d", [S, 512], BF16)
    ogT_d = P.dram("ogT_d", [4, 128, S], BF16)

    ident = P.sb("ident", [128, 128], BF16)
    trineg = P.sb("trineg", [128, 128], BF16)
    onesneg = P.sb("onesneg", [128, 128], BF16)
    ones = P.sb("ones", [128, 128], BF16)
    negmask = P.sb("negmask", [128, 4, 512], BF16)
    PERSIST = P.sb_off
    for i, t in enumerate((ident, trineg, onesneg, ones)):
        P.dma("sp", t, cmat[i], w=[("const", i)])
    P.dma("sp", negmask, cmask.rearrange("j p t -> p j t"), w=[("const", "negmask")])
    CONST_KEYS = [("const", i) for i in range(4)] + [("const", "negmask")]

    psum = nc.alloc_psum_tensor("psum", [128, 4096], F32).ap()
    banks = [psum[:, i * 512:(i + 1) * 512] for i in range(8)]
    tpb = [psum[:, i * 1024:(i + 1) * 1024].bitcast(BF16).rearrange("p (k t) -> p k t", t=128) for i in range(2)]
    tph = [psum[:, i * 512:(i + 1) * 512].bitcast(BF16).rearrange("p (k t) -> p k t", t=128) for i in range(2)]

    def pview(bank, off, n, dt=F32, inner=None):
        v = psum[:, bank * 512 + off: bank * 512 + off + n]
        if dt == BF16:
            v = v.bitcast(BF16)
        if inner is not None:
            v = v.rearrange("p (a b) -> p a b", b=inner)
        return v

    def sb_reset():
        P.sb_off = PERSIST

    def load_weights(w_dram, ncols, ln_dram, wbf, wst, lnT):
        P.dma("sp", lnT, ln_dram, w=["lnT"])
        for kc in range(KC):
            s = kc % 2
            P.dma("sp", wst[s][:, :ncols], w_dram[kc * 128:(kc + 1) * 128, :], w=[("wst", s)])
            eng = "pool" if kc % 2 else "dve"
            P.op(eng, lambda e, kc=kc, s=s: e.tensor_scalar(
                wbf[:, kc, :], wst[s][:, :ncols], lnT[:, kc:kc + 1], None, ALU.mult),
                r=[("wst", s), "lnT"], w=[("wbf", kc)])

    def load_wout(w_dram, wob, wst):
        for c in range(4):
            s = c % len(wst)
            P.dma("sp", wst[s][:, :D_MODEL], w_dram[c * 128:(c + 1) * 128, :], w=[("wst", s)])
            eng = "pool" if c % 2 else "dve"
            P.op(eng, lambda e, c=c, s=s: e.tensor_copy(wob[:, c, :], wst[s][:, :D_MODEL]),
                 r=[("wst", s)], w=[("wob", c)])

    def norm_transpose_group(G, x_src, xin, junk, ss, rstd, xn, hT, hs, tpb, even=False):
        gp = G % 2
        nx = len(xin)
        for t in range(4):
            tile_i = G * 4 + t
            xs_ = t % nx
            P.dma("sp", xin[xs_], x_src[tile_i * 128:(tile_i + 1) * 128, :], r=[("xres", G)], w=[("xin", xs_)])
            P.op("act", lambda e, t=t, gp=gp, xs_=xs_: e.activation(junk, xin[xs_], AF.Square, accum_out=ss[gp][:, t:t + 1]),
                 r=[("xin", xs_)], w=["junk", ("ss", gp)])
        if even:
            P.op("act", lambda e, gp=gp: e.activation(rstd[gp], ss[gp], AF.Ln, bias=EPS, scale=1.0 / D_MODEL),
                 r=[("ss", gp)], w=[("rstd", gp)])
            P.op("act", lambda e, gp=gp: e.activation(rstd[gp], rstd[gp], AF.Exp, scale=-0.5),
                 r=[("rstd", gp)], w=[("rstd", gp)])
        else:
            P.op("act", lambda e, gp=gp: e.activation(rstd[gp], ss[gp], AF.Sqrt, bias=EPS, scale=1.0 / D_MODEL),
                 r=[("ss", gp)], w=[("rstd", gp)])
            P.op("dve", lambda e, gp=gp: e.reciprocal(rstd[gp], rstd[gp]), r=[("rstd", gp)], w=[("rstd", gp)])
        nh = 2 if even else 1
        kh = KC // nh

        def tpk(sl):
            return ("bank", sl) if even else ("tp", sl)
        for t in range(4):
            s2 = t % 2
            xs_ = t % nx
            if nx < 4:
                tile_i = G * 4 + t
                P.dma("sp", xin[xs_], x_src[tile_i * 128:(tile_i + 1) * 128, :], r=[("xres", G)], w=[("xin", xs_)])
            P.op("dve", lambda e, s2=s2, t=t, gp=gp, xs_=xs_: e.tensor_scalar(xn[s2], xin[xs_], rstd[gp][:, t:t + 1], None, ALU.mult),
                 r=[("xin", xs_), ("rstd", gp)], w=[("xn", s2)])
            for hf in range(nh):
                sl = (t * nh + hf) % 2
                tp = tpb[sl]
                for k in range(kh):
                    kc = hf * kh + k
                    P.op("pe", lambda e, kc=kc, k=k, s2=s2, tp=tp: e.transpose(tp[:, k, :], xn[s2][:, kc * 128:(kc + 1) * 128], ident),
                         r=[("xn", s2), ("const", 0)], w=[tpk(sl)])
                dst = hT[hs][:, hf * kh:(hf + 1) * kh, t * 128:(t + 1) * 128]
                if (t * nh + hf) % 2 == 0:
                    P.op("act", lambda e, dst=dst, tp=tp: e.activation(dst, tp, AF.Copy),
                         r=[tpk(sl)], w=[("hT", hs)])
                else:
                    P.op("dve", lambda e, dst=dst, tp=tp: e.tensor_copy(dst, tp),
                         r=[tpk(sl)], w=[("hT", hs)])

    def out_phase(x_src, wob, last):
        og = [P.sb(f"og{i}", [128, 4, 512], BF16) for i in range(2)]
        xt = [P.sb(f"xt{i}", [128, D_MODEL], F32) for i in range(2)]
        ys = [P.sb(f"ys{i}", [128, D_MODEL], F32) for i in range(2)]
        for G in range(NG):
            gs = G % 2
            P.dma("sp", og[gs], ogT_d.rearrange("c p t -> p c t")[:, :, G * 512:(G + 1) * 512],
                  r=[("ogT_d", G)], w=[("og", gs)])
            for t in range(4):
                ti = G * 4 + t
                s2 = ti % 2
                P.dma("sp", xt[s2], x_src[ti * 128:(ti + 1) * 128, :], r=[("xres", G)], w=[("xt", s2)])
                for jg in range(4):
                    b = (ti * 4 + jg) % 4
                    for c in range(4):
                        P.op("pe", lambda e, b=b, c=c, jg=jg, t=t, gs=gs: e.matmul(
                            banks[b], og[gs][:, c, t * 128:(t + 1) * 128], wob[:, c, jg * 512:(jg + 1) * 512],
                            start=(c == 0), stop=(c == 3)),
                            r=[("og", gs), ("wob", c)], w=[("bank", b)])
                    P.op("dve", lambda e, b=b, jg=jg, s2=s2: e.scalar_tensor_tensor(
                        ys[s2][:, jg * 512:(jg + 1) * 512], xt[s2][:, jg * 512:(jg + 1) * 512], 0.25, banks[b],
                        ALU.mult, ALU.add),
                        r=[("bank", b), ("xt", s2)], w=[("ys", s2)])
                P.dma("sp", ypart[ti * 128:(ti + 1) * 128, :], ys[s2], r=[("ys", s2)], w=[("ypart", G)])
            rows = slice(G * 512, (G + 1) * 512)
            P.rec.add("pool", lambda e, rows=rows: e.collective_compute(
                "AllReduce", ALU.add, replica_groups=GROUPS, ins=[ypart[rows, :]], outs=[xres[rows, :]]),
                r=[("ypart", G)], w=[("xres", G)], kind="cc")
            if last:
                P.dma("sp", y_out[rows, :], xres[rows, :], r=[("xres", G)], w=[("yout", G)])

    def odd_layer(li, oi, x_src, last):
        rec = P.rec
        sb_reset()
        wbf = P.sb("wbf", [128, KC, 2048], BF16)
        wst = [P.sb(f"wst{i}", [128, 2048], F32) for i in range(2)]
        lnT = P.sb("lnT", [128, KC], F32)
        xin = [P.sb(f"xin{i}", [128, D_MODEL], F32) for i in range(4)]
        junk = P.sb("junk", [128, D_MODEL], BF16)
        ss = [P.sb(f"ss{i}", [128, 4], F32) for i in range(2)]
        rstd = [P.sb(f"rstd{i}", [128, 4], F32) for i in range(2)]
        xn = [P.sb(f"xn{i}", [128, D_MODEL], BF16) for i in range(2)]
        hT = [P.sb(f"hT{i}", [128, KC, 512], BF16) for i in range(2)]
        stq = [P.sb(f"stq{i}", [128, 4, 512], BF16) for i in range(2)]
        stk = [P.sb(f"stk{i}", [128, 4, 512], BF16) for i in range(2)]
        stg = [P.sb(f"stg{i}", [128, 4, 512], BF16) for i in range(2)]
        stv = [P.sb(f"stv{i}", [128, 4, 512], BF16) for i in range(2)]
        load_weights(wio[oi], 2048, lno[oi], wbf, wst, lnT)
        scale = 128 ** -0.5
        norm_transpose_group(0, x_src, xin, junk, ss, rstd, xn, hT, 0, tpb)
        for G in range(NG):
            hs = G % 2
            if G + 1 < NG:
                norm_transpose_group(G + 1, x_src, xin, junk, ss, rstd, xn, hT, (G + 1) % 2, tpb)
            n_acc = 0
            for typ in range(4):
                if typ == 2:
                    for t in range(4):
                        b = 4 + (n_acc % 4)
                        n_acc += 1
                        for kc in range(KC):
                            P.op("pe", lambda e, b=b, kc=kc, t=t, hs=hs: e.matmul(
                                banks[b], hT[hs][:, kc, t * 128:(t + 1) * 128], wbf[:, kc, 1024:1536],
                                start=(kc == 0), stop=(kc == KC - 1)),
                                r=[("hT", hs), ("wbf", kc)], w=[("bank", b)])
                        P.op("dve", lambda e, b=b, t=t, hs=hs: e.tensor_copy(stv[hs][:, t, :], banks[b]),
                             r=[("bank", b)], w=[("stv", hs)])
                    continue
                for h in range(4):
                    b = 4 + (n_acc % 4)
                    n_acc += 1
                    col = typ * 512 + h * 128
                    for kc in range(KC):
                        P.op("pe", lambda e, b=b, kc=kc, col=col, hs=hs: e.matmul(
                            banks[b], wbf[:, kc, col:col + 128], hT[hs][:, kc, :],
                            start=(kc == 0), stop=(kc == KC - 1)),
                            r=[("hT", hs), ("wbf", kc)], w=[("bank", b)])
                    if typ == 0:
                        P.op("act", lambda e, b=b, h=h, hs=hs: e.activation(stq[hs][:, h, :], banks[b], AF.Copy, scale=scale),
                             r=[("bank", b)], w=[("stq", hs)])
                    elif typ == 1:
                        P.op("dve", lambda e, b=b, h=h, hs=hs: e.tensor_copy(stk[hs][:, h, :], banks[b]),
                             r=[("bank", b)], w=[("stk", hs)])
                    else:
                        P.op("act", lambda e, b=b, h=h, hs=hs: e.activation(stg[hs][:, h, :], banks[b], AF.Silu),
                             r=[("bank", b)], w=[("stg", hs)])
            cols = slice(G * 512, (G + 1) * 512)
            P.dma("sp", qT_d.rearrange("h p t -> p h t")[:, :, cols], stq[hs], r=[("stq", hs)], w=[("qT_d", G)])
            P.dma("sp", kT_d.rearrange("h p t -> p h t")[:, :, cols], stk[hs], r=[("stk", hs)], w=[("kT_d", G)])
            P.dma("sp", sgT_d.rearrange("h p t -> p h t")[:, :, cols], stg[hs], r=[("stg", hs)], w=[("sgT_d", G)])
            P.dma("sp", v_d[G * 512:(G + 1) * 512, :].rearrange("(t p) c -> p t c", p=128), stv[hs],
                  r=[("stv", hs)], w=[("v_d", G)])
        rec.barrier()
        sb_reset()
        wst2 = [P.sb("wsto0", [128, 2048], F32)]
        wob = P.sb("wob", [128, 4, D_MODEL], BF16)
        load_wout(woo[oi], wob, wst2)
        kT = [P.sb(f"kT{i}", [128, S], BF16) for i in range(4)]
        vv = [P.sb(f"vv{i}", [128, NT, 128], BF16) for i in range(4)]
        NQ = 3
        qg = [P.sb(f"qg{i}", [128, 512], BF16) for i in range(NQ)]
        sg = [P.sb(f"sg{i}", [128, 512], BF16) for i in range(NQ)]
        e_sb = [P.sb(f"e{i}", [128, 512], F32) for i in range(2)]
        sp_sb = [P.sb(f"sp{i}", [128, 512], BF16) for i in range(3)]
        rcs = [P.sb(f"rcs{i}", [128, 512], F32) for i in range(3)]
        w_sb = [P.sb(f"w{i}", [128, 512], BF16) for i in range(2)]
        crep = [P.sb(f"crep{i}", [128, 512], F32) for i in range(2)]
        ogs = [P.sb(f"ogs{i}", [128, 4, 512], BF16) for i in range(2)]
        xt = [P.sb(f"xt{i}", [128, D_MODEL], F32) for i in range(2)]
        for h in range(4):
            P.dma("sp", kT[h], kT_d[h], w=[("kT", h)])
            P.dma("sp", vv[h], v_d.rearrange("(n p) (h d) -> p n h d", p=128, h=4)[:, :, h, :], w=[("vv", h)])
        units = []
        gi = 0
        for g in range(NG):
            for h in range(4):
                kbs = list(range(4 * g + 3, -1, -1))
                for n, kb in enumerate(kbs):
                    units.append(dict(h=h, g=g, kb=kb, first=(n == 0), last=(n == len(kbs) - 1),
                                      diag=(kb - 4 * g) if kb >= 4 * g else None, gi=gi))
                gi += 1
        NU = len(units)

        def load_group(h, g, gi):
            s = gi % NQ
            cols = slice(g * 512, (g + 1) * 512)
            P.dma("sp", qg[s], qT_d[h][:, cols], w=[("qg", s)])
            P.dma("sp", sg[s], sgT_d[h][:, cols], w=[("sg", s)])

        def out_group(g):
            os_ = g % 2
            for t in range(4):
                ti = g * 4 + t
                s2 = ti % 2
                P.dma("sp", xt[s2], x_src[ti * 128:(ti + 1) * 128, :], r=[("xres", g)], w=[("xt", s2)])
                for jg in range(4):
                    b = 4 + jg % 2
                    for c in range(4):
                        P.op("pe", lambda e, b=b, c=c, jg=jg, t=t, os_=os_: e.matmul(
                            banks[b], ogs[os_][:, c, t * 128:(t + 1) * 128], wob[:, c, jg * 512:(jg + 1) * 512],
                            start=(c == 0), stop=(c == 3)),
                            r=[("ogs", os_), ("wob", c)], w=[("bank", b)])
                    xs_ = xt[s2][:, jg * 512:(jg + 1) * 512]
                    P.op("dve", lambda e, b=b, xs_=xs_: e.scalar_tensor_tensor(xs_, xs_, 0.25, banks[b], ALU.mult, ALU.add),
                         r=[("bank", b), ("xt", s2)], w=[("xt", s2)])
                P.dma("sp", ypart[ti * 128:(ti + 1) * 128, :], xt[s2], r=[("xt", s2)], w=[("ypart", g)])
            rows = slice(g * 512, (g + 1) * 512)
            P.rec.add("pool", lambda e, rows=rows: e.collective_compute(
                "AllReduce", ALU.add, replica_groups=GROUPS, ins=[ypart[rows, :]], outs=[xres[rows, :]]),
                r=[("ypart", g)], w=[("xres", g)], kind="cc")
            if last:
                P.dma("sp", y_out[rows, :], xres[rows, :], r=[("xres", g)], w=[("yout", g)])

        load_group(0, 0, 0)
        for step in range(-1, NU + 2):
            sa_ = step + 1
            if 0 <= sa_ < NU:
                u = units[sa_]
                h, g, kb, gi_ = u["h"], u["g"], u["kb"], u["gi"]
                qs = gi_ % NQ
                if u["first"]:
                    if sa_ + (4 * g + 4) < NU:
                        un = units[sa_ + 4 * g + 4]
                        load_group(un["h"], un["g"], un["gi"])
                zb = sa_ % 2
                kblk = kT[h][:, kb * 128:(kb + 1) * 128]
                dg = u["diag"]
                P.op("pe", lambda e, zb=zb, kblk=kblk, qs=qs, dg=dg: e.matmul(
                    banks[zb], kblk, qg[qs], start=True, stop=(dg is None)),
                    r=[("kT", h), ("qg", qs)], w=[("bank", zb)])
                if dg is not None:
                    P.op("pe", lambda e, zb=zb, dg=dg: e.matmul(banks[zb], ident, negmask[:, dg, :], start=False, stop=True),
                         r=CONST_KEYS, w=[("bank", zb)])
            if 0 <= step < NU:
                zb = step % 2
                es, ss_ = step % 2, step % 3
                P.op("act", lambda e, zb=zb, es=es: e.activation(e_sb[es], banks[zb], AF.Exp),
                     r=[("bank", zb)], w=[("e", es)])
                P.op("act", lambda e, es=es, ss_=ss_: e.activation(sp_sb[ss_], e_sb[es], AF.Ln, bias=1.0),
                     r=[("e", es)], w=[("sp", ss_)])
            if 0 <= step - 1 < NU:
                s1 = step - 1
                u = units[s1]
                h, g, kb, gi_ = u["h"], u["g"], u["kb"], u["gi"]
                qs = gi_ % NQ
                ss_ = s1 % 3
                rb, ab = 2 + s1 % 2, 4 + s1 % 2
                kblk = kT[h][:, kb * 128:(kb + 1) * 128]
                dg = u["diag"]
                P.op("pe", lambda e, rb=rb, ss_=ss_: e.matmul(banks[rb], trineg, sp_sb[ss_], start=True, stop=False),
                     r=[("sp", ss_), ("const", 1)], w=[("bank", rb)])
                P.op("pe", lambda e, rb=rb, kblk=kblk, qs=qs, dg=dg: e.matmul(
                    banks[rb], kblk, qg[qs], start=False, stop=(dg is None)),
                    r=[("kT", h), ("qg", qs)], w=[("bank", rb)])
                if dg is not None:
                    P.op("pe", lambda e, rb=rb, dg=dg: e.matmul(banks[rb], ident, negmask[:, dg, :], start=False, stop=True),
                         r=CONST_KEYS, w=[("bank", rb)])
                cs_old, cs_new = s1 % 2, (s1 + 1) % 2
                rs = s1 % 3
                if not u["last"]:
                    P.op("pe", lambda e, ab=ab, ss_=ss_: e.matmul(banks[ab], onesneg, sp_sb[ss_], start=True, stop=True),
                         r=[("sp", ss_), ("const", 2)], w=[("bank", ab)])
                if u["first"]:
                    P.op("dve", lambda e, rs=rs, rb=rb: e.tensor_copy(rcs[rs], banks[rb]),
                         r=[("bank", rb)], w=[("rcs", rs)])
                    if not u["last"]:
                        P.op("dve", lambda e, cs_new=cs_new, ab=ab: e.tensor_copy(crep[cs_new], banks[ab]),
                             r=[("bank", ab)], w=[("crep", cs_new)])
                else:
                    P.op("dve", lambda e, rs=rs, rb=rb, cs_old=cs_old: e.tensor_tensor(rcs[rs], banks[rb], crep[cs_old], ALU.add),
                         r=[("bank", rb), ("crep", cs_old)], w=[("rcs", rs)])
                    if not u["last"]:
                        P.op("dve", lambda e, cs_new=cs_new, cs_old=cs_old, ab=ab: e.tensor_tensor(
                            crep[cs_new], banks[ab], crep[cs_old], ALU.add),
                            r=[("bank", ab), ("crep", cs_old)], w=[("crep", cs_new)])
            if 0 <= step - 2 < NU:
                s2 = step - 2
                u = units[s2]
                h, g, kb, gi_ = u["h"], u["g"], u["kb"], u["gi"]
                qs = gi_ % NQ
                rs, ws = s2 % 3, s2 % 2
                ob = 6 + gi_ % 2
                P.op("act", lambda e, rs=rs, ws=ws: e.activation(w_sb[ws], rcs[rs], AF.Exp),
                     r=[("rcs", rs)], w=[("w", ws)])
                P.op("pe", lambda e, ob=ob, h=h, kb=kb, ws=ws, u=u: e.matmul(
                    banks[ob], vv[h][:, kb, :], w_sb[ws], start=u["first"], stop=u["last"]),
                    r=[("vv", h), ("w", ws)], w=[("bank", ob)])
                if u["last"]:
                    os_ = g % 2
                    P.op("dve", lambda e, ob=ob, qs=qs, os_=os_, h=h: e.tensor_tensor(ogs[os_][:, h, :], banks[ob], sg[qs], ALU.mult),
                         r=[("bank", ob), ("sg", qs)], w=[("ogs", os_)])
                    if h == 3:
                        out_group(g)
        rec.barrier()

    def even_layer(li, ei, x_src, last):
        rec = P.rec
        sb_reset()
        NCOL = 1664
        wbf = P.sb("wbf", [128, KC, NCOL], BF16)
        lnT = P.sb("lnT", [128, KC], F32)
        xin = [P.sb(f"xin{i}", [128, D_MODEL], F32) for i in range(2)]
        junk = P.sb("junk", [128, D_MODEL], BF16)
        ss = [P.sb(f"ss{i}", [128, 4], F32) for i in range(2)]
        rstd = [P.sb(f"rstd{i}", [128, 4], F32) for i in range(2)]
        xn = [P.sb(f"xn{i}", [128, D_MODEL], BF16) for i in range(2)]
        hT = [P.sb(f"hT{i}", [128, KC, 512], BF16) for i in range(2)]
        wqk = P.sb("wqk", [128, 320], F32)
        esink = P.sb("esink", [128, 4], F32)
        lbv = P.sb("lbv", [128, 4], F32)
        lb = P.sb("lb", [128, 2], F32)
        oml = P.sb("oml", [128, 2], F32)
        gn = P.sb("gn", [128, 1], F32)
        swam = P.sb("swam", [128, 2, 128], BF16)
        m16 = P.sb("m16", [128, 128], BF16)
        bmk = P.sb("bmk", [128, 8], BF16)
        onesf = P.sb("onesf", [128, 512], F32)
        ropeg = [P.sb(f"ropeg{i}", [128, 4, 64], F32) for i in range(2)]
        P.dma("sp", wqk, qkn[ei].partition_broadcast(128).rearrange("p a b -> p (a b)"), w=["wqk"])
        P.dma("sp", esink, snk[ei].partition_broadcast(128).rearrange("p a b -> p (a b)"), w=["esink"])
        P.dma("sp", lbv, lbt, w=["lbv"])
        P.dma("sp", gn, gnt[ei], w=["gn"])
        P.dma("sp", swam[:, 0, :], emask[0][:, 0:128], w=["swam"])
        P.dma("sp", swam[:, 1, :], emask[1][:, 0:128], w=["swam"])
        P.dma("sp", m16, emask[2][:, 0:128], w=["m16"])
        P.dma("sp", bmk, bmask, w=["bmk"])
        P.op("pool", lambda e: e.memset(onesf, 1.0), w=["onesf"])
        P.op("act", lambda e: e.activation(esink, esink, AF.Exp), r=["esink"], w=["esink"])
        if ei == 0:
            P.op("dve", lambda e: e.memset(lb, 0.0), w=["lb"])
            P.op("dve", lambda e: e.memset(oml, 1.0), w=["oml"])
        else:
            P.op("dve", lambda e: e.tensor_tensor(lb, lbv[:, 0:2], lbv[:, 2:4], ALU.subtract), r=["lbv"], w=["lb"])
            P.op("act", lambda e: e.activation(lb, lb, AF.Exp), r=["lb"], w=["lb"])
            P.op("dve", lambda e: e.tensor_scalar(lb, lb, 1.0, None, ALU.add), r=["lb"], w=["lb"])
            P.op("dve", lambda e: e.reciprocal(lb, lb), r=["lb"], w=["lb"])
            P.op("dve", lambda e: e.tensor_scalar(oml, lb, -1.0, 1.0, ALU.mult, ALU.add), r=["lb"], w=["oml"])
        mix_start = P.sb_off
        wst = [P.sb(f"wst{i}", [128, 2048], F32) for i in range(2)]
        load_weights(wie[ei], NCOL, lne[ei], wbf, wst, lnT)
        rec.barrier()
        P.sb_off = mix_start
        sga = [P.sb(f"sga{i}", [128, 256], F32) for i in range(4)]
        sq = P.sb("sq", [128, 320], F32)
        s5 = P.sb("s5", [128, 5], F32)
        qk = P.sb("qk", [128, 5, 64], F32)
        rt = [P.sb(f"rt{i}", [128, 5, 32], F32) for i in range(4)]
        qkr = P.sb("qkr", [128, 6, 64], BF16)
        qkT = [P.sb(f"qkT{i}", [128, 3, 128], BF16) for i in range(3)]
        vaug = [P.sb(f"vaug{i}", [128, 66], BF16) for i in range(3)]
        qz = [P.sb(f"qz{i}", [128, 4, 128], BF16) for i in range(3)]
        ex = [P.sb(f"ex{i}", [128, 2, 2, 128], F32) for i in range(2)]
        pm = [P.sb(f"pm{i}", [128, 2, 2, 128], BF16) for i in range(2)]
        den = P.sb("den", [128, 4], F32)
        ya1 = P.sb("ya1", [128, 4, 64], F32)
        ya = P.sb("ya", [128, 256], BF16)
        yaT = [P.sb("yaT0", [128, 2, 512], BF16)] * 2
        qs = [P.sb(f"qs{i}", [128, 512], F32) for i in range(2)]
        sgb = [P.sb(f"sgb{i}", [128, 512], F32) for i in range(2)]
        ef = [P.sb(f"ef{i}", [128, 512], F32) for i in range(2)]
        fbuf = P.sb("fbuf", [128, 512], F32)
        logf = P.sb("logf", [128, 512], F32)
        kk = P.sb("kk", [128, 512], F32)
        Gb = [[P.sb(f"Gb{h}", [128, 516], F32)] * 2 for h in range(2)]
        gcar = [P.sb(f"gcar{h}", [128, 1], F32) for h in range(2)]
        aa = P.sb("aa", [128, 512], F32)
        ea = P.sb("ea", [128, 512], F32)
        qt = [P.sb(f"qt{i}", [128, 512], BF16) for i in range(2)]
        kt = [P.sb(f"kt{i}", [128, 512], BF16) for i in range(2)]
        kh = [P.sb(f"kh{i}", [128, 512], BF16) for i in range(2)]
        dec = [P.sb(f"dec{i}", [128, 32], F32) for i in range(2)]
        vt = [P.sb(f"vt{i}", [128, 256], BF16) for i in range(4)]
        attT = [P.sb(f"attT{i}", [128, 128], BF16) for i in range(2)]
        kblk = [P.sb(f"kblk{i}", [128, 8, 128], BF16) for i in range(2)]
        Sf = [[P.sb(f"Sf{h}{i}", [128, 128], F32) for i in range(2)] for h in range(2)]
        NSB = 4
        Sb = [[P.sb(f"Sb{h}{i}", [128, 128], BF16) for i in range(NSB)] for h in range(2)]
        osq = P.sb("osq", [128, 128], BF16)
        rs = P.sb("rs", [128, 128], F32)
        otmp = P.sb("otmp", [128, 128], F32)
        ybT = [P.sb("ybT0", [128, 2, 512], BF16)] * 2
        for i in range(3):
            P.op("pool", lambda e, i=i: e.memset(vaug[i], 1.0), w=[("vaug", i)])
            P.op("pool", lambda e, i=i: e.memset(qz[i], 0.0), w=[("qkT", i)])
        for h in range(2):
            P.op("pool", lambda e, h=h: e.memset(Sf[h][0], 0.0), w=[("Sf", h, 0)])
            P.op("pool", lambda e, h=h: e.memset(Sb[h][0], 0.0), w=[("Sb", h, 0)])
            P.op("pool", lambda e, h=h: e.memset(gcar[h], 0.0), w=[("gcar", h)])
        ACC = [2, 3]
        sc_v = [pview(4, 0, 512).rearrange("p (a b c) -> p a b c", a=2, b=2), pview(5, 0, 512).rearrange("p (a b c) -> p a b c", a=2, b=2)]
        av_ps = pview(6, 0, 264, inner=66)
        qkT_ps = pview(6, 264, 192, BF16, inner=128)
        att_ps = pview(7, 0, 128)
        khT_ps = pview(7, 128, 64, BF16)
        o_ps = pview(7, 192, 128)
        yT_ps = pview(7, 320, 128, BF16, inner=128)
        ssq_ps = pview(7, 384, 128)
        sc_v = [pview(4, 0, 512).rearrange("p (a b c) -> p a b c", a=2, b=2)]
        U_ps = pview(5, 0, 512, inner=128)
        sstate = {"sb": [0, 0], "sf": [0, 0], "nacc": 0}

        def acc_bank():
            b = ACC[sstate["nacc"] % 2]
            sstate["nacc"] += 1
            return b

        def project_fm(col, hs):
            b = acc_bank()
            for kc in range(KC):
                P.op("pe", lambda e, b=b, kc=kc, col=col, hs=hs: e.matmul(
                    banks[b], wbf[:, kc, col:col + 128], hT[hs][:, kc, :], start=(kc == 0), stop=(kc == KC - 1)),
                    r=[("hT", hs), ("wbf", kc)], w=[("bank", b)])
            return b

        def project_tm(c0, c1, hs, t):
            b = acc_bank()
            for kc in range(KC):
                P.op("pe", lambda e, b=b, kc=kc, hs=hs, t=t: e.matmul(
                    banks[b][:, 0:c1 - c0], hT[hs][:, kc, t * 128:(t + 1) * 128], wbf[:, kc, c0:c1],
                    start=(kc == 0), stop=(kc == KC - 1)),
                    r=[("hT", hs), ("wbf", kc)], w=[("bank", b)])
            return b

        def swa_tile(G, t, hs):
            ti = G * 4 + t
            gp = G % 2
            cur, prv = ti % 3, (ti - 1) % 3
            b = project_tm(0, 384, hs, t)
            pa = banks[b]
            P.op("act", lambda e: e.activation(sq, pa[:, 0:320], AF.Square), r=[("bank", b)], w=["sq"])
            P.op("dve", lambda e: e.tensor_reduce(s5, sq.rearrange("p (a b) -> p a b", b=64), AX.X, ALU.add), r=["sq"], w=["s5"])
            P.op("act", lambda e: e.activation(s5, s5, AF.Ln, bias=EPS, scale=1.0 / 64), r=["s5"], w=["s5"])
            P.op("act", lambda e: e.activation(s5, s5, AF.Exp, scale=-0.5), r=["s5"], w=["s5"])
            P.op("dve", lambda e: e.tensor_tensor(qk, pa[:, 0:320].rearrange("p (a b) -> p a b", b=64),
                                                  s5.unsqueeze(2).to_broadcast([128, 5, 64]), ALU.mult),
                 r=[("bank", b), "s5"], w=["qk"])
            P.op("dve", lambda e, cur=cur: e.tensor_copy(vaug[cur][:, 0:64], pa[:, 320:384]), r=[("bank", b)], w=[("vaug", cur)])
            P.op("dve", lambda e: e.tensor_tensor(qk, qk, wqk.rearrange("p (a b) -> p a b", b=64), ALU.mult),
                 r=["qk", "wqk"], w=["qk"])
            cs = ropeg[gp][:, t, :]
            cosb = cs[:, 0:32].unsqueeze(1).to_broadcast([128, 5, 32])
            sinb = cs[:, 32:64].unsqueeze(1).to_broadcast([128, 5, 32])
            x1, x2 = qk[:, :, 0:32], qk[:, :, 32:64]
            P.op("dve", lambda e: e.tensor_tensor(rt[0], x1, cosb, ALU.mult), r=["qk", ("ropeg", gp)], w=[("rt", 0)])
            P.op("pool", lambda e: e.tensor_tensor(rt[1], x2, sinb, ALU.mult), r=["qk", ("ropeg", gp)], w=[("rt", 1)])
            P.op("dve", lambda e: e.tensor_tensor(rt[2], x2, cosb, ALU.mult), r=["qk", ("ropeg", gp)], w=[("rt", 2)])
            P.op("pool", lambda e: e.tensor_tensor(rt[3], x1, sinb, ALU.mult), r=["qk", ("ropeg", gp)], w=[("rt", 3)])
            P.op("dve", lambda e: e.tensor_tensor(qkr[:, 0:5, 0:32], rt[0], rt[1], ALU.subtract),
                 r=[("rt", 0), ("rt", 1)], w=["qkr"])
            P.op("dve", lambda e: e.tensor_tensor(qkr[:, 0:5, 32:64], rt[2], rt[3], ALU.add),
                 r=[("rt", 2), ("rt", 3)], w=["qkr"])
            P.op("dve", lambda e: e.tensor_copy(qkr[:, 5, :], qkr[:, 4, :]), r=["qkr"], w=["qkr"])
            for i in range(3):
                P.op("pe", lambda e, i=i: e.transpose(qkT_ps[:, i, :], qkr[:, 2 * i:2 * i + 2, :].rearrange("p a b -> p (a b)"), ident),
                     r=["qkr", ("const", 0)], w=[("bank", 6)])
            P.op("act", lambda e, cur=cur: e.activation(qkT[cur], qkT_ps, AF.Copy), r=[("bank", 6)], w=[("qkT", cur)])
            qz4 = qz[cur].rearrange("p (a b) t -> p a b t", b=2)
            P.op("dve", lambda e, qz4=qz4: e.tensor_copy(qz4[0:64, :, 0, :], qkT_ps[0:64, 0:2, :]), r=[("bank", 6)], w=[("qkT", cur)])
            P.op("dve", lambda e, qz4=qz4: e.tensor_copy(qz4[64:128, :, 1, :], qkT_ps[64:128, 0:2, :]), r=[("bank", 6)], w=[("qkT", cur)])
            import os
            lvl = int(os.environ.get("DBG_SWA", "9"))
            if lvl < 1:
                return
            has_prev = ti > 0
            nkb = 2 if has_prev else 1
            for hp in range(2):
                sc = sc_v[0]
                xs = hp
                for kb in range(nkb):
                    src = qkT[cur] if kb == 0 else qkT[prv]
                    P.op("pe", lambda e, sc=sc, kb=kb, src=src, hp=hp, cur=cur: e.matmul(
                        sc[:, kb, :, :], src[:, 2, :], qz[cur][:, 2 * hp:2 * hp + 2, :], start=True, stop=True),
                        r=[("qkT", cur), ("qkT", prv)], w=[("bank", 4)])
                if lvl < 2:
                    continue
                P.op("act", lambda e, sc=sc, xs=xs, nkb=nkb: e.activation(ex[xs][:, 0:nkb], sc[:, 0:nkb], AF.Exp, scale=0.125),
                     r=[("bank", 4)], w=[("ex", xs)])
                P.op("dve", lambda e, xs=xs, nkb=nkb: e.tensor_tensor(
                    pm[xs][:, 0:nkb], ex[xs][:, 0:nkb], swam[:, 0:nkb].unsqueeze(2).to_broadcast([128, nkb, 2, 128]), ALU.mult),
                    r=[("ex", xs), "swam"], w=[("pm", xs)])
                if lvl < 3:
                    continue
                for hi in range(2):
                    hh = 2 * hp + hi
                    for kb in range(nkb):
                        vsrc = vaug[cur] if kb == 0 else vaug[prv]
                        P.op("pe", lambda e, hh=hh, xs=xs, kb=kb, hi=hi, vsrc=vsrc, nkb=nkb: e.matmul(
                            av_ps[:, hh, 0:65], pm[xs][:, kb, hi, :], vsrc[:, 0:65], start=(kb == 0), stop=(kb == nkb - 1)),
                            r=[("pm", xs), ("vaug", cur), ("vaug", prv)], w=[("bank", 6)])
            if lvl < 4:
                return
            P.op("dve", lambda e: e.tensor_tensor(den, av_ps[:, :, 64], esink, ALU.add), r=[("bank", 6), "esink"], w=["den"])
            P.op("dve", lambda e: e.reciprocal(den, den), r=["den"], w=["den"])
            P.op("dve", lambda e: e.tensor_tensor(ya1, av_ps[:, :, 0:64], den.unsqueeze(2).to_broadcast([128, 4, 64]), ALU.mult),
                 r=[("bank", 6), "den"], w=["ya1"])
            P.op("pool", lambda e, t=t: e.tensor_tensor(ya, ya1.rearrange("p a b -> p (a b)"), sga[t], ALU.mult),
                 r=["ya1", ("sga", t)], w=["ya"])
            yb_ = acc_bank()
            yTp = pview(yb_, 0, 128, BF16, inner=128)
            for i in range(2):
                P.op("pe", lambda e, i=i, yTp=yTp: e.transpose(yTp[:, i, :], ya[:, i * 128:(i + 1) * 128], ident),
                     r=["ya", ("const", 0)], w=[("bank", yb_)])
            P.op("act", lambda e, gp=gp, t=t, yTp=yTp: e.activation(yaT[gp][:, :, t * 128:(t + 1) * 128], yTp, AF.Copy),
                 r=[("bank", yb_)], w=[("yaT", 0)])

        def hgrn_group_prep(G, hh):
            gp = G % 2
            Gc, Gp = Gb[hh][gp], Gb[hh][1 - gp]
            P.op("dve", lambda e: e.tensor_scalar(fbuf, ef[hh], 1.0, None, ALU.add), r=[("ef", hh)], w=["fbuf"])
            P.op("dve", lambda e: e.reciprocal(fbuf, fbuf), r=["fbuf"], w=["fbuf"])
            P.op("dve", lambda e: e.tensor_scalar(fbuf, fbuf, oml[:, hh:hh + 1], lb[:, hh:hh + 1], ALU.mult, ALU.add),
                 r=["fbuf", "oml", "lb"], w=["fbuf"])
            P.op("act", lambda e: e.activation(logf, fbuf, AF.Ln), r=["fbuf"], w=["logf"])
            P.op("pool", lambda e: e.tensor_scalar(kk, fbuf, -1.0, 1.0, ALU.mult, ALU.add), r=["fbuf"], w=["kk"])
            P.op("dve", lambda e: e.tensor_copy(Gc[:, 0:1], gcar[hh]), r=[("gcar", hh)], w=[("Gb", hh)])
            P.op("dve", lambda e: e.tensor_tensor_scan(Gc[:, 1:513], onesf, logf, Gc[:, 0:1], ALU.mult, ALU.add),
                 r=["onesf", "logf", ("Gb", hh)], w=[("Gb", hh)])
            P.op("dve", lambda e: e.tensor_copy(gcar[hh], Gc[:, 512:513]), r=[("Gb", hh)], w=[("gcar", hh)])
            Gi = Gc[:, 1:513].rearrange("p (c i) -> p c i", i=16)
            Rc = Gc[:, 0:512].rearrange("p (c i) -> p c i", i=16)[:, :, 0:1]
            Ec = Gc[:, 1:513].rearrange("p (c i) -> p c i", i=16)[:, :, 15:16]
            a3 = aa.rearrange("p (c i) -> p c i", i=16)
            P.op("dve", lambda e: e.tensor_tensor(a3, Gi, Rc.to_broadcast([128, 32, 16]), ALU.subtract),
                 r=[("Gb", hh)], w=["aa"])
            P.op("act", lambda e: e.activation(ea, aa, AF.Exp), r=["aa"], w=["ea"])
            P.op("dve", lambda e: e.tensor_tensor(qt[hh], qs[hh], ea, ALU.mult), r=[("qs", hh), "ea"], w=[("qt", hh)])
            P.op("pool", lambda e: e.tensor_scalar(aa, aa, -1.0, 80.0, ALU.mult, ALU.min), r=["aa"], w=["aa"])
            P.op("act", lambda e: e.activation(ea, aa, AF.Exp), r=["aa"], w=["ea"])
            P.op("dve", lambda e: e.tensor_tensor(kt[hh], kk, ea, ALU.mult), r=["kk", "ea"], w=[("kt", hh)])
            P.op("dve", lambda e: e.tensor_tensor(a3, Ec.to_broadcast([128, 32, 16]), Gi, ALU.subtract),
                 r=[("Gb", hh)], w=["aa"])
            P.op("act", lambda e: e.activation(ea, aa, AF.Exp), r=["aa"], w=["ea"])
            P.op("dve", lambda e: e.tensor_tensor(kh[hh], kk, ea, ALU.mult), r=["kk", "ea"], w=[("kh", hh)])
            P.op("dve", lambda e: e.tensor_tensor(dec[hh], Ec.rearrange("p c i -> p (c i)"), Rc.rearrange("p c i -> p (c i)"), ALU.subtract),
                 r=[("Gb", hh)], w=[("dec", hh)])
            P.op("act", lambda e: e.activation(dec[hh], dec[hh], AF.Exp), r=[("dec", hh)], w=[("dec", hh)])

        def hgrn_tile(G, t, hh):
            gp = G % 2
            cols = slice(t * 128, (t + 1) * 128)
            vth = vt[t][:, hh * 128:(hh + 1) * 128]
            sa = (t * 2 + hh) % 2
            P.op("pe", lambda e: e.matmul(att_ps, kt[hh][:, cols], qt[hh][:, cols], start=True, stop=True),
                 r=[("kt", hh), ("qt", hh)], w=[("bank", 7)])
            P.op("dve", lambda e: e.tensor_tensor(attT[sa], att_ps, m16, ALU.mult), r=[("bank", 7), "m16"], w=[("attT", sa)])
            P.op("pe", lambda e: e.transpose(khT_ps, kh[hh][:, cols], ident), r=[("kh", hh), ("const", 0)], w=[("bank", 7)])
            P.op("dve", lambda e: e.tensor_tensor(
                kblk[sa], khT_ps.unsqueeze(1).to_broadcast([128, 8, 128]), bmk.unsqueeze(2).to_broadcast([128, 8, 128]), ALU.mult),
                r=[("bank", 7), "bmk"], w=[("kblk", sa)])
            P.op("pe", lambda e: e.matmul(o_ps, vth, attT[sa], start=True, stop=False),
                 r=[("vt", t), ("attT", sa)], w=[("bank", 7)])
            for c in range(8):
                if c % 4 == 0:
                    for c2 in range(c, c + 4):
                        P.op("pe", lambda e, c2=c2: e.matmul(U_ps[:, c2 % 4, :], kblk[sa][:, c2, :], vth, start=True, stop=True),
                             r=[("kblk", sa), ("vt", t)], w=[("bank", 5)])
                sbi = sstate["sb"][hh]
                sfi = sstate["sf"][hh]
                ccol = t * 128 + 16 * c
                P.op("pe", lambda e, sbi=sbi, ccol=ccol, c=c: e.matmul(
                    o_ps[:, 16 * c:16 * c + 16], Sb[hh][sbi], qt[hh][:, ccol:ccol + 16], start=False, stop=(c == 7)),
                    r=[("Sb", hh, sbi), ("qt", hh)], w=[("bank", 7)])
                nsf, nsb = 1 - sfi, (sbi + 1) % NSB
                dcol = dec[hh][:, 8 * t + c:8 * t + c + 1]
                P.op("dve", lambda e, sfi=sfi, nsf=nsf, dcol=dcol, c=c: e.scalar_tensor_tensor(
                    Sf[hh][nsf], Sf[hh][sfi], dcol, U_ps[:, c % 4, :], ALU.mult, ALU.add),
                    r=[("Sf", hh, sfi), ("dec", hh), ("bank", 5)], w=[("Sf", hh, nsf)])
                P.op("pool", lambda e, nsf=nsf, nsb=nsb: e.tensor_copy(Sb[hh][nsb], Sf[hh][nsf]),
                     r=[("Sf", hh, nsf)], w=[("Sb", hh, nsb)])
                sstate["sb"][hh] = nsb
                sstate["sf"][hh] = nsf
            P.op("act", lambda e: e.activation(osq, o_ps, AF.Square), r=[("bank", 7)], w=["osq"])
            P.op("pe", lambda e: e.matmul(ssq_ps, ones, osq, start=True, stop=True), r=["osq", ("const", 3)], w=[("bank", 7)])
            P.op("act", lambda e: e.activation(rs, ssq_ps, AF.Ln, bias=EPS, scale=1.0 / 128), r=[("bank", 7)], w=["rs"])
            P.op("act", lambda e: e.activation(rs, rs, AF.Exp, scale=-0.5), r=["rs"], w=["rs"])
            P.op("dve", lambda e: e.tensor_tensor(otmp, o_ps, rs, ALU.mult), r=[("bank", 7), "rs"], w=["otmp"])
            P.op("dve", lambda e: e.scalar_tensor_tensor(ybT[gp][:, hh, cols], otmp, gn[:, 0:1], sgb[hh][:, cols], ALU.mult, ALU.mult),
                 r=["otmp", "gn", ("sgb", hh)], w=[("ybT", 0)])

        def group_body(G):
            hs = G % 2
            gp = G % 2
            cols = slice(G * 512, (G + 1) * 512)
            P.dma("sp", ropeg[gp], rope[G * 512:(G + 1) * 512, :].rearrange("(t p) c -> p t c", p=128), w=[("ropeg", gp)])
            for hh in range(2):
                b = project_fm(640 + hh * 128, hs)
                P.op("act", lambda e, b=b, hh=hh: e.activation(qs[hh], banks[b], AF.Silu), r=[("bank", b)], w=[("qs", hh)])
            for hh in range(2):
                b = project_fm(1408 + hh * 128, hs)
                P.op("act", lambda e, b=b, hh=hh: e.activation(sgb[hh], banks[b], AF.Silu), r=[("bank", b)], w=[("sgb", hh)])
            for t in range(4):
                b = project_tm(384, 640, hs, t)
                P.op("act", lambda e, b=b, t=t: e.activation(sga[t], banks[b][:, 0:256], AF.Silu), r=[("bank", b)], w=[("sga", t)])
            for hh in range(2):
                b = project_fm(896 + hh * 128, hs)
                P.op("act", lambda e, b=b, hh=hh: e.activation(ef[hh], banks[b], AF.Exp, scale=-1.0), r=[("bank", b)], w=[("ef", hh)])
            for t in range(4):
                b = project_tm(1152, 1408, hs, t)
                P.op("dve", lambda e, b=b, t=t: e.tensor_copy(vt[t], banks[b][:, 0:256]), r=[("bank", b)], w=[("vt", t)])
            import os
            dbg = int(os.environ.get("DBG_EVEN", "7"))
            for hh in range(2):
                if dbg & 2:
                    hgrn_group_prep(G, hh)
            for t in range(4):
                P.cap = []
                if dbg & 1:
                    swa_tile(G, t, hs)
                ca = P.cap
                P.cap = []
                for hh in range(2):
                    if dbg & 4:
                        hgrn_tile(G, t, hh)
                cb = P.cap
                P.cap = None
                P.merge(ca, cb)
            P.dma("sp", ogT_d[0:2].rearrange("c p t -> p c t")[:, :, cols], yaT[gp], r=[("yaT", 0)], w=[("ogT_d", G)])
            P.dma("sp", ogT_d[2:4].rearrange("c p t -> p c t")[:, :, cols], ybT[gp], r=[("ybT", 0)], w=[("ogT_d", G)])

        norm_transpose_group(0, x_src, xin, junk, ss, rstd, xn, hT, 0, tph, even=True)
        for G in range(NG):
            if G + 1 < NG:
                norm_transpose_group(G + 1, x_src, xin, junk, ss, rstd, xn, hT, (G + 1) % 2, tph, even=True)
            group_body(G)
        rec.barrier()
        sb_reset()
        wst2 = [P.sb(f"wste{i}", [128, 2048], F32) for i in range(2)]
        wob = P.sb("wob", [128, 4, D_MODEL], BF16)
        load_wout(woe[ei], wob, wst2)
        out_phase(x_src, wob, last)
        rec.barrier()

    cur = x_in
    no = ne = 0
    for li, typ in enumerate(layers):
        last = li == len(layers) - 1
        if typ == "o":
            odd_layer(li, no, cur, last)
            no += 1
        else:
            even_layer(li, ne, cur, last)
            ne += 1
        cur = xres
    P.rec.barrier()
    P.rec.emit(nc)
    return P


def _bf(a):
    return np.asarray(a, dtype=np.float32).astype(ml_dtypes.bfloat16)


def make_consts():
    j = np.arange(128)[:, None]
    s = np.arange(128)[None, :]
    ident = (j == s).astype(np.float32)
    trineg = -(j >= s).astype(np.float32)
    onesneg = -np.ones((128, 128), np.float32)
    ones = np.ones((128, 128), np.float32)
    cmat = _bf(np.stack([ident, trineg, onesneg, ones]))
    tri = np.where(j >= s, NEG, 0.0).astype(np.float32)
    cm = np.zeros((4, 128, 512), np.float32)
    for d in range(4):
        for i in range(4):
            blk = cm[d][:, i * 128:(i + 1) * 128]
            if i < d:
                blk[:] = NEG
            elif i == d:
                blk[:] = tri
    return cmat, _bf(cm)


def make_in_maps(inputs, S, layers):
    x = np.asarray(inputs["x"], np.float32)
    cmat, cmask = make_consts()
    half = 32
    inv_freq = (1.0 / (np.float32(10000.0) ** (np.arange(half, dtype=np.float32) / np.float32(half)))).astype(np.float32)
    ang = (np.arange(S, dtype=np.float32)[:, None] * inv_freq[None, :]).astype(np.float32)
    rope_tab = np.ascontiguousarray(np.concatenate([np.cos(ang), np.sin(ang)], axis=1).astype(np.float32))
    kk_ = np.arange(128)[:, None]
    qq_ = np.arange(128)[None, :]
    em = np.zeros((3, 128, 512), np.float32)
    em[0][:, :128] = (kk_ <= qq_)
    em[1][:, :128] = (kk_ > qq_)
    em[2][:, :128] = ((kk_ // 16) == (qq_ // 16)) & (kk_ <= qq_)
    emask = _bf(em)
    bmask = _bf((np.arange(128)[:, None] // 16) == np.arange(8)[None, :])
    maps = []
    for c in range(8):
        b, r = c // 4, c % 4
        m = {"x": np.ascontiguousarray(x[b, :S]), "cmat": cmat, "cmask": cmask}
        no = ne = 0
        for typ in layers:
            if typ == "o":
                w = np.asarray(inputs["w_in_odd"][no], np.float32)
                cols = []
                for part in range(4):
                    cols.append(w[:, part * 2048 + r * 512: part * 2048 + (r + 1) * 512])
                m[f"wio{no}"] = np.ascontiguousarray(np.concatenate(cols, axis=1))
                m[f"lno{no}"] = np.ascontiguousarray(np.asarray(inputs["ln_odd"][no], np.float32).reshape(KC, 128).T)
                m[f"woo{no}"] = np.ascontiguousarray(np.asarray(inputs["w_out_odd"][no], np.float32)[r * 512:(r + 1) * 512])
                no += 1
            else:
                w = np.asarray(inputs["w_in_even"][ne], np.float32)
                kv = r // 2
                cols = [w[:, r * 256:(r + 1) * 256],
                        w[:, 1024 + kv * 64:1024 + (kv + 1) * 64],
                        w[:, 1152 + kv * 64:1152 + (kv + 1) * 64],
                        w[:, 1280 + r * 256:1280 + (r + 1) * 256],
                        w[:, 2304 + r * 256:2304 + (r + 1) * 256],
                        w[:, 3328 + r * 256:3328 + (r + 1) * 256],
                        w[:, 4352 + r * 256:4352 + (r + 1) * 256],
                        w[:, 5376 + r * 256:5376 + (r + 1) * 256]]
                m[f"wie{ne}"] = np.ascontiguousarray(np.concatenate(cols, axis=1))
                m[f"lne{ne}"] = np.ascontiguousarray(np.asarray(inputs["ln_even"][ne], np.float32).reshape(KC, 128).T)
                wo = np.asarray(inputs["w_out_even"][ne], np.float32)
                m[f"woe{ne}"] = np.ascontiguousarray(np.concatenate([wo[r * 256:(r + 1) * 256], wo[1024 + r * 256:1024 + (r + 1) * 256]], axis=0))
                qn = np.asarray(inputs["q_norm_a"][ne], np.float32)
                kn = np.asarray(inputs["k_norm_a"][ne], np.float32)
                m[f"qkn{ne}"] = np.ascontiguousarray(np.concatenate([qn, qn, qn, qn, kn])[None, :])
                m[f"snk{ne}"] = np.ascontiguousarray(np.asarray(inputs["sinks_a"][ne], np.float32)[4 * r:4 * r + 4][None, :])
                m[f"gnt{ne}"] = np.ascontiguousarray(np.asarray(inputs["g_norm_b"][ne], np.float32)[:, None])
                ne += 1
        if "e" in layers:
            lbw = np.asarray(inputs["lower_bounds"], np.float32)
            m["lbt"] = np.ascontiguousarray(np.stack(
                [lbw[l, (2 * r + hh) * 128:(2 * r + hh + 1) * 128] for l in range(2) for hh in range(2)], axis=1))
            m["rope"] = rope_tab
            m["emask"] = emask
            m["bmask"] = bmask
        maps.append(m)
    return maps


_CACHE = {}


def run_layers(inputs, S, layers):
    key = (S, tuple(layers))
    if key not in _CACHE:
        _CACHE[key] = build(S, layers)
    P = _CACHE[key]
    maps = make_in_maps(inputs, S, layers)
    res = run_bass_kernel_spmd(P.nc, maps, core_ids=list(range(8)))
    out = np.stack([res.results[0]["y"], res.results[4]["y"]])
    return out


def kernel(**inputs):
    return run_layers(inputs, SEQ, ["e", "o", "e", "o"]).astype(np.float32)
```
